# Optimizing a Trainium2 kernel written in Bass

```python
import math
import jax, jax.numpy as jnp
from jax import lax
import numpy as np

D_MODEL = 1024
BATCH = 16
SEQ = 4096
DEPTH = 2
DEC_BATCH = 8
DEC_SEQ = 32
PAST_LEN = 1024

CHUNK = 64
N_META = 16
N_A_LAYERS = max(1, DEPTH // 2)
N_B_LAYERS = DEPTH - N_A_LAYERS
D_RNN = 1280
N_RNN_BLOCKS = 10
RNN_BLOCK = D_RNN // N_RNN_BLOCKS
CONV_W = 4
LRU_C = 8.0
D_FF = 4 * D_MODEL
N_HEADS = 8
HD = 64
VD = 2 * HD
N_BUCKETS = 32
MAX_DIST = 128
Q_BLOCK = 128
EPS = 1e-6
NEG = -1e30

kernel_name = 'yoco_rglru_diffattn_stream_step'


def rmsnorm(x, g):
    xf = x.astype(jnp.float32)
    y = xf * lax.rsqrt(jnp.mean(xf * xf, axis=-1, keepdims=True) + EPS)
    return (y * g.astype(jnp.float32)).astype(x.dtype)


def sq_relu_mlp(x, w_up, w_down):
    h = jax.nn.relu(x @ w_up)
    return (h * h) @ w_down


def causal_conv(x, buf, w, b):
    t = x.shape[1]
    xp = jnp.concatenate([buf.astype(x.dtype), x], axis=1)
    y = xp[:, 0:t] * w[0]
    for j in range(1, CONV_W):
        y = y + xp[:, j:j + t] * w[j]
    return y + b, xp[:, -(CONV_W - 1):]


def rg_lru(x, h0, w_r, b_r, w_i, b_i, lam):
    bsz, t, _ = x.shape
    xb = x.reshape(bsz, t, N_RNN_BLOCKS, RNN_BLOCK)
    r = jax.nn.sigmoid(jnp.einsum('btni,nij->btnj', xb, w_r).reshape(bsz, t, D_RNN) + b_r).astype(jnp.float32)
    i = jax.nn.sigmoid(jnp.einsum('btni,nij->btnj', xb, w_i).reshape(bsz, t, D_RNN) + b_i).astype(jnp.float32)
    log_a = -LRU_C * r * jax.nn.softplus(-lam.astype(jnp.float32))
    a = jnp.exp(log_a)
    u = jnp.sqrt(-jnp.expm1(2.0 * log_a)) * (i * x.astype(jnp.float32))

    def combine(e1, e2):
        a1, b1 = e1
        a2, b2 = e2
        return a1 * a2, a2 * b1 + b2

    a_cum, hs = lax.associative_scan(combine, (a, u), axis=1)
    hs = hs + a_cum * h0.astype(jnp.float32)[:, None, :]
    return hs.astype(x.dtype), hs[:, -1].astype(x.dtype)


def recurrent_block(x, h0, conv_buf, w_in, conv_w, conv_b, w_r, b_r, w_i, b_i, lam, w_out):
    gx = x @ w_in
    gate, xr = gx[..., :D_RNN], gx[..., D_RNN:]
    xc, new_buf = causal_conv(xr, conv_buf, conv_w, conv_b)
    hs, h_last = rg_lru(xc, h0, w_r, b_r, w_i, b_i, lam)
    return (jax.nn.gelu(gate) * hs) @ w_out, h_last, new_buf


def t5_bucket(rel):
    nb = N_BUCKETS // 2
    max_exact = nb // 2
    ret = jnp.where(rel > 0, nb, 0)
    n = jnp.abs(rel)
    nf = jnp.maximum(n, 1).astype(jnp.float32)
    large = max_exact + (jnp.log(nf / max_exact) / math.log(MAX_DIST / max_exact) * (nb - max_exact)).astype(jnp.int32)
    large = jnp.minimum(large, nb - 1)
    return ret + jnp.where(n < max_exact, n, large)


def diff_attend(q, k, v, q_pos, k_pos, lam, rel_bias):
    s = jnp.einsum('bqhmd,bkhmd->bmhqk', q, k, preferred_element_type=jnp.float32) * (HD ** -0.5)
    bias = jnp.transpose(rel_bias.astype(jnp.float32)[t5_bucket(k_pos[None, :] - q_pos[:, None])], (2, 0, 1))
    visible = (k_pos[None, :] // CHUNK) <= (q_pos[:, None] // CHUNK)
    s = jnp.where(visible, s + bias, NEG)
    p = jax.nn.softmax(s, axis=-1)
    w = p[:, 0] - lam * p[:, 1]
    return jnp.einsum('bhqk,bkhd->bqhd', w.astype(v.dtype), v)


def diff_lambda(lq1, lk1, lq2, lk2, lam_init):
    f = jnp.float32
    return jnp.exp(jnp.sum(lq1.astype(f) * lk1.astype(f))) - jnp.exp(jnp.sum(lq2.astype(f) * lk2.astype(f))) + lam_init


def shared_kv(h, g, w_kv):
    bsz, t = h.shape[:2]
    kv = rmsnorm(h, g) @ w_kv
    k = kv[..., :N_HEADS * VD].reshape(bsz, t, N_HEADS, VD)
    v = kv[..., N_HEADS * VD:].reshape(bsz, t, N_HEADS, VD)
    return k, v


def trunk(h, h0, conv0, n_lead, make_attend, P):
    new_h, new_conv = [], []
    attend = None
    k = v = None
    for layer in range(DEPTH):
        if layer < N_A_LAYERS:
            a = layer
            y, h_last, buf = recurrent_block(
                rmsnorm(h, P['norm_mix_g'][layer]), h0[a], conv0[a],
                P['w_in_a'][a], P['conv_w'][a], P['conv_b'][a], P['w_gate_r'][a], P['b_gate_r'][a],
                P['w_gate_i'][a], P['b_gate_i'][a], P['lru_lambda'][a], P['w_out_a'][a])
            new_h.append(h_last)
            new_conv.append(buf)
        else:
            b = layer - N_A_LAYERS
            lam_init = 0.8 - 0.6 * math.exp(-0.3 * layer)
            lam = diff_lambda(P['lambda_q1'][b], P['lambda_k1'][b], P['lambda_q2'][b], P['lambda_k2'][b], lam_init)
            bsz, t = h.shape[:2]
            q = (rmsnorm(h, P['norm_mix_g'][layer]) @ P['w_q'][b]).reshape(bsz, t, N_HEADS, 2, HD)
            o = attend(q, lam)
            o = rmsnorm(o, P['subln_g'][b]) * (1.0 - lam_init)
            y = o.reshape(bsz, t, N_HEADS * VD) @ P['w_o'][b]
        h = h + y
        h = h + sq_relu_mlp(rmsnorm(h, P['norm_mlp_g'][layer]), P['w_mlp_up'][layer], P['w_mlp_down'][layer])
        if layer == N_A_LAYERS - 1:
            k, v = shared_kv(h, P['norm_kv_g'], P['w_kv'])
            attend = make_attend(k, v)
            h = h[:, n_lead:]
    return rmsnorm(h, P['norm_f_g']), jnp.stack(new_h), jnp.stack(new_conv), k, v


def setup_inputs(seed: int = 0) -> dict:
    key = jax.random.key(seed)
    ks = iter(jax.random.split(key, 48))

    def nrm(shape, scale):
        return scale * jax.random.normal(next(ks), shape, jnp.float32)

    def gain(shape):
        return 1.0 + nrm(shape, 0.02)

    NA, NB = N_A_LAYERS, N_B_LAYERS
    u = jax.random.uniform(next(ks), (NA, D_RNN), jnp.float32, 0.9, 0.999)
    s = u ** (1.0 / LRU_C)
    lru_lambda = jnp.log(s) - jnp.log1p(-s)
    return {
        'x_prompt': nrm((BATCH, SEQ, D_MODEL), 1.0),
        'x_sample': nrm((DEC_BATCH, DEC_SEQ, D_MODEL), 1.0),
        'state_h': nrm((NA, DEC_BATCH, D_RNN), 0.5),
        'state_conv': nrm((NA, DEC_BATCH, CONV_W - 1, D_RNN), 1.0),
        'cache_meta_k': nrm((DEC_BATCH, N_META, N_HEADS, VD), 1.0),
        'cache_meta_v': nrm((DEC_BATCH, N_META, N_HEADS, VD), 1.0),
        'cache_k': nrm((DEC_BATCH, PAST_LEN, N_HEADS, VD), 1.0),
        'cache_v': nrm((DEC_BATCH, PAST_LEN, N_HEADS, VD), 1.0),
        'meta_tokens': nrm((N_META, D_MODEL), 1.0),
        'norm_mix_g': gain((DEPTH, D_MODEL)),
        'norm_mlp_g': gain((DEPTH, D_MODEL)),
        'w_mlp_up': nrm((DEPTH, D_MODEL, D_FF), D_MODEL ** -0.5),
        'w_mlp_down': nrm((DEPTH, D_FF, D_MODEL), D_FF ** -0.5),
        'w_in_a': nrm((NA, D_MODEL, 2 * D_RNN), D_MODEL ** -0.5),
        'conv_w': nrm((NA, CONV_W, D_RNN), CONV_W ** -0.5),
        'conv_b': nrm((NA, D_RNN), 0.02),
        'w_gate_r': nrm((NA, N_RNN_BLOCKS, RNN_BLOCK, RNN_BLOCK), RNN_BLOCK ** -0.5),
        'b_gate_r': nrm((NA, D_RNN), 0.02),
        'w_gate_i': nrm((NA, N_RNN_BLOCKS, RNN_BLOCK, RNN_BLOCK), RNN_BLOCK ** -0.5),
        'b_gate_i': nrm((NA, D_RNN), 0.02),
        'lru_lambda': lru_lambda,
        'w_out_a': nrm((NA, D_RNN, D_MODEL), D_RNN ** -0.5),
        'norm_kv_g': gain((D_MODEL,)),
        'w_kv': nrm((D_MODEL, 2 * N_HEADS * VD), D_MODEL ** -0.5),
        'w_q': nrm((NB, D_MODEL, N_HEADS * VD), D_MODEL ** -0.5),
        'lambda_q1': nrm((NB, HD), 0.1),
        'lambda_k1': nrm((NB, HD), 0.1),
        'lambda_q2': nrm((NB, HD), 0.1),
        'lambda_k2': nrm((NB, HD), 0.1),
        'subln_g': gain((NB, VD)),
        'w_o': nrm((NB, N_HEADS * VD, D_MODEL), (N_HEADS * VD) ** -0.5),
        'rel_bias': nrm((N_BUCKETS, N_HEADS), 0.5),
        'norm_f_g': gain((D_MODEL,)),
    }


def reference(x_prompt, x_sample, state_h, state_conv, cache_meta_k, cache_meta_v, cache_k, cache_v,
              meta_tokens, norm_mix_g, norm_mlp_g, w_mlp_up, w_mlp_down, w_in_a, conv_w, conv_b,
              w_gate_r, b_gate_r, w_gate_i, b_gate_i, lru_lambda, w_out_a, norm_kv_g, w_kv, w_q,
              lambda_q1, lambda_k1, lambda_q2, lambda_k2, subln_g, w_o, rel_bias, norm_f_g):
    P = {
        'norm_mix_g': norm_mix_g, 'norm_mlp_g': norm_mlp_g, 'w_mlp_up': w_mlp_up, 'w_mlp_down': w_mlp_down,
        'w_in_a': w_in_a, 'conv_w': conv_w, 'conv_b': conv_b, 'w_gate_r': w_gate_r, 'b_gate_r': b_gate_r,
        'w_gate_i': w_gate_i, 'b_gate_i': b_gate_i, 'lru_lambda': lru_lambda, 'w_out_a': w_out_a,
        'norm_kv_g': norm_kv_g, 'w_kv': w_kv, 'w_q': w_q, 'lambda_q1': lambda_q1, 'lambda_k1': lambda_k1,
        'lambda_q2': lambda_q2, 'lambda_k2': lambda_k2, 'subln_g': subln_g, 'w_o': w_o, 'norm_f_g': norm_f_g,
    }

    bp, tp, _ = x_prompt.shape
    meta = jnp.broadcast_to(meta_tokens.astype(x_prompt.dtype)[None], (bp, N_META, D_MODEL))
    hp = jnp.concatenate([meta, x_prompt], axis=1)
    h0_p = jnp.zeros((N_A_LAYERS, bp, D_RNN), x_prompt.dtype)
    conv0_p = jnp.zeros((N_A_LAYERS, bp, CONV_W - 1, D_RNN), x_prompt.dtype)
    n_blk = tp // Q_BLOCK
    k_pos_p = jnp.arange(-N_META, tp, dtype=jnp.int32)

    def prompt_make_attend(k, v):
        k5 = k.reshape(bp, N_META + tp, N_HEADS, 2, HD)

        def attend(q, lam):
            qb = jnp.moveaxis(q.reshape(bp, n_blk, Q_BLOCK, N_HEADS, 2, HD), 1, 0)

            def one(args):
                q_i, start = args
                q_pos = start + jnp.arange(Q_BLOCK, dtype=jnp.int32)
                return diff_attend(q_i, k5, v, q_pos, k_pos_p, lam, rel_bias)

            o = lax.map(one, (qb, jnp.arange(n_blk, dtype=jnp.int32) * Q_BLOCK))
            return jnp.moveaxis(o, 0, 1).reshape(bp, tp, N_HEADS, VD)
        return attend

    y_prompt, sh_p, sc_p, k_p, v_p = trunk(hp, h0_p, conv0_p, N_META, prompt_make_attend, P)

    bs, ts, _ = x_sample.shape
    past = cache_k.shape[1]
    k_pos_s = jnp.arange(-N_META, past + ts, dtype=jnp.int32)
    q_pos_s = past + jnp.arange(ts, dtype=jnp.int32)

    def sample_make_attend(k, v):
        k_all = jnp.concatenate([cache_meta_k.astype(k.dtype), cache_k.astype(k.dtype), k], axis=1)
        v_all = jnp.concatenate([cache_meta_v.astype(v.dtype), cache_v.astype(v.dtype), v], axis=1)
        k5 = k_all.reshape(bs, N_META + past + ts, N_HEADS, 2, HD)

        def attend(q, lam):
            return diff_attend(q, k5, v_all, q_pos_s, k_pos_s, lam, rel_bias)
        return attend

    y_sample, sh_s, sc_s, k_s, v_s = trunk(x_sample, state_h, state_conv, 0, sample_make_attend, P)

    return (y_prompt, y_sample, sh_p, sc_p, k_p[:, :N_META], v_p[:, :N_META], k_p[:, N_META:], v_p[:, N_META:],
            sh_s, sc_s, k_s, v_s)
```

```python
import math
from contextlib import ExitStack

import numpy as np
import concourse.bass as bass
import concourse.mybir as mybir
from concourse.bass_utils import run_bass_kernel_spmd

F32 = mybir.dt.float32
BF16 = mybir.dt.bfloat16
AF = mybir.ActivationFunctionType
ALU = mybir.AluOpType

D = 1024
DR = 1280
NRC = 10
DFF = 4096
NH = 8
EPS = 1e-6
NT = 512
PAST = 1024
NMETA = 16
DEC = 32
LAM_INIT = 0.8 - 0.6 * math.exp(-0.3 * 1)


class Buf:
    __slots__ = ("name", "w", "r")

    def __init__(self, name):
        self.name = name
        self.w = None
        self.r = {}


class Q:
    def __init__(self, K, name, track_self=True):
        self.name = name
        self.sem = K.new_sem("q_" + name)
        self.cnt = 0
        self.waited = {}
        self.track_self = track_self
        self.prog = []

    def wait_ev(self, ev):
        if ev is None:
            return
        sem, val = ev
        if (not self.track_self) and sem is self.sem:
            return
        k = id(sem)
        if self.waited.get(k, 0) >= val:
            return
        self.prog.append(lambda eng, s=sem, v=val: eng.wait_ge(s, v))
        self.waited[k] = val

    def deps(self, reads, writes):
        for b in reads:
            self.wait_ev(b.w)
        for b in writes:
            self.wait_ev(b.w)
            for ev in b.r.values():
                self.wait_ev(ev)

    @staticmethod
    def mark(ev, reads, writes):
        k = id(ev[0])
        for b in reads:
            old = b.r.get(k)
            if old is None or old[1] < ev[1]:
                b.r[k] = ev
        for b in writes:
            b.w = ev
            b.r = {}

    def do(self, reads, writes, fn):
        self.deps(reads, writes)
        self.cnt += 1
        self.prog.append(lambda eng, f=fn, s=self.sem: f(eng).then_inc(s, 1))
        ev = (self.sem, self.cnt)
        self.mark(ev, reads, writes)
        return ev

    def dma(self, reads, writes, slot, out, in_, **kw):
        self.deps(reads, writes)
        if slot.cnt > 0:
            self.wait_ev((slot.sem, slot.cnt))
        slot.cnt += 16
        self.prog.append(
            lambda eng, o=out, i=in_, k=kw, s=slot.sem: eng.dma_start(out=o, in_=i, **k).then_inc(s, 16))
        ev = (slot.sem, slot.cnt)
        self.mark(ev, reads, writes)
        return ev


class Slot:
    def __init__(self, K, name):
        self.sem = K.new_sem("d_" + name)
        self.cnt = 0


class Kern:
    def __init__(self, nc, stack):
        self.nc = nc
        self.stack = stack
        self.nsem = 0

    def new_sem(self, name):
        self.nsem += 1
        return self.stack.enter_context(self.nc.semaphore(name))

    def sb(self, name, shape, dt, stack=None):
        return (stack or self.stack).enter_context(self.nc.sbuf_tensor("s_" + name, shape, dt))

    def ps(self, name, shape, dt):
        return self.stack.enter_context(self.nc.psum_tensor(name, shape, dt))

    def emit(self, queues):
        with self.nc.Block() as block:
            for nm, q in queues.items():
                def body(eng, q=q):
                    for th in q.prog:
                        th(eng)
                getattr(block, nm)(body)


class Ring:
    def __init__(self, items):
        self.items = items
        self.i = 0

    def next(self):
        it = self.items[self.i % len(self.items)]
        self.i += 1
        return it


def _t5_bucket(rel):
    nb = 16
    max_exact = 8
    ret = np.where(rel > 0, nb, 0)
    n = np.abs(rel)
    nf = np.maximum(n, 1).astype(np.float32)
    large = max_exact + (np.log(nf / max_exact) / math.log(128 / max_exact) * (nb - max_exact)).astype(np.int32)
    large = np.minimum(large, nb - 1)
    return ret + np.where(n < max_exact, n, large)


def _static_consts():
    rel = 127 - np.arange(384)
    bk = _t5_bucket(rel.astype(np.int32))
    ohz = np.zeros((32, 384), np.float32)
    ohz[bk, np.arange(384)] = 1.0
    kp = np.arange(128)[:, None]
    qf = np.arange(128)[None, :]
    mask0 = ((kp // 64) <= (qf // 64)).astype(np.float32)
    ident = np.eye(128, dtype=np.float32)
    return ohz, mask0, ident


def build(SEQ):
    assert SEQ % NT == 0
    NTL = SEQ // NT
    NKEY = NMETA + SEQ
    NKT = 1 + SEQ // 128
    SKEY = NMETA + PAST + DEC
    SKT = 1 + PAST // 128 + 1
    NKEYM = max(NKEY, SKEY)
    NKTM = max(NKT, SKT)

    nc = bass.Bass("TRN2", target_bir_lowering=False)

    def din(name, shape, dt=F32):
        return nc.dram_tensor(name, shape, dt, kind="ExternalInput").ap()

    def dout(name, shape):
        return nc.dram_tensor(name, shape, F32, kind="ExternalOutput").ap()

    def dscr(name, shape, dt):
        return nc.dram_tensor(name, shape, dt, kind="Internal").ap()

    xp = din("xp", [2, SEQ, D])
    xs = din("xs", [DEC, D])
    meta = din("meta", [NMETA, D])
    cst_d = din("cst", [128, 128])
    sst_d = din("sst", [128, 40])
    cmk = din("cmk", [NMETA, D])
    cmv = din("cmv", [NMETA, D])
    ck = din("ck", [PAST, D])
    cv = din("cv", [PAST, D])
    w_in = din("w_in", [D, 2 * DR])
    w_gr = din("w_gr", [NRC, 128, 128])
    w_gi = din("w_gi", [NRC, 128, 128])
    w_out = din("w_out", [DR, D])
    w_up = din("w_up", [2, D, DFF])
    w_down = din("w_down", [2, DFF, D])
    w_kv = din("w_kv", [D, 2 * D])
    w_q = din("w_q", [D, D])
    w_o = din("w_o", [D, D])
    lamv = din("lamv", [1, 256])
    subg = din("subg", [1, 128])
    relb = din("relb", [32, 8])
    ohz_d = din("ohz", [32, 384])
    mask0_d = din("mask0", [128, 128])
    ident_d = din("ident", [128, 128])
    bsel_d = din("bsel", [2, 256])

    y_p = dout("y_p", [2, SEQ, D])
    y_s = dout("y_s", [DEC, D])
    sh_p = dout("sh_p", [2, DR])
    sc_p = dout("sc_p", [2, 3, DR])
    mk_p = dout("mk_p", [2, NMETA, D])
    mv_p = dout("mv_p", [2, NMETA, D])
    k_p = dout("k_p", [2, SEQ, D])
    v_p = dout("v_p", [2, SEQ, D])
    sh_s = dout("sh_s", [1, DR])
    sc_s = dout("sc_s", [1, 3, DR])
    k_s = dout("k_s", [DEC, D])
    v_s = dout("v_s", [DEC, D])

    wb_in = dscr("wb_in", [D, 2 * DR], BF16)
    wb_out = dscr("wb_out", [DR, D], BF16)
    wb_up = dscr("wb_up", [2, D, DFF], BF16)
    wb_down = dscr("wb_down", [2, DFF, D], BF16)
    wb_kv = dscr("wb_kv", [D, 2 * D], BF16)
    wb_q = dscr("wb_q", [D, D], BF16)
    wb_o = dscr("wb_o", [D, D], BF16)
    KTs = dscr("KTs", [3, NH, 128, NKEYM], BF16)
    VAs = dscr("VAs", [3, NH, 128, NKTM, 130], BF16)
    zs = dscr("zs", [128, NH, 384], F32)

    with ExitStack() as st:
        K = Kern(nc, st)
        sy = Q(K, "sync")
        gp = Q(K, "gpsimd")
        pe = Q(K, "tensor", track_self=False)
        ve = Q(K, "vector")
        ac = Q(K, "scalar")

        cst = K.sb("cst", [128, 128], F32)
        Bcst = Buf("cst")
        dc = K.sb("dc", [128, 64], F32)
        Bdc = Buf("dc")
        ident = K.sb("ident", [128, 128], F32)
        Bident = Buf("ident")
        ones = K.sb("ones", [128, 128], BF16)
        Bones = Buf("ones")
        wg = K.sb("wg", [128, 2, NRC, 128], BF16)
        Bwg = Buf("wg")
        EBD = K.sb("EBD", [128, NH, 256], BF16)
        EBM = K.sb("EBM", [128, NH, 128], BF16)
        BEB = Buf("EB")
        Gt = K.sb("Gt", [128, 128], F32)
        BG = Buf("G")
        nlam = K.sb("nlam", [128, 1], F32)
        Bnlam = Buf("nlam")
        Gp = K.sb("Gp", [128, 1], F32)
        BGp = Buf("Gp")
        esel = K.sb("esel", [128, 4], BF16)
        Besel = Buf("esel")
        bsel = K.sb("bsel", [2, 256], F32)
        Bbsel = Buf("bsel")
        rl2 = K.sb("rl2", [2, 2, NT], F32)
        rlh = K.sb("rlh", [2, 2, NT], BF16)
        rll = K.sb("rll", [2, 2, NT], BF16)
        Brl2 = [Buf("rl2_0"), Buf("rl2_1")]
        bselb = K.sb("bselb", [2, 256], BF16)
        hT = K.sb("hT", [128, 8, NT], F32)
        BhT = [Buf(f"hT{c}") for c in range(8)]
        xn = K.sb("xn", [128, 8, NT], BF16)
        Bxn = [Buf(f"xn{c}") for c in range(8)]
        qz = K.sb("qz", [128, NH, 2, NT], BF16)
        Bqz = [Buf(f"qz{c}") for c in range(NH)]
        sqc = K.sb("sqc", [128, 2, NT], BF16)
        sqR = Ring([(sqc[:, i, :], Buf(f"sqc{i}")) for i in range(2)])
        rstd = K.sb("rstd", [128, NT], F32)
        Brstd = Buf("rstd")
        NWS = 3
        wsl = K.sb("wsl", [128, NWS, 4096], BF16)
        wR = Ring([(wsl[:, i, :], Buf(f"ws{i}"), Slot(K, f"ws{i}")) for i in range(NWS)])
        NKVO = 3
        kvo = K.sb("kvo", [128, NKVO, 512], F32)
        oR = Ring([(kvo[:, i, :], Buf(f"kvo{i}"), Slot(K, f"kvo{i}")) for i in range(NKVO)])
        vast = K.sb("vast", [128, 4, NH, 130], BF16)
        Bvast = Buf("vast")
        Svast = Slot(K, "vast")
        Skst = Slot(K, "kst")
        pages = K.sb("pages", [128, 32, NT], BF16)
        XPW = NT + 4
        xpall = K.sb("xpall", [128, NRC, XPW], F32)
        Bxp = [Buf(f"xp{c}") for c in range(NRC)]
        Bpg = [Buf(f"pg{i}") for i in range(32)]
        hstate = K.sb("hstate", [128, 4, NRC], F32)
        Bhst = [Buf(f"hst{i}") for i in range(4)]
        ctail = K.sb("ctail", [128, 4, NRC, 3], F32)
        Bct = [Buf(f"ct{i}") for i in range(4)]
        ST_M, ST_S, ST_A, ST_B = 0, 1, 2, 3

        PS = K.ps("PS", [128, 8 * 512], F32)
        bank = [PS[:, 512 * k:512 * (k + 1)] for k in range(8)]
        Bbank = [Buf(f"bank{k}") for k in range(8)]
        accR = Ring([(bank[k], Bbank[k]) for k in (0, 1, 2, 4, 5, 6, 7)])
        stR = Ring([(bank[k], Bbank[k]) for k in (0, 1)])
        LB0 = 2
        OTB = 0
        OB0 = 4

        def page_f32(i0):
            return pages[:, i0:i0 + 2, :].rearrange("p a n -> p (a n)").bitcast(F32), [Bpg[i0], Bpg[i0 + 1]]

        Bw = {}
        BKT = [[Buf(f"KT{s}_{i}") for i in range(NTL + 2)] for s in range(3)]
        BVA = [[Buf(f"VA{s}_{i}") for i in range(NTL + 2)] for s in range(3)]
        Bzs = Buf("zs")
        Sgen = Slot(K, "gen")
        Ssy = Slot(K, "sygen")

        def gp_dma_sync(reads, writes, out, in_, **kw):
            ev = gp.dma(reads, writes, Sgen, out, in_, **kw)
            return ev

        st2 = ExitStack()
        xin = K.sb("xin", [128, 2, 512], F32, stack=st2)
        xR = Ring([(xin[:, i, :], Buf(f"xin{i}"), Slot(K, f"xin{i}")) for i in range(2)])
        def convert(src, dst, name, nsplit=1):
            n = 1
            for s_ in src.shape:
                n *= s_
            names = " ".join(f"d{i}" for i in range(len(src.shape)))
            s2 = src.rearrange(f"{names} -> ({names})").rearrange("(p f) -> p f", p=128)
            d2 = dst.rearrange(f"{names} -> ({names})").rearrange("(p f) -> p f", p=128)
            f = n // 128
            step = f // nsplit
            Bw[name] = []
            for i in range(nsplit):
                b_ = Buf(f"wb_{name}_{i}")
                sl = Slot(K, f"cv_{name}_{i}")
                gp.dma([], [b_], sl, d2[:, i * step:(i + 1) * step], s2[:, i * step:(i + 1) * step])
                Bw[name].append(b_)

        sy.dma([], [Bcst], Ssy, cst[:], cst_d[:, :])
        sy.dma([], [Bident], Ssy, ident[:], ident_d[:, :])
        gp.dma([], [Bwg], Sgen, wg[:, 0, :, :], w_gr.rearrange("n i j -> i n j"))
        gp.wait_ev((Sgen.sem, Sgen.cnt))
        gp.dma([], [Bwg], Sgen, wg[:, 1, :, :], w_gi.rearrange("n i j -> i n j"))
        gp.wait_ev((Sgen.sem, Sgen.cnt))

        ve.do([], [Bones], lambda e: e.memset(ones[:], 1.0))
        ve.do([], [Besel], lambda e: e.memset(esel[:], 0.0))
        ve.do([Besel], [Besel], lambda e: e.memset(esel[:, 0:1], 1.0))
        ve.do([Besel], [Besel], lambda e: e.memset(esel[:, 3:4], 1.0))
        sy.dma([], [Bbsel], Ssy, bsel[:], bsel_d[:, :])
        ve.do([Bbsel], [Bbsel], lambda e: e.tensor_copy(out=bselb[:], in_=bsel[:]))
        ac.do([Bcst], [Bdc], lambda e: e.activation(out=dc[:, 40:50], in_=cst[:, 118:128], func=AF.Exp, scale=-1.0))
        ac.do([Bdc], [Bdc], lambda e: e.activation(out=dc[:, 50:60], in_=dc[:, 40:50], func=AF.Ln, bias=1.0))
        ve.do([Bdc], [Bdc], lambda e: e.tensor_scalar_mul(out=dc[:, 0:10], in0=dc[:, 50:60], scalar1=-4.0))
        ve.do([Bdc], [Bdc], lambda e: e.tensor_scalar_mul(out=dc[:, 10:20], in0=dc[:, 50:60], scalar1=4.0))
        ve.do([Bdc], [Bdc], lambda e: e.tensor_scalar_mul(out=dc[:, 40:50], in0=dc[:, 50:60], scalar1=-8.0))
        ve.do([Bcst, Bdc], [Bdc], lambda e: e.tensor_scalar_mul(out=dc[:, 20:40], in0=cst[:, 98:118], scalar1=0.5))

        lmt = K.sb("lmt", [128, 256], F32, stack=st2)
        Blmt = Buf("lmt")
        sy.dma([], [Blmt], Ssy, lmt[:], lamv.partition_broadcast(128))
        lmp = K.sb("lmp", [128, 2, 64], F32, stack=st2)
        Blmp = Buf("lmp")
        lms = K.sb("lms", [128, 4], F32, stack=st2)
        Blms = Buf("lms")
        ve.do([Blmt], [Blmp], lambda e: e.tensor_tensor(out=lmp[:, 0, :], in0=lmt[:, 0:64], in1=lmt[:, 64:128], op=ALU.mult))
        ve.do([Blmt], [Blmp], lambda e: e.tensor_tensor(out=lmp[:, 1, :], in0=lmt[:, 128:192], in1=lmt[:, 192:256], op=ALU.mult))
        ve.do([Blmp], [Blms], lambda e: e.reduce_sum(out=lms[:, 0:2], in_=lmp[:, :, :], axis=mybir.AxisListType.X))
        ac.do([Blms], [Blms], lambda e: e.activation(out=lms[:, 2:4], in_=lms[:, 0:2], func=AF.Exp))
        ve.do([Blms], [Bnlam], lambda e: e.tensor_tensor(out=nlam[:], in0=lms[:, 3:4], in1=lms[:, 2:3], op=ALU.subtract))
        ve.do([Bnlam], [Bnlam], lambda e: e.tensor_scalar_add(out=nlam[:], in0=nlam[:], scalar1=-LAM_INIT))
        sy.dma([], [BG], Ssy, Gt[:], subg.partition_broadcast(128))
        ve.do([BG], [BG], lambda e: e.tensor_scalar_mul(out=Gt[:], in0=Gt[:], scalar1=1.0 - LAM_INIT))
        sy.dma([], [BGp], Ssy, Gp[:], subg.rearrange("o p -> p o"), allow_slow_non_contiguous=True)
        ve.do([BGp], [BGp], lambda e: e.tensor_scalar_mul(out=Gp[:], in0=Gp[:], scalar1=1.0 - LAM_INIT))

        ohz = K.sb("ohz", [32, 384], F32, stack=st2)
        Bohz = Buf("ohz")
        rbt = K.sb("rbt", [32, 8], F32, stack=st2)
        Brbt = Buf("rbt")
        sy.dma([], [Bohz], Ssy, ohz[:], ohz_d[:, :])
        sy.dma([], [Brbt], Ssy, rbt[:], relb[:, :])
        onef = K.sb("onef", [32, 128], F32, stack=st2)
        Bonef = Buf("onef")
        ve.do([], [Bonef], lambda e: e.memset(onef[:], 1.0))
        ohs = K.sb("ohs", [32, 384], F32, stack=st2)
        Bohs = Buf("ohs")
        zall, Bzall = pages[:, 0:24, :].rearrange("p a n -> p (a n)").bitcast(F32), [Bpg[i] for i in range(24)]
        zall3 = zall[:, 0:NH * 384].rearrange("p (h r) -> p h r", h=NH)
        negc = K.sb("negc", [128, NH], F32, stack=st2)
        Bnegc = Buf("negc")
        for h in range(NH):
            ve.do([Bohz, Brbt], [Bohs], lambda e, h=h: e.tensor_scalar_mul(out=ohs[:], in0=ohz[:], scalar1=rbt[:, h:h + 1]))
            pe.do([Bohs, Bonef], [Bbank[0]], lambda e: e.matmul(bank[0][:, 0:384], lhsT=onef[:, :], rhs=ohs[:, :], start=True, stop=True))
            ve.do([Bbank[0]], [Bnegc], lambda e, h=h: e.tensor_scalar_mul(out=negc[:, h:h + 1], in0=bank[0][:, 382:383], scalar1=-1.0))
            ac.do([Bbank[0], Bnegc], Bzall, lambda e, h=h: e.activation(out=zall3[:, h, :], in_=bank[0][:, 0:384], func=AF.Exp, bias=negc[:, h:h + 1]))
        gp_dma_sync(Bzall, [Bzs], zs[:, :, :], zall3)
        gp.wait_ev((Sgen.sem, Sgen.cnt))
        d0f = K.sb("d0f", [128, NH, 256], F32, stack=st2)
        Bd0f = Buf("d0f")
        msk = K.sb("msk", [128, 128], F32, stack=st2)
        Bmsk = Buf("msk")
        sy.dma([], [Bmsk], Ssy, msk[:], mask0_d[:, :])
        sy.dma([Bzs], [Bd0f], Ssy, d0f[:, :, 0:128], bass.AP(zs.tensor, 127, [[NH * 384 - 1, 128], [384, NH], [1, 128]]))
        sy.wait_ev((Ssy.sem, Ssy.cnt))
        sy.dma([Bzs], [Bd0f], Ssy, d0f[:, :, 128:256], bass.AP(zs.tensor, 255, [[NH * 384 - 1, 128], [384, NH], [1, 128]]))
        sy.wait_ev((Ssy.sem, Ssy.cnt))
        for h in range(NH):
            ve.do([Bd0f, Bmsk], [Bd0f], lambda e, h=h: e.tensor_tensor(out=d0f[:, h, 0:128], in0=d0f[:, h, 0:128], in1=msk[:], op=ALU.mult))
        ve.do([Bd0f], [BEB], lambda e: e.tensor_copy(out=EBD[:], in_=d0f[:]))
        mf = K.sb("mf", [16, NH, 128], F32, stack=st2)
        Bmf = Buf("mf")
        sy.dma([Bzs], [Bmf], Ssy, mf[:], bass.AP(zs.tensor, 143, [[NH * 384 - 1, 16], [384, NH], [1, 128]]))
        sy.wait_ev((Ssy.sem, Ssy.cnt))
        ve.do([Bmf], [BEB], lambda e: e.tensor_copy(out=EBM[0:16, :, :], in_=mf[:]))

        sstt = K.sb("sstt", [128, 40], F32, stack=st2)
        Bsst = Buf("sst")
        sy.dma([], [Bsst], Ssy, sstt[:], sst_d[:, :])
        ve.do([], [Bhst[ST_M]], lambda e: e.memset(hstate[:, ST_M, :], 0.0))
        ve.do([], [Bct[ST_M]], lambda e: e.memset(ctail[:, ST_M, :, :], 0.0))
        ve.do([Bsst], [Bhst[ST_S]], lambda e: e.tensor_copy(out=hstate[:, ST_S, :], in_=sstt[:, 0:10]))
        ve.do([Bsst], [Bct[ST_S]], lambda e: e.tensor_copy(out=ctail[:, ST_S, :, :], in_=sstt[:, 10:40].rearrange("p (c j) -> p c j", j=3)))
        ve.do([], [Bvast], lambda e: e.memset(vast[:, :, :, 128:129], 1.0))
        ve.do([], [Bvast], lambda e: e.memset(vast[:, :, :, 129:130], 0.0))

        def cache_tile(srck, srcv, nrow, kt, kcol):
            for half in range(2):
                xa, xb_, xsl = xR.next()
                sy.dma([], [xb_], xsl, xa[0:nrow, :], srck[:, 512 * half:512 * half + 512])
                for j in range(4):
                    h = 4 * half + j
                    pe.do([xb_, Bident], [Bbank[OTB]], lambda e, xa=xa, j=j: e.transpose(
                        out=bank[OTB][:, 128 * j:128 * j + nrow], in_=xa[0:nrow, 128 * j:128 * j + 128], identity=ident[0:nrow, 0:nrow]))
                ve.do([Bbank[OTB]], [Bpg[4 * half + j] for j in range(4)], lambda e, half=half: e.tensor_copy(
                    out=pages[:, 4 * half:4 * half + 4, 0:nrow],
                    in_=bank[OTB][:, :].rearrange("p (j t) -> p j t", j=4)[:, :, 0:nrow]))
            gp.dma(Bpg[0:8], [BKT[2][0]], Skst, KTs[2].rearrange("h p n -> p h n")[:, :, kcol:kcol + nrow], pages[:, 0:8, 0:nrow])
            for half in range(2):
                xa, xb_, xsl = xR.next()
                sy.dma([], [xb_], xsl, xa[0:nrow, :], srcv[:, 512 * half:512 * half + 512])
                ve.do([xb_], [Bvast], lambda e, xa=xa, half=half: e.tensor_copy(
                    out=vast[0:nrow, 0, 4 * half:4 * half + 4, 0:128],
                    in_=xa[0:nrow, :].rearrange("p (h c) -> p h c", h=4)))
            gp.dma([Bvast], [BVA[2][0]], Svast, VAs[2].rearrange("h p t c -> p t h c")[0:nrow, kt, :, :], vast[0:nrow, 0, :, :])

        convert(w_in, wb_in, "in", 2)
        convert(w_out, wb_out, "out", 1)
        cache_tile(cmk, cmv, NMETA, 0, 0)
        for t in range(PAST // 128):
            cache_tile(ck[128 * t:128 * t + 128, :], cv[128 * t:128 * t + 128, :], 128, 1 + t, NMETA + 128 * t)
        convert(w_up[0], wb_up[0], "up0", 4)
        convert(w_down[0], wb_down[0], "down0", 4)
        convert(w_kv, wb_kv, "kv", 2)
        convert(w_q, wb_q, "q", 1)
        convert(w_o, wb_o, "o", 1)
        convert(w_up[1], wb_up[1], "up1", 4)
        convert(w_down[1], wb_down[1], "down1", 4)


        def fence():
            qs = [sy, gp, pe, ve, ac]
            slots = [Ssy, Sgen, Skst, Svast] + [x[2] for x in xR.items] + [x[2] for x in wR.items] + [x[2] for x in oR.items]
            for q in qs:
                for p in qs:
                    if p is not q and p.cnt > 0:
                        q.wait_ev((p.sem, p.cnt))
                for sl in slots:
                    if sl.cnt > 0:
                        q.wait_ev((sl.sem, sl.cnt))
        fence()
        st2.close()
        SHR = 22528
        shr = K.sb("shr", [128, SHR], BF16)

        class Carver:
            def __init__(self):
                self.off = 0

            def bf(self, n):
                v = shr[:, self.off:self.off + n]
                self.off += n
                assert self.off <= SHR
                return v

            def f32(self, n):
                v = shr[:, self.off:self.off + 2 * n].bitcast(F32)
                self.off += 2 * n
                assert self.off <= SHR
                return v

        cv1 = Carver()
        NSET = 4
        TW = NT + 4
        tmpv = [[cv1.f32(TW) for j in range(3)] for i in range(NSET)]
        Btmp = [[Buf(f"tmp{i}_{j}") for j in range(3)] for i in range(NSET)]
        tqv = [cv1.f32(NT) for i in range(5)]
        Btq = [Buf(f"tq{i}") for i in range(5)]
        ixv = [cv1.bf(NT) for i in range(5)]
        Bixc = [Buf(f"ixc{i}") for i in range(5)]
        xcbv = [cv1.bf(NT) for i in range(NSET)]
        Bxcb = [Buf(f"xcb{i}") for i in range(NSET)]
        cv2 = Carver()
        NKB = 3
        kR = Ring([(cv2.bf(1024), Buf(f"kb{i}"), Slot(K, f"kb{i}")) for i in range(NKB)])
        vR = Ring([(cv2.bf(8 * 130).rearrange("p (t c) -> p t c", t=8), Buf(f"vb{i}"), Slot(K, f"vb{i}")) for i in range(NKB)])
        NPT = 6
        pR = Ring([(cv2.bf(NT), Buf(f"pT{i}")) for i in range(NPT)])
        p2R = Ring([(cv2.bf(NT), Buf(f"p2s{i}")) for i in range(5)])
        rlv = [cv2.f32(NT) for m in range(2)]
        Brlv = [Buf(f"rlv{m}") for m in range(2)]
        ofv = cv2.f32(NT)
        Bofv = Buf("ofv")
        otv = cv2.f32(NT)
        Botv = Buf("otv")
        rsov = cv2.f32(NT)
        Brsov = Buf("rsov")
        sqov = cv2.bf(NT)
        Bsqov = Buf("sqov")
        cv3 = Carver()
        NXS = 8
        xR = Ring([(cv3.f32(512), Buf(f"xs{i}"), Slot(K, f"xs{i}")) for i in range(NXS)])
        x_alias = [it[1] for it in kR.items] + [it[1] for it in vR.items] + [it[1] for it in pR.items]
        for ts_ in Btmp:
            x_alias += list(ts_)
        x_alias += Btq + Bixc + Bxcb
        gp.do([], Bqz, lambda e: e.memset(qz[64:128, :, 0, :], 0.0))
        gp.do([], Bqz, lambda e: e.memset(qz[0:64, :, 1, :], 0.0))

        wseq = []
        for g in (2, 3, 4, 0, 1):
            wseq.append(("in", g))
        for g in range(4):
            wseq.append(("out", g))
        for l in range(2):
            if l == 1:
                for g in range(2):
                    wseq.append(("q", g))
                for g in range(2):
                    wseq.append(("o", g))
            for g in range(8):
                wseq.append((f"up{l}", g))
            for cg in range(4):
                for kh in range(2):
                    wseq.append((f"down{l}", cg, kh))
            if l == 0:
                for g in range(4):
                    wseq.append(("kv", g))
        NTILES = 1 + 2 * NTL
        wglobal = wseq * NTILES
        wstate = {"issued": 0, "used": 0, "info": []}

        def w_src(item):
            kind = item[0]
            if kind == "in":
                g = item[1]
                return wb_in.rearrange("(k p) n -> p k n", p=128)[:, :, 512 * g:512 * g + 512], (8, 512), Bw["in"]
            if kind == "out":
                g = item[1]
                return wb_out.rearrange("(k p) n -> p k n", p=128)[:, :, 256 * g:256 * g + 256], (10, 256), Bw["out"]
            if kind.startswith("up"):
                l, g = int(kind[2]), item[1]
                return wb_up[l].rearrange("(k p) n -> p k n", p=128)[:, :, 512 * g:512 * g + 512], (8, 512), Bw[kind]
            if kind.startswith("down"):
                l, cg, kh = int(kind[4]), item[1], item[2]
                return (wb_down[l].rearrange("(k p) n -> p k n", p=128)[:, 16 * kh:16 * kh + 16, 256 * cg:256 * cg + 256],
                        (16, 256), Bw[kind])
            if kind == "kv":
                g = item[1]
                return wb_kv.rearrange("(k p) n -> p k n", p=128)[:, :, 512 * g:512 * g + 512], (8, 512), Bw["kv"]
            if kind == "q":
                g = item[1]
                return wb_q.rearrange("(k p) n -> p k n", p=128)[:, :, 512 * g:512 * g + 512], (8, 512), Bw["q"]
            if kind == "o":
                g = item[1]
                return wb_o.rearrange("(k p) n -> p k n", p=128)[:, :, 512 * g:512 * g + 512], (8, 512), Bw["o"]
            raise ValueError(kind)

        def w_issue_upto(n):
            while wstate["issued"] < min(n, len(wglobal)):
                item = wglobal[wstate["issued"]]
                src, (kk, nn), sbuf = w_src(item)
                ap, b, sl = wR.next()
                view = ap[:, 0:kk * nn].rearrange("p (k n) -> p k n", k=kk)
                sy.dma(list(sbuf), [b], sl, view, src)
                wstate["info"].append((item, view, b))
                wstate["issued"] += 1

        def w_next(expect):
            i = wstate["used"]
            w_issue_upto(i + NWS)
            item, view, b = wstate["info"][i]
            assert item == expect, (item, expect)
            wstate["used"] += 1
            return view, b

        def evac_alt(i):
            return ac if (i % 2 == 0) else ve

        def copy_on(q, out, in_):
            if q is ac:
                return lambda e: e.copy(out=out, in_=in_)
            return lambda e: e.tensor_copy(out=out, in_=in_)

        nstat = {"pending": None, "count": 0}
        NB_ = 3

        def stat_emit(j, c0, c1):
            n = c1 - c0
            nb, Bnb = bank[NB_], Bbank[NB_]
            sa, sb_ = sqR.next()
            cnt_ = nstat["count"]
            ac.do([BhT[j]], [sb_], lambda e: e.activation(out=sa[:, 0:n], in_=hT[:, j, c0:c1], func=AF.Square))
            pe.do([sb_, Bones], [Bnb], lambda e: e.matmul(nb[:, 0:n], lhsT=ones[:, :], rhs=sa[:, 0:n], start=(cnt_ == 0), stop=(cnt_ == 7)))
            nstat["count"] = cnt_ + 1

        def stat_defer(j, c0, c1):
            if nstat["pending"] is not None:
                stat_emit(*nstat["pending"])
            nstat["pending"] = (j, c0, c1)

        def rmsnorm(gcol, c0, c1, out_f32=False, reuse=False):
            n = c1 - c0
            nb, Bnb = bank[NB_], Bbank[NB_]
            if not reuse:
                if nstat["pending"] is not None:
                    stat_emit(*nstat["pending"])
                    nstat["pending"] = None
                if nstat["count"] == 0:
                    for c in range(8):
                        stat_emit(c, c0, c1)
                assert nstat["count"] == 8, nstat
                nstat["count"] = 0
                ac.do([Bnb], [Brstd], lambda e: e.activation(out=rstd[:, c0:c1], in_=nb[:, 0:n], func=AF.Ln, scale=1.0 / D, bias=EPS))
                ac.do([Brstd], [Brstd], lambda e: e.activation(out=rstd[:, c0:c1], in_=rstd[:, c0:c1], func=AF.Exp, scale=-0.5))
            for c in range(8):
                if out_f32:
                    ve.do([BhT[c], Brstd, Bcst], [BhT[c]], lambda e, c=c: e.scalar_tensor_tensor(
                        out=hT[:, c, c0:c1], in0=hT[:, c, c0:c1], scalar=cst[:, gcol + c:gcol + c + 1], in1=rstd[:, c0:c1],
                        op0=ALU.mult, op1=ALU.mult))
                else:
                    ve.do([BhT[c], Brstd, Bcst], [Bxn[c]], lambda e, c=c: e.scalar_tensor_tensor(
                        out=xn[:, c, c0:c1], in0=hT[:, c, c0:c1], scalar=cst[:, gcol + c:gcol + c + 1], in1=rstd[:, c0:c1],
                        op0=ALU.mult, op1=ALU.mult))

        def resid_add(j, pb, Bpb, c0, c1):
            n = c1 - c0
            ve.do([Bpb, BhT[j]], [BhT[j]], lambda e: e.tensor_tensor(out=hT[:, j, c0:c1], in0=hT[:, j, c0:c1], in1=pb[:, 0:n], op=ALU.add))
            stat_defer(j, c0, c1)

        def mlp(l, gcol, c0, c1):
            n = c1 - c0
            rmsnorm(gcol, c0, c1)
            for g in range(8):
                wv, wbuf = w_next((f"up{l}", g))
                for jj in range(4):
                    j = 4 * g + jj
                    pb, Bpb = accR.next()
                    for kc in range(8):
                        pe.do([wbuf, Bxn[kc]], [Bpb], lambda e, kc=kc, jj=jj, pb=pb, wv=wv: e.matmul(
                            pb[:, 0:n], lhsT=wv[:, kc, 128 * jj:128 * jj + 128], rhs=xn[:, kc, c0:c1], start=(kc == 0), stop=(kc == 7)))
                    if j % 2 == 0:
                        ve.do([Bpb], [Bpg[j]], lambda e, j=j, pb=pb: e.tensor_scalar_max(out=pages[:, j, 0:n], in0=pb[:, 0:n], scalar1=0.0))
                    else:
                        ac.do([Bpb], [Bpg[j]], lambda e, j=j, pb=pb: e.activation(out=pages[:, j, 0:n], in_=pb[:, 0:n], func=AF.Relu))
                    gp.do([Bpg[j]], [Bpg[j]], lambda e, j=j: e.tensor_tensor(out=pages[:, j, 0:n], in0=pages[:, j, 0:n], in1=pages[:, j, 0:n], op=ALU.mult))
            for cg in range(4):
                pbs = [accR.next(), accR.next()]
                for kh in range(2):
                    wv, wbuf = w_next((f"down{l}", cg, kh))
                    for jj in range(2):
                        pb, Bpb = pbs[jj]
                        for k2 in range(16):
                            kc = 16 * kh + k2
                            pe.do([wbuf, Bpg[kc]], [Bpb], lambda e, k2=k2, kc=kc, jj=jj, pb=pb, wv=wv: e.matmul(
                                pb[:, 0:n], lhsT=wv[:, k2, 128 * jj:128 * jj + 128], rhs=pages[:, kc, 0:n], start=(kc == 0), stop=(kc == 31)))
                for jj in range(2):
                    resid_add(2 * cg + jj, pbs[jj][0], pbs[jj][1], c0, c1)

        def layer_a(ncol, segs):
            rmsnorm(0, 0, ncol)
            n = ncol
            gel_pg = 0
            worder = (2, 3, 4, 0, 1)
            xr_view = [(xpall[:, c, 3:3 + NT], [Bxp[c]]) for c in range(NRC)]
            for g in worder:
                wv, wbuf = w_next(("in", g))
                for jj in range(4):
                    j = 4 * g + jj
                    pb, Bpb = accR.next()
                    for kc in range(8):
                        pe.do([wbuf, Bxn[kc]], [Bpb], lambda e, kc=kc, jj=jj, pb=pb, wv=wv: e.matmul(
                            pb[:, 0:n], lhsT=wv[:, kc, 128 * jj:128 * jj + 128], rhs=xn[:, kc, 0:n], start=(kc == 0), stop=(kc == 7)))
                    if j < NRC:
                        ac.do([Bpb], [Bpg[gel_pg + j]], lambda e, j=j, pb=pb: e.activation(
                            out=pages[:, gel_pg + j, 0:n], in_=pb[:, 0:n], func=AF.Gelu_apprx_tanh))
                    else:
                        c = j - NRC
                        xv, xb_ = xr_view[c]
                        ve.do([Bpb], xb_, lambda e, xv=xv, pb=pb: e.tensor_copy(out=xv[:, 0:n], in_=pb[:, 0:n]))
            return xr_view

        def layer_a_chunks(ncol, segs, xr_view):
            n = ncol
            gel_pg = 0

            def P1a(c):
                tb = Btmp[c % NSET]
                xpc, xc = tmpv[c % NSET][0], tmpv[c % NSET][1]
                xv, xb_ = xr_view[c]
                for si, (s0, sn, stt) in enumerate(segs):
                    ve.do([Bct[stt]], xb_, lambda e, s0=s0, stt=stt: e.tensor_copy(out=xpall[:, c, s0:s0 + 3], in_=ctail[:, stt, c, :]))
                    ve.do(xb_ + [Bcst], [tb[1]], lambda e, s0=s0, sn=sn: e.tensor_scalar(
                        out=xc[:, s0:s0 + sn], in0=xpall[:, c, s0:s0 + sn], scalar1=cst[:, 48 + c:49 + c], scalar2=cst[:, 88 + c:89 + c],
                        op0=ALU.mult, op1=ALU.add))
                    for j in range(1, 4):
                        ve.do(xb_ + [Bcst, tb[1]], [tb[1]], lambda e, s0=s0, sn=sn, j=j: e.scalar_tensor_tensor(
                            out=xc[:, s0:s0 + sn], in0=xpall[:, c, s0 + j:s0 + j + sn], scalar=cst[:, 48 + 10 * j + c:49 + 10 * j + c],
                            in1=xc[:, s0:s0 + sn], op0=ALU.mult, op1=ALU.add))
                    ve.do(xb_, [Bct[stt]], lambda e, s0=s0, stt=stt, sn=sn: e.tensor_copy(out=ctail[:, stt, c, :], in_=xpall[:, c, s0 + sn:s0 + sn + 3]))
                xcb_ = xcbv[c % NSET]
                ac.do([tb[1]], [Bxcb[c % NSET]], lambda e: e.copy(out=xcb_[:, 0:n], in_=xc[:, 0:n]))
                pr, Bpr = accR.next()
                pi, Bpi = accR.next()
                pe.do([Bwg, Bxcb[c % NSET]], [Bpr], lambda e: e.matmul(pr[:, 0:n], lhsT=wg[:, 0, c, :], rhs=xcb_[:, 0:n], start=True, stop=True))
                pe.do([Bwg, Bxcb[c % NSET]], [Bpi], lambda e: e.matmul(pi[:, 0:n], lhsT=wg[:, 1, c, :], rhs=xcb_[:, 0:n], start=True, stop=True))
                return (pr, Bpr, pi, Bpi)

            def P1t(c, k, banks):
                pr, Bpr, pi, Bpi = banks
                tb = Btmp[c % NSET]
                ti = tmpv[c % NSET][2]
                xv, xb_ = xr_view[c]
                ac.do([Bpr, Bdc], xb_, lambda e: e.activation(out=xv[:, 0:n], in_=pr[:, 0:n], func=AF.Tanh, scale=0.5, bias=dc[:, 20 + c:21 + c]))
                ac.do([Bpi, Bdc], [tb[2]], lambda e: e.activation(out=ti[:, 0:n], in_=pi[:, 0:n], func=AF.Tanh, scale=0.5, bias=dc[:, 30 + c:31 + c]))
                ac.do(xb_ + [Bdc], [Btq[k]], lambda e: e.activation(out=tqv[k][:, 0:n], in_=xv[:, 0:n], func=AF.Tanh, scale=dc[:, 10 + c:11 + c], bias=dc[:, 10 + c:11 + c]))

            def P1x(c, k):
                tb = Btmp[c % NSET]
                xc, ti = tmpv[c % NSET][1], tmpv[c % NSET][2]
                ve.do([tb[2], tb[1]], [Bixc[k]], lambda e: e.scalar_tensor_tensor(
                    out=ixv[k][:, 0:n], in0=ti[:, 0:n], scalar=1.0, in1=xc[:, 0:n], op0=ALU.add, op1=ALU.mult))

            def P2a(c, k):
                tb = Btmp[c % NSET]
                a_, w_ = tmpv[c % NSET][0], tmpv[c % NSET][1]
                xv, xb_ = xr_view[c]
                ac.do(xb_ + [Bdc], [tb[0]], lambda e: e.activation(out=a_[:, 0:n], in_=xv[:, 0:n], func=AF.Exp, scale=dc[:, c:c + 1], bias=dc[:, c:c + 1]))
                ac.do(xb_ + [Bdc], [tb[1]], lambda e: e.activation(out=w_[:, 0:n], in_=xv[:, 0:n], func=AF.Exp, scale=dc[:, 40 + c:41 + c], bias=dc[:, 40 + c:41 + c]))

            def P2w(c, k):
                tb = Btmp[c % NSET]
                w_ = tmpv[c % NSET][1]
                ve.do([tb[1], Btq[k]], [tb[1]], lambda e: e.scalar_tensor_tensor(
                    out=w_[:, 0:n], in0=w_[:, 0:n], scalar=1.0, in1=tqv[k][:, 0:n], op0=ALU.add, op1=ALU.mult))

            def P2b(c, k):
                tb = Btmp[c % NSET]
                w_ = tmpv[c % NSET][1]
                ac.do([tb[1]], [tb[1]], lambda e: e.activation(out=w_[:, 0:n], in_=w_[:, 0:n], func=AF.Ln))
                ac.do([tb[1]], [tb[1]], lambda e: e.activation(out=w_[:, 0:n], in_=w_[:, 0:n], func=AF.Exp, scale=0.5, bias=math.log(0.5)))

            def P2u(c, k):
                tb = Btmp[c % NSET]
                w_ = tmpv[c % NSET][1]
                gp.do([tb[1], Bixc[k]], [tb[1]], lambda e: e.tensor_tensor(out=w_[:, 0:n], in0=w_[:, 0:n], in1=ixv[k][:, 0:n], op=ALU.mult))

            def P2s(c, k):
                tb = Btmp[c % NSET]
                a_, w_, hs = tmpv[c % NSET][0], tmpv[c % NSET][1], tmpv[c % NSET][2]
                for (s0, sn, stt) in segs:
                    ve.do([tb[0], tb[1], Bhst[stt]], [tb[2]], lambda e, s0=s0, sn=sn, stt=stt: e.tensor_tensor_scan(
                        out=hs[:, s0:s0 + sn], data0=a_[:, s0:s0 + sn], data1=w_[:, s0:s0 + sn], initial=hstate[:, stt, c:c + 1],
                        op0=ALU.mult, op1=ALU.add))
                    ve.do([tb[2]], [Bhst[stt]], lambda e, s0=s0, sn=sn, stt=stt: e.tensor_copy(out=hstate[:, stt, c:c + 1], in_=hs[:, s0 + sn - 1:s0 + sn]))
                gp.do([tb[2], Bpg[gel_pg + c]], [Bpg[gel_pg + c]], lambda e: e.tensor_tensor(
                    out=pages[:, gel_pg + c, 0:n], in0=pages[:, gel_pg + c, 0:n], in1=hs[:, 0:n], op=ALU.mult))

            for half in range(2):
                cs = list(range(5 * half, 5 * half + 5))
                banks = {}
                for t in range(5 + 2):
                    if t < 5:
                        banks[cs[t]] = P1a(cs[t])
                    if 0 <= t - 1 < 5:
                        P1t(cs[t - 1], t - 1, banks[cs[t - 1]])
                    if 0 <= t - 2 < 5:
                        P1x(cs[t - 2], t - 2)
                for t in range(5 + 3):
                    if t < 5:
                        P2a(cs[t], t)
                    if 0 <= t - 1 < 5:
                        P2b(cs[t - 1], t - 1)
                    if 0 <= t - 2 < 5:
                        P2u(cs[t - 2], t - 2)
                    if 0 <= t - 3 < 5:
                        P2s(cs[t - 3], t - 3)
                    if t < 5:
                        P2w(cs[t], t)
            for g in range(4):
                wv, wbuf = w_next(("out", g))
                for jj in range(2):
                    j = 2 * g + jj
                    pb, Bpb = accR.next()
                    for kc in range(NRC):
                        pe.do([wbuf, Bpg[gel_pg + kc]], [Bpb], lambda e, kc=kc, jj=jj, pb=pb, wv=wv: e.matmul(
                            pb[:, 0:n], lhsT=wv[:, kc, 128 * jj:128 * jj + 128], rhs=pages[:, gel_pg + kc, 0:n], start=(kc == 0), stop=(kc == NRC - 1)))
                    resid_add(j, pb, Bpb, 0, n)

        def kv_stage(ncol, subs, kouts, vouts, kt_dsts, va_dsts):
            n = ncol
            rmsnorm(16, 0, n)
            for g in range(4):
                wv, wbuf = w_next(("kv", g))
                for si, (s0, sn) in enumerate(subs):
                    pb, Bpb = accR.next()
                    for kc in range(8):
                        pe.do([wbuf, Bxn[kc]], [Bpb], lambda e, kc=kc, pb=pb, wv=wv, s0=s0, sn=sn: e.matmul(
                            pb[0:sn, 0:512], lhsT=xn[:, kc, s0:s0 + sn], rhs=wv[:, kc, :], start=(kc == 0), stop=(kc == 7)))
                    oa, ob, osl = oR.next()
                    ac.do([Bpb], [ob], lambda e, oa=oa, pb=pb, sn=sn: e.copy(out=oa[0:sn, :], in_=pb[0:sn, 0:512]))
                    outs = kouts[si] if g < 2 else vouts[si]
                    for (dst_fn, r0, r1) in outs:
                        ac.dma([ob], [], osl, dst_fn(g % 2), oa[r0:r1, :])
                    if g >= 2:
                        ve.do([ob], [Bvast], lambda e, oa=oa, sn=sn, si=si, g=g: e.tensor_copy(
                            out=vast[0:sn, si, 4 * (g - 2):4 * (g - 2) + 4, 0:128], in_=oa[0:sn, :].rearrange("p (h c) -> p h c", h=4)))
                if g < 2:
                    for jj in range(4):
                        h = 4 * g + jj
                        pb, Bpb = accR.next()
                        for kc in range(8):
                            pe.do([wbuf, Bxn[kc]], [Bpb], lambda e, kc=kc, jj=jj, pb=pb, wv=wv: e.matmul(
                                pb[:, 0:n], lhsT=wv[:, kc, 128 * jj:128 * jj + 128], rhs=xn[:, kc, 0:n], start=(kc == 0), stop=(kc == 7)))
                        q_ = evac_alt(jj)
                        q_.do([Bpb], [Bpg[h]], copy_on(q_, pages[:, h, 0:n], pb[:, 0:n]))
            for (seq, bi, sc0, ncs, dc0) in kt_dsts:
                gp.dma(Bpg[0:8], [BKT[seq][bi]], Skst, KTs[seq].rearrange("h p n -> p h n")[:, :, dc0:dc0 + ncs], pages[:, 0:8, sc0:sc0 + ncs])
            for (seq, bi, si, r0, r1, kt, p0) in va_dsts:
                gp.dma([Bvast], [BVA[seq][bi]], Svast, VAs[seq].rearrange("h p t c -> p t h c")[p0:p0 + (r1 - r0), kt, :, :], vast[r0:r1, si, :, :])

        def attention(seq, c0, subs, ktiles, q_base_sub):
            nsub = len(subs)
            ncols_all = subs[-1][0] + subs[-1][1]
            nblk = (len(ktiles) + 7) // 8
            LAG = 2
            blocks = []
            steps = []
            for h in range(NH):
                for b in range(nblk):
                    kts = ktiles[8 * b:8 * b + 8]
                    blocks.append((h, b, kts))
                    for ti_, t_ in enumerate(kts):
                        ks = t_[3]
                        s_first = max(0, ks - q_base_sub) if ks >= 0 else 0
                        if s_first >= nsub:
                            continue
                        for m in range(2):
                            steps.append((h, len(blocks) - 1, ti_, m, s_first))
            blk_loaded = {}

            def load_block(bi):
                if bi >= len(blocks) or bi in blk_loaded:
                    return
                h, b, kts = blocks[bi]
                kc_lo = kts[0][2]
                kc_hi = kts[-1][2] + kts[-1][1]
                ka, kb_, ksl = kR.next()
                rbufs = []
                for t_ in kts:
                    for bb in t_[4]:
                        if bb not in rbufs:
                            rbufs.append(bb)
                sy.dma([x for x in rbufs if x.name.startswith("KT")], [kb_], ksl, ka[:, 0:kc_hi - kc_lo], KTs[seq, h, :, kc_lo:kc_hi])
                va, vb_, vsl = vR.next()
                t_lo = kts[0][0]
                gi = 0
                while gi < len(kts):
                    gj = gi
                    while gj + 1 < len(kts) and kts[gj + 1][1] == kts[gi][1]:
                        gj += 1
                    nkg = kts[gi][1]
                    sy.dma([x for x in rbufs if x.name.startswith("VA")], [vb_], vsl, va[0:nkg, gi:gj + 1, :],
                           VAs[seq, h, 0:nkg, t_lo + gi:t_lo + gj + 1, :])
                    gi = gj + 1
                blk_loaded[bi] = (ka, kb_, va, vb_, kc_lo)

            started = set()
            pend = {}
            deferred = []
            cur_iter = [0]
            last_hm = {}
            for j_, st_ in enumerate(steps):
                last_hm[(st_[0], st_[3])] = j_
            front_info = {}
            head_first_step = {}
            head_last_step = {}
            for j, st_ in enumerate(steps):
                head_first_step.setdefault(st_[0], j)
                head_last_step[st_[0]] = j

            def front(j):
                h, bi, ti_, m, s_first = steps[j]
                load_block(bi)
                load_block(bi + 1)
                ka, kb_, va, vb_, kc_lo = blk_loaded[bi]
                kt, nk, kcol, ks, _ = blocks[bi][2][ti_]
                q_lo = subs[s_first][0]
                ncv = ncols_all - q_lo
                hp = h % 2
                sb_, Bsb = stR.next()
                pe.do([kb_, Bqz[h]], [Bsb], lambda e: e.matmul(
                    sb_[0:nk, 0:ncv], lhsT=ka[:, kcol - kc_lo:kcol - kc_lo + nk],
                    rhs=qz[:, h, m, c0 + q_lo:c0 + q_lo + ncv], start=True, stop=True))
                pa, Bpa = pR.next()
                ac.do([Bsb], [Bpa], lambda e: e.activation(out=pa[0:nk, 0:ncv], in_=sb_[0:nk, 0:ncv], func=AF.Exp, scale=0.125))
                for s in range(s_first, nsub):
                    qs = q_base_sub + s
                    off = subs[s][0] - q_lo
                    ns = subs[s][1]
                    eb = None
                    if ks < 0:
                        if qs == 0:
                            eb = EBM[0:nk, h, 0:ns]
                    elif ks == qs:
                        eb = EBD[0:nk, h, 0:ns]
                    elif ks == qs - 1:
                        eb = EBD[0:nk, h, 128:128 + ns]
                    if eb is not None:
                        ve.do([Bpa, BEB], [Bpa], lambda e, off=off, ns=ns, eb=eb: e.tensor_tensor(
                            out=pa[0:nk, off:off + ns], in0=pa[0:nk, off:off + ns], in1=eb, op=ALU.mult))
                front_info[j] = (pa, Bpa, va, vb_, nk, ti_, q_lo, ncv)

            def back(j):
                h, bi, ti_, m, s_first = steps[j]
                pa, Bpa, va, vb_, nk, ti_, q_lo, ncv = front_info.pop(j)
                bk = OB0 + 2 * (h % 2) + m
                first = (h, m) not in started
                started.add((h, m))
                pe.do([Bpa, vb_], [Bbank[bk]], lambda e: e.matmul(
                    bank[bk][:, q_lo:q_lo + ncv], lhsT=va[0:nk, ti_, 0:128], rhs=pa[0:nk, 0:ncv],
                    start=first, stop=False, skip_group_check=True))
                lbk = LB0 + (h % 2)

                def lmm(rhs, rb, nk_, q_lo_, ncv_):
                    firstl = ("l", h) not in started
                    started.add(("l", h))
                    pe.do([rb, Besel], [Bbank[lbk]], lambda e: e.matmul(
                        bank[lbk][0:2, q_lo_:q_lo_ + ncv_], lhsT=esel[0:nk_, 2 * m:2 * m + 2], rhs=rhs[0:nk_, 0:ncv_],
                        start=firstl, stop=False, skip_group_check=True))

                key = (h, m)
                full = (nk == 128 and q_lo == 0 and ncv == ncols_all)
                islast = (j == last_hm[key])
                if key in pend:
                    ppa, Bppa, pnk, pq, pncv = pend.pop(key)
                    if full:
                        s2, Bs2 = p2R.next()
                        ve.do([Bppa, Bpa], [Bs2], lambda e: e.tensor_tensor(out=s2[:, 0:ncv], in0=ppa[:, 0:ncv], in1=pa[:, 0:ncv], op=ALU.add))
                        deferred.append((cur_iter[0] + 2, lambda: lmm(s2, Bs2, 128, 0, ncv)))
                    else:
                        lmm(ppa, Bppa, pnk, pq, pncv)
                        lmm(pa, Bpa, nk, q_lo, ncv)
                elif full and not islast:
                    pend[key] = (pa, Bpa, nk, q_lo, ncv)
                else:
                    lmm(pa, Bpa, nk, q_lo, ncv)

            def finalize(h, stg):
                hp = h % 2
                nq = ncols_all
                lbk = LB0 + hp
                b0, b1 = OB0 + 2 * hp, OB0 + 2 * hp + 1
                if stg == 0:
                    ac.do([Bbank[lbk]], [Brl2[hp]], lambda e: e.activation(out=rl2[0:2, hp, 0:nq], in_=bank[lbk][0:2, 0:nq], func=AF.Ln))
                    ac.do([Brl2[hp]], [Brl2[hp]], lambda e: e.activation(out=rl2[0:2, hp, 0:nq], in_=rl2[0:2, hp, 0:nq], func=AF.Exp, scale=-1.0))
                    ac.do([Brl2[hp]], [Brl2[hp]], lambda e: e.copy(out=rlh[0:2, hp, 0:nq], in_=rl2[0:2, hp, 0:nq]))
                    ve.do([Brl2[hp]], [Brl2[hp]], lambda e: e.tensor_tensor(out=rll[0:2, hp, 0:nq], in0=rl2[0:2, hp, 0:nq], in1=rlh[0:2, hp, 0:nq], op=ALU.subtract))
                elif stg == 1:
                    for m in range(2):
                        lb, Blb = stR.next()
                        pe.do([Brl2[hp], Bbsel], [Blb], lambda e, m=m, lb=lb: e.matmul(lb[:, 0:nq], lhsT=bselb[0:2, 128 * m:128 * m + 128], rhs=rlh[0:2, hp, 0:nq], start=True, stop=False))
                        pe.do([Brl2[hp], Bbsel], [Blb], lambda e, m=m, lb=lb: e.matmul(lb[:, 0:nq], lhsT=bselb[0:2, 128 * m:128 * m + 128], rhs=rll[0:2, hp, 0:nq], start=False, stop=True))
                        ac.do([Blb], [Brlv[m]], lambda e, m=m, lb=lb: e.copy(out=rlv[m][:, 0:nq], in_=lb[:, 0:nq]))
                elif stg == 2:
                    ve.do([Bbank[b0], Brlv[0]], [Bofv], lambda e: e.tensor_tensor(out=ofv[:, 0:nq], in0=bank[b0][:, 0:nq], in1=rlv[0][:, 0:nq], op=ALU.mult))
                    ve.do([Bbank[b1], Brlv[1]], [Botv], lambda e: e.tensor_tensor(out=otv[:, 0:nq], in0=bank[b1][:, 0:nq], in1=rlv[1][:, 0:nq], op=ALU.mult))
                    ve.do([Bofv, Botv, Bnlam], [Bofv], lambda e: e.scalar_tensor_tensor(
                        out=ofv[:, 0:nq], in0=otv[:, 0:nq], scalar=nlam[:, 0:1], in1=ofv[:, 0:nq], op0=ALU.mult, op1=ALU.add))
                    ac.do([Bofv], [Bsqov], lambda e: e.activation(out=sqov[:, 0:nq], in_=ofv[:, 0:nq], func=AF.Square))
                elif stg == 3:
                    sbk, Bsbk = stR.next()
                    pe.do([Bsqov, Bones], [Bsbk], lambda e: e.matmul(sbk[:, 0:nq], lhsT=ones[:, :], rhs=sqov[:, 0:nq], start=True, stop=True))
                    ac.do([Bsbk], [Brsov], lambda e: e.activation(out=rsov[:, 0:nq], in_=sbk[:, 0:nq], func=AF.Ln, scale=1.0 / 128, bias=EPS))
                    ac.do([Brsov], [Brsov], lambda e: e.activation(out=rsov[:, 0:nq], in_=rsov[:, 0:nq], func=AF.Exp, scale=-0.5))
                else:
                    ve.do([Bofv, Brsov, BGp], [Bxn[h]], lambda e: e.scalar_tensor_tensor(
                        out=xn[:, h, c0:c0 + nq], in0=ofv[:, 0:nq], scalar=Gp[:, 0:1], in1=rsov[:, 0:nq], op0=ALU.mult, op1=ALU.mult))

            pending = []
            nst = len(steps)
            for j in range(nst + LAG):
                cur_iter[0] = j
                if j < nst:
                    front(j)
                jb = j - LAG
                if jb >= 0:
                    back(jb)
                    hb = steps[jb][0]
                    if jb == head_last_step[hb]:
                        while deferred:
                            deferred.pop(0)[1]()
                        for stg in range(5):
                            pending.append((j + 1 + 2 * stg, hb, stg))
                        pending.sort(key=lambda x: (x[1], x[2]))
                while deferred and deferred[0][0] <= j:
                    deferred.pop(0)[1]()
                while pending and pending[0][0] <= j:
                    _, h_, stg_ = pending.pop(0)
                    finalize(h_, stg_)
            while deferred:
                deferred.pop(0)[1]()
            pending.sort(key=lambda x: (x[1], x[2]))
            while pending:
                _, h_, stg_ = pending.pop(0)
                finalize(h_, stg_)

        def layer_b(seq, c0, c1, subs, ktiles, q_base_sub):
            n = c1 - c0
            rmsnorm(24, c0, c1, reuse=True)
            for g in range(2):
                wv, wbuf = w_next(("q", g))
                for jj in range(4):
                    h = 4 * g + jj
                    pb, Bpb = accR.next()
                    for kc in range(8):
                        pe.do([wbuf, Bxn[kc]], [Bpb], lambda e, kc=kc, jj=jj, pb=pb, wv=wv: e.matmul(
                            pb[:, 0:n], lhsT=wv[:, kc, 128 * jj:128 * jj + 128], rhs=xn[:, kc, c0:c1], start=(kc == 0), stop=(kc == 7)))
                    ac.do([Bpb], [Bqz[h]], lambda e, h=h, pb=pb: e.copy(out=qz[0:64, h, 0, c0:c1], in_=pb[0:64, 0:n]))
                    ve.do([Bpb], [Bqz[h]], lambda e, h=h, pb=pb: e.tensor_copy(out=qz[64:128, h, 1, c0:c1], in_=pb[64:128, 0:n]))
            attention(seq, c0, subs, ktiles, q_base_sub)
            for g in range(2):
                wv, wbuf = w_next(("o", g))
                for jj in range(4):
                    j = 4 * g + jj
                    pb, Bpb = accR.next()
                    for kc in range(8):
                        pe.do([wbuf, Bxn[kc]], [Bpb], lambda e, kc=kc, jj=jj, pb=pb, wv=wv: e.matmul(
                            pb[:, 0:n], lhsT=wv[:, kc, 128 * jj:128 * jj + 128], rhs=xn[:, kc, c0:c1], start=(kc == 0), stop=(kc == 7)))
                    resid_add(j, pb, Bpb, c0, c1)

        def final_out(c0, subs, dst_fn):
            c1 = c0 + subs[-1][0] + subs[-1][1]
            rmsnorm(40, c0, c1, out_f32=True)
            for si, (off, ns) in enumerate(subs):
                for half in range(2):
                    pb, Bpb = accR.next()
                    for j in range(4):
                        c = 4 * half + j
                        pe.do([BhT[c], Bident], [Bpb], lambda e, pb=pb, j=j, c=c, off=off, ns=ns: e.transpose(
                            out=pb[0:ns, 128 * j:128 * j + 128], in_=hT[:, c, c0 + off:c0 + off + ns], identity=ident[:, :]))
                    oa, ob, osl = oR.next()
                    ac.do([Bpb], [ob], lambda e, oa=oa, pb=pb, ns=ns: e.copy(out=oa[0:ns, :], in_=pb[0:ns, 0:512]))
                    ac.dma([ob], [], osl, dst_fn(si, half), oa[0:ns, :])

        def x_fetch(parts_list):
            got = []
            for parts in parts_list:
                for half in range(2):
                    xa, xb_, xsl = xR.next()
                    for (src, r0, nr) in parts:
                        sy.dma([], [xb_] + x_alias, xsl, xa[r0:r0 + nr, :], src[:, 512 * half:512 * half + 512])
                    got.append((xa, xb_, half, parts[-1][1] + parts[-1][2]))
            return got

        def x_consume(got):
            for (xa, xb_, half, nr_tot) in got:
                pb, Bpb = accR.next()
                for j in range(4):
                    pe.do([xb_, Bident], [Bpb], lambda e, pb=pb, xa=xa, j=j, nr_tot=nr_tot: e.transpose(
                        out=pb[:, 128 * j:128 * j + nr_tot], in_=xa[0:nr_tot, 128 * j:128 * j + 128], identity=ident[0:nr_tot, 0:nr_tot]))
                yield half, pb, Bpb, nr_tot

        NS_ = NMETA + DEC
        for half, pb, Bpb, nr in x_consume(x_fetch([[(meta, 0, NMETA), (xs, NMETA, DEC)]])):
            q_ = evac_alt(half)
            q_.do([Bpb], [BhT[4 * half + j] for j in range(4)], copy_on(
                q_, hT[:, 4 * half:4 * half + 4, 0:nr], pb[:, :].rearrange("p (j t) -> p j t", j=4)[:, :, 0:nr]))
        segs = [(0, NMETA, ST_M), (NMETA, DEC, ST_S)]
        xrv = layer_a(NS_, segs)
        layer_a_chunks(NS_, segs, xrv)
        mlp(0, 8, 0, NS_)
        kv_stage(
            NS_, [(0, NS_)],
            kouts=[[(lambda g2: mk_p[0, :, 512 * g2:512 * g2 + 512], 0, NMETA),
                    (lambda g2: mk_p[1, :, 512 * g2:512 * g2 + 512], 0, NMETA),
                    (lambda g2: k_s[:, 512 * g2:512 * g2 + 512], NMETA, NS_)]],
            vouts=[[(lambda g2: mv_p[0, :, 512 * g2:512 * g2 + 512], 0, NMETA),
                    (lambda g2: mv_p[1, :, 512 * g2:512 * g2 + 512], 0, NMETA),
                    (lambda g2: v_s[:, 512 * g2:512 * g2 + 512], NMETA, NS_)]],
            kt_dsts=[(0, 0, 0, NMETA, 0), (1, 0, 0, NMETA, 0), (2, 1, NMETA, DEC, NMETA + PAST)],
            va_dsts=[(0, 0, 0, 0, NMETA, 0, 0), (1, 0, 0, 0, NMETA, 0, 0), (2, 1, 0, NMETA, NS_, SKT - 1, 0)],
        )
        gp.dma([Bhst[ST_S]], [], Sgen, sh_s[0].rearrange("(c p) -> p c", p=128), hstate[:, ST_S, :], allow_slow_non_contiguous=True)
        for j3 in range(3):
            gp.dma([Bct[ST_S]], [], Sgen, sc_s[0, j3].rearrange("(c p) -> p c", p=128), ctail[:, ST_S, :, j3], allow_slow_non_contiguous=True)
        for stt in (ST_A, ST_B):
            ve.do([Bhst[ST_M]], [Bhst[stt]], lambda e, stt=stt: e.tensor_copy(out=hstate[:, stt, :], in_=hstate[:, ST_M, :]))
            ve.do([Bct[ST_M]], [Bct[stt]], lambda e, stt=stt: e.tensor_copy(out=ctail[:, stt, :, :], in_=ctail[:, ST_M, :, :]))
        s_kt = [(0, NMETA, 0, -1, [BKT[2][0], BVA[2][0]])]
        for t in range(PAST // 128):
            s_kt.append((1 + t, 128, NMETA + 128 * t, t, [BKT[2][0], BVA[2][0]]))
        s_kt.append((SKT - 1, DEC, NMETA + PAST, PAST // 128, [BKT[2][1], BVA[2][1]]))
        layer_b(2, NMETA, NS_, [(0, DEC)], s_kt, PAST // 128)
        mlp(1, 32, NMETA, NS_)
        final_out(NMETA, [(0, DEC)], lambda si, half: y_s[:, 512 * half:512 * half + 512])

        ptiles = [(seq_, i_) for seq_ in range(2) for i_ in range(NTL)]

        def fetch_tile(ti):
            seq_, i_ = ptiles[ti]
            return x_fetch([[(xp[seq_, NT * i_ + 128 * s_:NT * i_ + 128 * s_ + 128, :], 0, 128)] for s_ in range(4)])

        xgot = {0: fetch_tile(0)}
        for tix, (seq, i) in enumerate(ptiles):
            stt = ST_A + seq
            if True:
                f0 = NT * i
                got = xgot.pop(tix)
                for gi_, (half, pb, Bpb, nr) in enumerate(x_consume(got)):
                    s = gi_ // 2
                    q_ = evac_alt(half)
                    q_.do([Bpb], [BhT[4 * half + j] for j in range(4)], copy_on(
                        q_, hT[:, 4 * half:4 * half + 4, 128 * s:128 * s + 128], pb[:, :].rearrange("p (j t) -> p j t", j=4)))
                segs = [(0, NT, stt)]
                xrv = layer_a(NT, segs)
                layer_a_chunks(NT, segs, xrv)
                mlp(0, 8, 0, NT)
                subs4 = [(128 * s, 128) for s in range(4)]
                kv_stage(
                    NT, subs4,
                    kouts=[[(lambda g2, s=s: k_p[seq, f0 + 128 * s:f0 + 128 * s + 128, 512 * g2:512 * g2 + 512], 0, 128)] for s in range(4)],
                    vouts=[[(lambda g2, s=s: v_p[seq, f0 + 128 * s:f0 + 128 * s + 128, 512 * g2:512 * g2 + 512], 0, 128)] for s in range(4)],
                    kt_dsts=[(seq, 1 + i, 0, NT, NMETA + f0)],
                    va_dsts=[(seq, 1 + i, s, 0, 128, 1 + 4 * i + s, 0) for s in range(4)],
                )
                p_kt = [(0, NMETA, 0, -1, [BKT[seq][0], BVA[seq][0]])]
                for t in range(4 * (i + 1)):
                    p_kt.append((1 + t, 128, NMETA + 128 * t, t, [BKT[seq][1 + t // 4], BVA[seq][1 + t // 4]]))
                layer_b(seq, 0, NT, subs4, p_kt, 4 * i)
                if tix + 1 < len(ptiles):
                    xgot[tix + 1] = fetch_tile(tix + 1)
                mlp(1, 32, 0, NT)
                final_out(0, subs4, lambda si, half, f0=f0, seq=seq: y_p[seq, f0 + 128 * si:f0 + 128 * si + 128, 512 * half:512 * half + 512])
            if i == NTL - 1:
                gp.dma([Bhst[stt]], [], Sgen, sh_p[seq].rearrange("(c p) -> p c", p=128), hstate[:, stt, :], allow_slow_non_contiguous=True)
                for j3 in range(3):
                    gp.dma([Bct[stt]], [], Sgen, sc_p[seq, j3].rearrange("(c p) -> p c", p=128), ctail[:, stt, :, j3], allow_slow_non_contiguous=True)

        for (_, _, sl) in oR.items:
            ac.wait_ev((sl.sem, sl.cnt))
        gp.wait_ev((Sgen.sem, Sgen.cnt))
        gp.wait_ev((Skst.sem, Skst.cnt))
        gp.wait_ev((Svast.sem, Svast.cnt))
        K.emit({"sync": sy, "gpsimd": gp, "tensor": pe, "vector": ve, "scalar": ac})
    return nc


def _vec_pm(v, n):
    return np.ascontiguousarray(np.asarray(v, np.float32).reshape(n, 128).T)


def make_in_maps(inp, SEQ, ncores=8):
    ohz, mask0, ident = _static_consts()
    g = lambda k: np.asarray(inp[k], np.float32)
    cst = np.zeros((128, 128), np.float32)
    cst[:, 0:8] = _vec_pm(g("norm_mix_g")[0], 8)
    cst[:, 8:16] = _vec_pm(g("norm_mlp_g")[0], 8)
    cst[:, 16:24] = _vec_pm(g("norm_kv_g"), 8)
    cst[:, 24:32] = _vec_pm(g("norm_mix_g")[1], 8)
    cst[:, 32:40] = _vec_pm(g("norm_mlp_g")[1], 8)
    cst[:, 40:48] = _vec_pm(g("norm_f_g"), 8)
    for j in range(4):
        cst[:, 48 + 10 * j:58 + 10 * j] = _vec_pm(g("conv_w")[0, j], 10)
    cst[:, 88:98] = _vec_pm(g("conv_b")[0], 10)
    cst[:, 98:108] = _vec_pm(g("b_gate_r")[0], 10)
    cst[:, 108:118] = _vec_pm(g("b_gate_i")[0], 10)
    cst[:, 118:128] = _vec_pm(g("lru_lambda")[0], 10)
    lamv = np.concatenate([g("lambda_q1")[0], g("lambda_k1")[0], g("lambda_q2")[0], g("lambda_k2")[0]])[None, :]
    shared = {
        "meta": g("meta_tokens"), "cst": cst,
        "w_in": g("w_in_a")[0], "w_gr": g("w_gate_r")[0], "w_gi": g("w_gate_i")[0], "w_out": g("w_out_a")[0],
        "w_up": g("w_mlp_up"), "w_down": g("w_mlp_down"), "w_kv": g("w_kv"), "w_q": g("w_q")[0], "w_o": g("w_o")[0],
        "lamv": np.ascontiguousarray(lamv), "subg": g("subln_g")[0][None, :].copy(), "relb": g("rel_bias"),
        "ohz": ohz, "mask0": mask0, "ident": ident,
        "bsel": np.concatenate([np.eye(2, dtype=np.float32)[:, 0:1].repeat(128, 1), np.eye(2, dtype=np.float32)[:, 1:2].repeat(128, 1)], axis=1),
    }
    maps = []
    for k in range(ncores):
        sst = np.zeros((128, 40), np.float32)
        sst[:, 0:10] = _vec_pm(g("state_h")[0, k], 10)
        sc = g("state_conv")[0, k]
        sst[:, 10:40] = np.stack([_vec_pm(sc[j], 10) for j in range(3)], axis=2).reshape(128, 30)
        m = dict(shared)
        m.update({
            "xp": np.ascontiguousarray(g("x_prompt")[2 * k:2 * k + 2, :SEQ]),
            "xs": np.ascontiguousarray(g("x_sample")[k]),
            "sst": sst,
            "cmk": np.ascontiguousarray(g("cache_meta_k")[k].reshape(NMETA, D)),
            "cmv": np.ascontiguousarray(g("cache_meta_v")[k].reshape(NMETA, D)),
            "ck": np.ascontiguousarray(g("cache_k")[k].reshape(PAST, D)),
            "cv": np.ascontiguousarray(g("cache_v")[k].reshape(PAST, D)),
        })
        maps.append(m)
    return maps


def gather(results, SEQ, ncores=8):
    cat = lambda k: np.concatenate([np.asarray(r[k]) for r in results], axis=0)
    y_p = cat("y_p")
    y_s = np.stack([np.asarray(r["y_s"]) for r in results], 0)
    sh_p = cat("sh_p")[None]
    sc_p = cat("sc_p")[None]
    mk_p = cat("mk_p").reshape(2 * ncores, NMETA, NH, 128)
    mv_p = cat("mv_p").reshape(2 * ncores, NMETA, NH, 128)
    k_p = cat("k_p").reshape(2 * ncores, SEQ, NH, 128)
    v_p = cat("v_p").reshape(2 * ncores, SEQ, NH, 128)
    sh_s = cat("sh_s")[None]
    sc_s = cat("sc_s")[None]
    k_s = np.stack([np.asarray(r["k_s"]) for r in results], 0).reshape(ncores, DEC, NH, 128)
    v_s = np.stack([np.asarray(r["v_s"]) for r in results], 0).reshape(ncores, DEC, NH, 128)
    outs = (y_p, y_s, sh_p, sc_p, mk_p, mv_p, k_p, v_p, sh_s, sc_s, k_s, v_s)
    return tuple(np.ascontiguousarray(o, dtype=np.float32) for o in outs)


def kernel(**inputs):
    SEQ = int(np.asarray(inputs["x_prompt"]).shape[1])
    nc = build(SEQ)
    maps = make_in_maps(inputs, SEQ, 8)
    res = run_bass_kernel_spmd(nc, maps, core_ids=list(range(8)))
    return gather(res.results, SEQ, 8)
```

```python
import math
from contextlib import ExitStack

import numpy as np
import concourse.bass as bass
import concourse.mybir as mybir
from concourse.bass_utils import run_bass_kernel_spmd

F32 = mybir.dt.float32
BF16 = mybir.dt.bfloat16
AF = mybir.ActivationFunctionType
ALU = mybir.AluOpType

D = 1024
DR = 1280
NRC = 10
DFF = 4096
NH = 8
EPS = 1e-6
NT = 512
PAST = 1024
NMETA = 16
DEC = 32
LAM_INIT = 0.8 - 0.6 * math.exp(-0.3 * 1)


class Buf:
    __slots__ = ("name", "w", "r")

    def __init__(self, name):
        self.name = name
        self.w = None
        self.r = {}


class Q:
    def __init__(self, K, name, track_self=True):
        self.name = name
        self.sem = K.new_sem("q_" + name)
        self.cnt = 0
        self.waited = {}
        self.track_self = track_self
        self.prog = []

    def wait_ev(self, ev):
        if ev is None:
            return
        sem, val = ev
        if (not self.track_self) and sem is self.sem:
            return
        k = id(sem)
        if self.waited.get(k, 0) >= val:
            return
        self.prog.append(lambda eng, s=sem, v=val: eng.wait_ge(s, v))
        self.waited[k] = val

    def deps(self, reads, writes):
        for b in reads:
            self.wait_ev(b.w)
        for b in writes:
            self.wait_ev(b.w)
            for ev in b.r.values():
                self.wait_ev(ev)

    @staticmethod
    def mark(ev, reads, writes):
        k = id(ev[0])
        for b in reads:
            old = b.r.get(k)
            if old is None or old[1] < ev[1]:
                b.r[k] = ev
        for b in writes:
            b.w = ev
            b.r = {}

    def do(self, reads, writes, fn):
        self.deps(reads, writes)
        self.cnt += 1
        self.prog.append(lambda eng, f=fn, s=self.sem: f(eng).then_inc(s, 1))
        ev = (self.sem, self.cnt)
        self.mark(ev, reads, writes)
        return ev

    def dma(self, reads, writes, slot, out, in_, **kw):
        self.deps(reads, writes)
        if slot.cnt > 0:
            self.wait_ev((slot.sem, slot.cnt))
        slot.cnt += 16
        self.prog.append(
            lambda eng, o=out, i=in_, k=kw, s=slot.sem: eng.dma_start(out=o, in_=i, **k).then_inc(s, 16))
        ev = (slot.sem, slot.cnt)
        self.mark(ev, reads, writes)
        return ev


class Slot:
    def __init__(self, K, name):
        self.sem = K.new_sem("d_" + name)
        self.cnt = 0


class Kern:
    def __init__(self, nc, stack):
        self.nc = nc
        self.stack = stack
        self.nsem = 0

    def new_sem(self, name):
        self.nsem += 1
        return self.stack.enter_context(self.nc.semaphore(name))

    def sb(self, name, shape, dt, stack=None):
        return (stack or self.stack).enter_context(self.nc.sbuf_tensor("s_" + name, shape, dt))

    def ps(self, name, shape, dt):
        return self.stack.enter_context(self.nc.psum_tensor(name, shape, dt))

    def emit(self, queues):
        with self.nc.Block() as block:
            for nm, q in queues.items():
                def body(eng, q=q):
                    for th in q.prog:
                        th(eng)
                getattr(block, nm)(body)


class Ring:
    def __init__(self, items):
        self.items = items
        self.i = 0

    def next(self):
        it = self.items[self.i % len(self.items)]
        self.i += 1
        return it


def _t5_bucket(rel):
    nb = 16
    max_exact = 8
    ret = np.where(rel > 0, nb, 0)
    n = np.abs(rel)
    nf = np.maximum(n, 1).astype(np.float32)
    large = max_exact + (np.log(nf / max_exact) / math.log(128 / max_exact) * (nb - max_exact)).astype(np.int32)
    large = np.minimum(large, nb - 1)
    return ret + np.where(n < max_exact, n, large)


def _static_consts():
    rel = 127 - np.arange(384)
    bk = _t5_bucket(rel.astype(np.int32))
    ohz = np.zeros((32, 384), np.float32)
    ohz[bk, np.arange(384)] = 1.0
    kp = np.arange(128)[:, None]
    qf = np.arange(128)[None, :]
    mask0 = ((kp // 64) <= (qf // 64)).astype(np.float32)
    ident = np.eye(128, dtype=np.float32)
    return ohz, mask0, ident


def build(SEQ):
    assert SEQ % NT == 0
    NTL = SEQ // NT
    NKEY = NMETA + SEQ
    NKT = 1 + SEQ // 128
    SKEY = NMETA + PAST + DEC
    SKT = 1 + PAST // 128 + 1
    NKEYM = max(NKEY, SKEY)
    NKTM = max(NKT, SKT)

    nc = bass.Bass("TRN2", target_bir_lowering=False)

    def din(name, shape, dt=F32):
        return nc.dram_tensor(name, shape, dt, kind="ExternalInput").ap()

    def dout(name, shape):
        return nc.dram_tensor(name, shape, F32, kind="ExternalOutput").ap()

    def dscr(name, shape, dt):
        return nc.dram_tensor(name, shape, dt, kind="Internal").ap()

    xp = din("xp", [2, SEQ, D])
    xs = din("xs", [DEC, D])
    meta = din("meta", [NMETA, D])
    cst_d = din("cst", [128, 128])
    sst_d = din("sst", [128, 40])
    cmk = din("cmk", [NMETA, D])
    cmv = din("cmv", [NMETA, D])
    ck = din("ck", [PAST, D])
    cv = din("cv", [PAST, D])
    w_in = din("w_in", [D, 2 * DR])
    w_gr = din("w_gr", [NRC, 128, 128])
    w_gi = din("w_gi", [NRC, 128, 128])
    w_out = din("w_out", [DR, D])
    w_up = din("w_up", [2, D, DFF])
    w_down = din("w_down", [2, DFF, D])
    w_kv = din("w_kv", [D, 2 * D])
    w_q = din("w_q", [D, D])
    w_o = din("w_o", [D, D])
    lamv = din("lamv", [1, 256])
    subg = din("subg", [1, 128])
    relb = din("relb", [32, 8])
    ohz_d = din("ohz", [32, 384])
    mask0_d = din("mask0", [128, 128])
    ident_d = din("ident", [128, 128])
    bsel_d = din("bsel", [2, 256])

    y_p = dout("y_p", [2, SEQ, D])
    y_s = dout("y_s", [DEC, D])
    sh_p = dout("sh_p", [2, DR])
    sc_p = dout("sc_p", [2, 3, DR])
    mk_p = dout("mk_p", [2, NMETA, D])
    mv_p = dout("mv_p", [2, NMETA, D])
    k_p = dout("k_p", [2, SEQ, D])
    v_p = dout("v_p", [2, SEQ, D])
    sh_s = dout("sh_s", [1, DR])
    sc_s = dout("sc_s", [1, 3, DR])
    k_s = dout("k_s", [DEC, D])
    v_s = dout("v_s", [DEC, D])

    wb_in = dscr("wb_in", [D, 2 * DR], BF16)
    wb_out = dscr("wb_out", [DR, D], BF16)
    wb_up = dscr("wb_up", [2, D, DFF], BF16)
    wb_down = dscr("wb_down", [2, DFF, D], BF16)
    wb_kv = dscr("wb_kv", [D, 2 * D], BF16)
    wb_q = dscr("wb_q", [D, D], BF16)
    wb_o = dscr("wb_o", [D, D], BF16)
    KTs = dscr("KTs", [3, NH, 128, NKEYM], BF16)
    VAs = dscr("VAs", [3, NH, 128, NKTM, 130], BF16)
    zs = dscr("zs", [128, NH, 384], F32)

    with ExitStack() as st:
        K = Kern(nc, st)
        sy = Q(K, "sync")
        gp = Q(K, "gpsimd")
        pe = Q(K, "tensor", track_self=False)
        ve = Q(K, "vector")
        ac = Q(K, "scalar")

        cst = K.sb("cst", [128, 128], F32)
        Bcst = Buf("cst")
        dc = K.sb("dc", [128, 64], F32)
        Bdc = Buf("dc")
        ident = K.sb("ident", [128, 128], F32)
        Bident = Buf("ident")
        ones = K.sb("ones", [128, 128], BF16)
        Bones = Buf("ones")
        wg = K.sb("wg", [128, 2, NRC, 128], BF16)
        Bwg = Buf("wg")
        EBD = K.sb("EBD", [128, NH, 256], BF16)
        EBM = K.sb("EBM", [128, NH, 128], BF16)
        BEB = Buf("EB")
        Gt = K.sb("Gt", [128, 128], F32)
        BG = Buf("G")
        nlam = K.sb("nlam", [128, 1], F32)
        Bnlam = Buf("nlam")
        Gp = K.sb("Gp", [128, 1], F32)
        BGp = Buf("Gp")
        esel = K.sb("esel", [128, 4], BF16)
        Besel = Buf("esel")
        bsel = K.sb("bsel", [2, 256], F32)
        Bbsel = Buf("bsel")
        rl2 = K.sb("rl2", [2, 2, NT], F32)
        rlh = K.sb("rlh", [2, 2, NT], BF16)
        rll = K.sb("rll", [2, 2, NT], BF16)
        Brl2 = [Buf("rl2_0"), Buf("rl2_1")]
        bselb = K.sb("bselb", [2, 256], BF16)
        hT = K.sb("hT", [128, 8, NT], F32)
        BhT = [Buf(f"hT{c}") for c in range(8)]
        xn = K.sb("xn", [128, 8, NT], BF16)
        Bxn = [Buf(f"xn{c}") for c in range(8)]
        qz = K.sb("qz", [128, NH, 2, NT], BF16)
        Bqz = [Buf(f"qz{c}") for c in range(NH)]
        sqc = K.sb("sqc", [128, 2, NT], BF16)
        sqR = Ring([(sqc[:, i, :], Buf(f"sqc{i}")) for i in range(2)])
        rstd = K.sb("rstd", [128, NT], F32)
        Brstd = Buf("rstd")
        NWS = 3
        wsl = K.sb("wsl", [128, NWS, 4096], BF16)
        wR = Ring([(wsl[:, i, :], Buf(f"ws{i}"), Slot(K, f"ws{i}")) for i in range(NWS)])
        NKVO = 3
        kvo = K.sb("kvo", [128, NKVO, 512], F32)
        oR = Ring([(kvo[:, i, :], Buf(f"kvo{i}"), Slot(K, f"kvo{i}")) for i in range(NKVO)])
        vast = K.sb("vast", [128, 4, NH, 130], BF16)
        Bvast = Buf("vast")
        Svast = Slot(K, "vast")
        Skst = Slot(K, "kst")
        pages = K.sb("pages", [128, 32, NT], BF16)
        XPW = NT + 4
        xpall = K.sb("xpall", [128, NRC, XPW], F32)
        Bxp = [Buf(f"xp{c}") for c in range(NRC)]
        Bpg = [Buf(f"pg{i}") for i in range(32)]
        hstate = K.sb("hstate", [128, 4, NRC], F32)
        Bhst = [Buf(f"hst{i}") for i in range(4)]
        ctail = K.sb("ctail", [128, 4, NRC, 3], F32)
        Bct = [Buf(f"ct{i}") for i in range(4)]
        ST_M, ST_S, ST_A, ST_B = 0, 1, 2, 3

        PS = K.ps("PS", [128, 8 * 512], F32)
        bank = [PS[:, 512 * k:512 * (k + 1)] for k in range(8)]
        Bbank = [Buf(f"bank{k}") for k in range(8)]
        accR = Ring([(bank[k], Bbank[k]) for k in range(8)])
        stR = Ring([(bank[k], Bbank[k]) for k in (0, 1)])
        LB0 = 2
        OTB = 0
        OB0 = 4

        def page_f32(i0):
            return pages[:, i0:i0 + 2, :].rearrange("p a n -> p (a n)").bitcast(F32), [Bpg[i0], Bpg[i0 + 1]]

        Bw = {}
        BKT = [[Buf(f"KT{s}_{i}") for i in range(NTL + 2)] for s in range(3)]
        BVA = [[Buf(f"VA{s}_{i}") for i in range(NTL + 2)] for s in range(3)]
        Bzs = Buf("zs")
        Sgen = Slot(K, "gen")
        Ssy = Slot(K, "sygen")

        def gp_dma_sync(reads, writes, out, in_, **kw):
            ev = gp.dma(reads, writes, Sgen, out, in_, **kw)
            return ev

        st2 = ExitStack()
        xin = K.sb("xin", [128, 2, 512], F32, stack=st2)
        xR = Ring([(xin[:, i, :], Buf(f"xin{i}"), Slot(K, f"xin{i}")) for i in range(2)])
        def convert(src, dst, name, nsplit=1):
            n = 1
            for s_ in src.shape:
                n *= s_
            names = " ".join(f"d{i}" for i in range(len(src.shape)))
            s2 = src.rearrange(f"{names} -> ({names})").rearrange("(p f) -> p f", p=128)
            d2 = dst.rearrange(f"{names} -> ({names})").rearrange("(p f) -> p f", p=128)
            f = n // 128
            step = f // nsplit
            Bw[name] = []
            for i in range(nsplit):
                b_ = Buf(f"wb_{name}_{i}")
                sl = Slot(K, f"cv_{name}_{i}")
                gp.dma([], [b_], sl, d2[:, i * step:(i + 1) * step], s2[:, i * step:(i + 1) * step])
                Bw[name].append(b_)

        sy.dma([], [Bcst], Ssy, cst[:], cst_d[:, :])
        sy.dma([], [Bident], Ssy, ident[:], ident_d[:, :])
        gp.dma([], [Bwg], Sgen, wg[:, 0, :, :], w_gr.rearrange("n i j -> i n j"))
        gp.wait_ev((Sgen.sem, Sgen.cnt))
        gp.dma([], [Bwg], Sgen, wg[:, 1, :, :], w_gi.rearrange("n i j -> i n j"))
        gp.wait_ev((Sgen.sem, Sgen.cnt))

        ve.do([], [Bones], lambda e: e.memset(ones[:], 1.0))
        ve.do([], [Besel], lambda e: e.memset(esel[:], 0.0))
        ve.do([Besel], [Besel], lambda e: e.memset(esel[:, 0:1], 1.0))
        ve.do([Besel], [Besel], lambda e: e.memset(esel[:, 3:4], 1.0))
        sy.dma([], [Bbsel], Ssy, bsel[:], bsel_d[:, :])
        ve.do([Bbsel], [Bbsel], lambda e: e.tensor_copy(out=bselb[:], in_=bsel[:]))
        ac.do([Bcst], [Bdc], lambda e: e.activation(out=dc[:, 40:50], in_=cst[:, 118:128], func=AF.Exp, scale=-1.0))
        ac.do([Bdc], [Bdc], lambda e: e.activation(out=dc[:, 50:60], in_=dc[:, 40:50], func=AF.Ln, bias=1.0))
        ve.do([Bdc], [Bdc], lambda e: e.tensor_scalar_mul(out=dc[:, 0:10], in0=dc[:, 50:60], scalar1=-4.0))
        ve.do([Bdc], [Bdc], lambda e: e.tensor_scalar_mul(out=dc[:, 10:20], in0=dc[:, 50:60], scalar1=4.0))
        ve.do([Bdc], [Bdc], lambda e: e.tensor_scalar_mul(out=dc[:, 40:50], in0=dc[:, 50:60], scalar1=-8.0))
        ve.do([Bcst, Bdc], [Bdc], lambda e: e.tensor_scalar_mul(out=dc[:, 20:40], in0=cst[:, 98:118], scalar1=0.5))

        lmt = K.sb("lmt", [128, 256], F32, stack=st2)
        Blmt = Buf("lmt")
        sy.dma([], [Blmt], Ssy, lmt[:], lamv.partition_broadcast(128))
        lmp = K.sb("lmp", [128, 2, 64], F32, stack=st2)
        Blmp = Buf("lmp")
        lms = K.sb("lms", [128, 4], F32, stack=st2)
        Blms = Buf("lms")
        ve.do([Blmt], [Blmp], lambda e: e.tensor_tensor(out=lmp[:, 0, :], in0=lmt[:, 0:64], in1=lmt[:, 64:128], op=ALU.mult))
        ve.do([Blmt], [Blmp], lambda e: e.tensor_tensor(out=lmp[:, 1, :], in0=lmt[:, 128:192], in1=lmt[:, 192:256], op=ALU.mult))
        ve.do([Blmp], [Blms], lambda e: e.reduce_sum(out=lms[:, 0:2], in_=lmp[:, :, :], axis=mybir.AxisListType.X))
        ac.do([Blms], [Blms], lambda e: e.activation(out=lms[:, 2:4], in_=lms[:, 0:2], func=AF.Exp))
        ve.do([Blms], [Bnlam], lambda e: e.tensor_tensor(out=nlam[:], in0=lms[:, 3:4], in1=lms[:, 2:3], op=ALU.subtract))
        ve.do([Bnlam], [Bnlam], lambda e: e.tensor_scalar_add(out=nlam[:], in0=nlam[:], scalar1=-LAM_INIT))
        sy.dma([], [BG], Ssy, Gt[:], subg.partition_broadcast(128))
        ve.do([BG], [BG], lambda e: e.tensor_scalar_mul(out=Gt[:], in0=Gt[:], scalar1=1.0 - LAM_INIT))
        sy.dma([], [BGp], Ssy, Gp[:], subg.rearrange("o p -> p o"), allow_slow_non_contiguous=True)
        ve.do([BGp], [BGp], lambda e: e.tensor_scalar_mul(out=Gp[:], in0=Gp[:], scalar1=1.0 - LAM_INIT))

        ohz = K.sb("ohz", [32, 384], F32, stack=st2)
        Bohz = Buf("ohz")
        rbt = K.sb("rbt", [32, 8], F32, stack=st2)
        Brbt = Buf("rbt")
        sy.dma([], [Bohz], Ssy, ohz[:], ohz_d[:, :])
        sy.dma([], [Brbt], Ssy, rbt[:], relb[:, :])
        onef = K.sb("onef", [32, 128], F32, stack=st2)
        Bonef = Buf("onef")
        ve.do([], [Bonef], lambda e: e.memset(onef[:], 1.0))
        ohs = K.sb("ohs", [32, 384], F32, stack=st2)
        Bohs = Buf("ohs")
        zall, Bzall = pages[:, 0:24, :].rearrange("p a n -> p (a n)").bitcast(F32), [Bpg[i] for i in range(24)]
        zall3 = zall[:, 0:NH * 384].rearrange("p (h r) -> p h r", h=NH)
        negc = K.sb("negc", [128, NH], F32, stack=st2)
        Bnegc = Buf("negc")
        for h in range(NH):
            ve.do([Bohz, Brbt], [Bohs], lambda e, h=h: e.tensor_scalar_mul(out=ohs[:], in0=ohz[:], scalar1=rbt[:, h:h + 1]))
            pe.do([Bohs, Bonef], [Bbank[0]], lambda e: e.matmul(bank[0][:, 0:384], lhsT=onef[:, :], rhs=ohs[:, :], start=True, stop=True))
            ve.do([Bbank[0]], [Bnegc], lambda e, h=h: e.tensor_scalar_mul(out=negc[:, h:h + 1], in0=bank[0][:, 382:383], scalar1=-1.0))
            ac.do([Bbank[0], Bnegc], Bzall, lambda e, h=h: e.activation(out=zall3[:, h, :], in_=bank[0][:, 0:384], func=AF.Exp, bias=negc[:, h:h + 1]))
        gp_dma_sync(Bzall, [Bzs], zs[:, :, :], zall3)
        gp.wait_ev((Sgen.sem, Sgen.cnt))
        d0f = K.sb("d0f", [128, NH, 256], F32, stack=st2)
        Bd0f = Buf("d0f")
        msk = K.sb("msk", [128, 128], F32, stack=st2)
        Bmsk = Buf("msk")
        sy.dma([], [Bmsk], Ssy, msk[:], mask0_d[:, :])
        sy.dma([Bzs], [Bd0f], Ssy, d0f[:, :, 0:128], bass.AP(zs.tensor, 127, [[NH * 384 - 1, 128], [384, NH], [1, 128]]))
        sy.wait_ev((Ssy.sem, Ssy.cnt))
        sy.dma([Bzs], [Bd0f], Ssy, d0f[:, :, 128:256], bass.AP(zs.tensor, 255, [[NH * 384 - 1, 128], [384, NH], [1, 128]]))
        sy.wait_ev((Ssy.sem, Ssy.cnt))
        for h in range(NH):
            ve.do([Bd0f, Bmsk], [Bd0f], lambda e, h=h: e.tensor_tensor(out=d0f[:, h, 0:128], in0=d0f[:, h, 0:128], in1=msk[:], op=ALU.mult))
        ve.do([Bd0f], [BEB], lambda e: e.tensor_copy(out=EBD[:], in_=d0f[:]))
        mf = K.sb("mf", [16, NH, 128], F32, stack=st2)
        Bmf = Buf("mf")
        sy.dma([Bzs], [Bmf], Ssy, mf[:], bass.AP(zs.tensor, 143, [[NH * 384 - 1, 16], [384, NH], [1, 128]]))
        sy.wait_ev((Ssy.sem, Ssy.cnt))
        ve.do([Bmf], [BEB], lambda e: e.tensor_copy(out=EBM[0:16, :, :], in_=mf[:]))

        sstt = K.sb("sstt", [128, 40], F32, stack=st2)
        Bsst = Buf("sst")
        sy.dma([], [Bsst], Ssy, sstt[:], sst_d[:, :])
        ve.do([], [Bhst[ST_M]], lambda e: e.memset(hstate[:, ST_M, :], 0.0))
        ve.do([], [Bct[ST_M]], lambda e: e.memset(ctail[:, ST_M, :, :], 0.0))
        ve.do([Bsst], [Bhst[ST_S]], lambda e: e.tensor_copy(out=hstate[:, ST_S, :], in_=sstt[:, 0:10]))
        ve.do([Bsst], [Bct[ST_S]], lambda e: e.tensor_copy(out=ctail[:, ST_S, :, :], in_=sstt[:, 10:40].rearrange("p (c j) -> p c j", j=3)))
        ve.do([], [Bvast], lambda e: e.memset(vast[:, :, :, 128:129], 1.0))
        ve.do([], [Bvast], lambda e: e.memset(vast[:, :, :, 129:130], 0.0))

        def cache_tile(srck, srcv, nrow, kt, kcol):
            for half in range(2):
                xa, xb_, xsl = xR.next()
                sy.dma([], [xb_], xsl, xa[0:nrow, :], srck[:, 512 * half:512 * half + 512])
                for j in range(4):
                    h = 4 * half + j
                    pe.do([xb_, Bident], [Bbank[OTB]], lambda e, xa=xa, j=j: e.transpose(
                        out=bank[OTB][:, 128 * j:128 * j + nrow], in_=xa[0:nrow, 128 * j:128 * j + 128], identity=ident[0:nrow, 0:nrow]))
                ve.do([Bbank[OTB]], [Bpg[4 * half + j] for j in range(4)], lambda e, half=half: e.tensor_copy(
                    out=pages[:, 4 * half:4 * half + 4, 0:nrow],
                    in_=bank[OTB][:, :].rearrange("p (j t) -> p j t", j=4)[:, :, 0:nrow]))
            gp.dma(Bpg[0:8], [BKT[2][0]], Skst, KTs[2].rearrange("h p n -> p h n")[:, :, kcol:kcol + nrow], pages[:, 0:8, 0:nrow])
            for half in range(2):
                xa, xb_, xsl = xR.next()
                sy.dma([], [xb_], xsl, xa[0:nrow, :], srcv[:, 512 * half:512 * half + 512])
                ve.do([xb_], [Bvast], lambda e, xa=xa, half=half: e.tensor_copy(
                    out=vast[0:nrow, 0, 4 * half:4 * half + 4, 0:128],
                    in_=xa[0:nrow, :].rearrange("p (h c) -> p h c", h=4)))
            gp.dma([Bvast], [BVA[2][0]], Svast, VAs[2].rearrange("h p t c -> p t h c")[0:nrow, kt, :, :], vast[0:nrow, 0, :, :])

        convert(w_in, wb_in, "in", 2)
        convert(w_out, wb_out, "out", 1)
        cache_tile(cmk, cmv, NMETA, 0, 0)
        for t in range(PAST // 128):
            cache_tile(ck[128 * t:128 * t + 128, :], cv[128 * t:128 * t + 128, :], 128, 1 + t, NMETA + 128 * t)
        convert(w_up[0], wb_up[0], "up0", 4)
        convert(w_down[0], wb_down[0], "down0", 4)
        convert(w_kv, wb_kv, "kv", 2)
        convert(w_q, wb_q, "q", 1)
        convert(w_o, wb_o, "o", 1)
        convert(w_up[1], wb_up[1], "up1", 4)
        convert(w_down[1], wb_down[1], "down1", 4)


        def fence():
            qs = [sy, gp, pe, ve, ac]
            slots = [Ssy, Sgen, Skst, Svast] + [x[2] for x in xR.items] + [x[2] for x in wR.items] + [x[2] for x in oR.items]
            for q in qs:
                for p in qs:
                    if p is not q and p.cnt > 0:
                        q.wait_ev((p.sem, p.cnt))
                for sl in slots:
                    if sl.cnt > 0:
                        q.wait_ev((sl.sem, sl.cnt))
        fence()
        st2.close()
        SHR = 22528
        shr = K.sb("shr", [128, SHR], BF16)

        class Carver:
            def __init__(self):
                self.off = 0

            def bf(self, n):
                v = shr[:, self.off:self.off + n]
                self.off += n
                assert self.off <= SHR
                return v

            def f32(self, n):
                v = shr[:, self.off:self.off + 2 * n].bitcast(F32)
                self.off += 2 * n
                assert self.off <= SHR
                return v

        cv1 = Carver()
        NSET = 4
        TW = NT + 4
        tmpv = [[cv1.f32(TW) for j in range(3)] for i in range(NSET)]
        Btmp = [[Buf(f"tmp{i}_{j}") for j in range(3)] for i in range(NSET)]
        tqv = [cv1.f32(NT) for i in range(5)]
        Btq = [Buf(f"tq{i}") for i in range(5)]
        ixv = [cv1.bf(NT) for i in range(5)]
        Bixc = [Buf(f"ixc{i}") for i in range(5)]
        xcbv = [cv1.bf(NT) for i in range(NSET)]
        Bxcb = [Buf(f"xcb{i}") for i in range(NSET)]
        cv2 = Carver()
        NKB = 3
        kR = Ring([(cv2.bf(1024), Buf(f"kb{i}"), Slot(K, f"kb{i}")) for i in range(NKB)])
        vR = Ring([(cv2.bf(8 * 130).rearrange("p (t c) -> p t c", t=8), Buf(f"vb{i}"), Slot(K, f"vb{i}")) for i in range(NKB)])
        NPT = 6
        pR = Ring([(cv2.bf(NT), Buf(f"pT{i}")) for i in range(NPT)])
        p2R = Ring([(cv2.bf(NT), Buf(f"p2s{i}")) for i in range(5)])
        rlv = [cv2.f32(NT) for m in range(2)]
        Brlv = [Buf(f"rlv{m}") for m in range(2)]
        ofv = cv2.f32(NT)
        Bofv = Buf("ofv")
        otv = cv2.f32(NT)
        Botv = Buf("otv")
        rsov = cv2.f32(NT)
        Brsov = Buf("rsov")
        sqov = cv2.bf(NT)
        Bsqov = Buf("sqov")
        cv3 = Carver()
        NXS = 8
        xR = Ring([(cv3.f32(512), Buf(f"xs{i}"), Slot(K, f"xs{i}")) for i in range(NXS)])
        x_alias = [it[1] for it in kR.items] + [it[1] for it in vR.items] + [it[1] for it in pR.items]
        for ts_ in Btmp:
            x_alias += list(ts_)
        x_alias += Btq + Bixc + Bxcb
        gp.do([], Bqz, lambda e: e.memset(qz[64:128, :, 0, :], 0.0))
        gp.do([], Bqz, lambda e: e.memset(qz[0:64, :, 1, :], 0.0))

        wseq = []
        for g in (2, 3, 4, 0, 1):
            wseq.append(("in", g))
        for g in range(4):
            wseq.append(("out", g))
        for l in range(2):
            if l == 1:
                for g in range(2):
                    wseq.append(("q", g))
                for g in range(2):
                    wseq.append(("o", g))
            for g in range(8):
                wseq.append((f"up{l}", g))
            for cg in range(4):
                for kh in range(2):
                    wseq.append((f"down{l}", cg, kh))
            if l == 0:
                for g in range(4):
                    wseq.append(("kv", g))
        NTILES = 1 + 2 * NTL
        n_l0 = 5 + 4 + 8 + 8 + 4
        wglobal = wseq[:n_l0] + wseq * (NTILES - 1) + wseq[n_l0:]
        wstate = {"issued": 0, "used": 0, "info": []}

        def w_src(item):
            kind = item[0]
            if kind == "in":
                g = item[1]
                return wb_in.rearrange("(k p) n -> p k n", p=128)[:, :, 512 * g:512 * g + 512], (8, 512), Bw["in"]
            if kind == "out":
                g = item[1]
                return wb_out.rearrange("(k p) n -> p k n", p=128)[:, :, 256 * g:256 * g + 256], (10, 256), Bw["out"]
            if kind.startswith("up"):
                l, g = int(kind[2]), item[1]
                return wb_up[l].rearrange("(k p) n -> p k n", p=128)[:, :, 512 * g:512 * g + 512], (8, 512), Bw[kind]
            if kind.startswith("down"):
                l, cg, kh = int(kind[4]), item[1], item[2]
                return (wb_down[l].rearrange("(k p) n -> p k n", p=128)[:, 16 * kh:16 * kh + 16, 256 * cg:256 * cg + 256],
                        (16, 256), Bw[kind])
            if kind == "kv":
                g = item[1]
                return wb_kv.rearrange("(k p) n -> p k n", p=128)[:, :, 512 * g:512 * g + 512], (8, 512), Bw["kv"]
            if kind == "q":
                g = item[1]
                return wb_q.rearrange("(k p) n -> p k n", p=128)[:, :, 512 * g:512 * g + 512], (8, 512), Bw["q"]
            if kind == "o":
                g = item[1]
                return wb_o.rearrange("(k p) n -> p k n", p=128)[:, :, 512 * g:512 * g + 512], (8, 512), Bw["o"]
            raise ValueError(kind)

        def w_issue_upto(n):
            while wstate["issued"] < min(n, len(wglobal)):
                item = wglobal[wstate["issued"]]
                src, (kk, nn), sbuf = w_src(item)
                ap, b, sl = wR.next()
                view = ap[:, 0:kk * nn].rearrange("p (k n) -> p k n", k=kk)
                sy.dma(list(sbuf), [b], sl, view, src)
                wstate["info"].append((item, view, b))
                wstate["issued"] += 1

        def w_next(expect):
            i = wstate["used"]
            w_issue_upto(i + NWS)
            item, view, b = wstate["info"][i]
            assert item == expect, (item, expect)
            wstate["used"] += 1
            return view, b

        def evac_alt(i):
            return ac if (i % 2 == 0) else ve

        def copy_on(q, out, in_):
            if q is ac:
                return lambda e: e.copy(out=out, in_=in_)
            return lambda e: e.tensor_copy(out=out, in_=in_)

        def rmsnorm(gcol, c0, c1, out_f32=False):
            n = c1 - c0
            nb, Bnb = bank[3], Bbank[3]
            for c in range(8):
                sa, sb_ = sqR.next()
                ac.do([BhT[c]], [sb_], lambda e, sa=sa, c=c: e.activation(out=sa[:, 0:n], in_=hT[:, c, c0:c1], func=AF.Square))
                pe.do([sb_, Bones], [Bnb], lambda e, sa=sa, c=c: e.matmul(nb[:, 0:n], lhsT=ones[:, :], rhs=sa[:, 0:n], start=(c == 0), stop=(c == 7)))
            ac.do([Bnb], [Brstd], lambda e: e.activation(out=rstd[:, 0:n], in_=nb[:, 0:n], func=AF.Ln, scale=1.0 / D, bias=EPS))
            ac.do([Brstd], [Brstd], lambda e: e.activation(out=rstd[:, 0:n], in_=rstd[:, 0:n], func=AF.Exp, scale=-0.5))
            for c in range(8):
                if out_f32:
                    ve.do([BhT[c], Brstd, Bcst], [BhT[c]], lambda e, c=c: e.scalar_tensor_tensor(
                        out=hT[:, c, c0:c1], in0=hT[:, c, c0:c1], scalar=cst[:, gcol + c:gcol + c + 1], in1=rstd[:, 0:n],
                        op0=ALU.mult, op1=ALU.mult))
                else:
                    ve.do([BhT[c], Brstd, Bcst], [Bxn[c]], lambda e, c=c: e.scalar_tensor_tensor(
                        out=xn[:, c, c0:c1], in0=hT[:, c, c0:c1], scalar=cst[:, gcol + c:gcol + c + 1], in1=rstd[:, 0:n],
                        op0=ALU.mult, op1=ALU.mult))

        def resid_add(j, pb, Bpb, c0, c1):
            n = c1 - c0
            ve.do([Bpb, BhT[j]], [BhT[j]], lambda e: e.tensor_tensor(out=hT[:, j, c0:c1], in0=hT[:, j, c0:c1], in1=pb[:, 0:n], op=ALU.add))

        def mlp(l, gcol, c0, c1):
            n = c1 - c0
            rmsnorm(gcol, c0, c1)
            for g in range(8):
                wv, wbuf = w_next((f"up{l}", g))
                for jj in range(4):
                    j = 4 * g + jj
                    pb, Bpb = accR.next()
                    for kc in range(8):
                        pe.do([wbuf, Bxn[kc]], [Bpb], lambda e, kc=kc, jj=jj, pb=pb, wv=wv: e.matmul(
                            pb[:, 0:n], lhsT=wv[:, kc, 128 * jj:128 * jj + 128], rhs=xn[:, kc, c0:c1], start=(kc == 0), stop=(kc == 7)))
                    if j % 2 == 0:
                        ve.do([Bpb], [Bpg[j]], lambda e, j=j, pb=pb: e.tensor_scalar_max(out=pages[:, j, 0:n], in0=pb[:, 0:n], scalar1=0.0))
                    else:
                        ac.do([Bpb], [Bpg[j]], lambda e, j=j, pb=pb: e.activation(out=pages[:, j, 0:n], in_=pb[:, 0:n], func=AF.Relu))
                    gp.do([Bpg[j]], [Bpg[j]], lambda e, j=j: e.tensor_tensor(out=pages[:, j, 0:n], in0=pages[:, j, 0:n], in1=pages[:, j, 0:n], op=ALU.mult))
            for cg in range(4):
                pbs = [accR.next(), accR.next()]
                for kh in range(2):
                    wv, wbuf = w_next((f"down{l}", cg, kh))
                    for jj in range(2):
                        pb, Bpb = pbs[jj]
                        for k2 in range(16):
                            kc = 16 * kh + k2
                            pe.do([wbuf, Bpg[kc]], [Bpb], lambda e, k2=k2, kc=kc, jj=jj, pb=pb, wv=wv: e.matmul(
                                pb[:, 0:n], lhsT=wv[:, k2, 128 * jj:128 * jj + 128], rhs=pages[:, kc, 0:n], start=(kc == 0), stop=(kc == 31)))
                for jj in range(2):
                    resid_add(2 * cg + jj, pbs[jj][0], pbs[jj][1], c0, c1)

        def layer_a(ncol, segs):
            rmsnorm(0, 0, ncol)
            n = ncol
            gel_pg = 0
            worder = (2, 3, 4, 0, 1)
            xr_view = [(xpall[:, c, 3:3 + NT], [Bxp[c]]) for c in range(NRC)]
            for g in worder:
                wv, wbuf = w_next(("in", g))
                for jj in range(4):
                    j = 4 * g + jj
                    pb, Bpb = accR.next()
                    for kc in range(8):
                        pe.do([wbuf, Bxn[kc]], [Bpb], lambda e, kc=kc, jj=jj, pb=pb, wv=wv: e.matmul(
                            pb[:, 0:n], lhsT=wv[:, kc, 128 * jj:128 * jj + 128], rhs=xn[:, kc, 0:n], start=(kc == 0), stop=(kc == 7)))
                    if j < NRC:
                        ac.do([Bpb], [Bpg[gel_pg + j]], lambda e, j=j, pb=pb: e.activation(
                            out=pages[:, gel_pg + j, 0:n], in_=pb[:, 0:n], func=AF.Gelu_apprx_tanh))
                    else:
                        c = j - NRC
                        xv, xb_ = xr_view[c]
                        ve.do([Bpb], xb_, lambda e, xv=xv, pb=pb: e.tensor_copy(out=xv[:, 0:n], in_=pb[:, 0:n]))
            return xr_view

        def layer_a_chunks(ncol, segs, xr_view):
            n = ncol
            gel_pg = 0

            def P1a(c):
                tb = Btmp[c % NSET]
                xpc, xc = tmpv[c % NSET][0], tmpv[c % NSET][1]
                xv, xb_ = xr_view[c]
                for si, (s0, sn, stt) in enumerate(segs):
                    ve.do([Bct[stt]], xb_, lambda e, s0=s0, stt=stt: e.tensor_copy(out=xpall[:, c, s0:s0 + 3], in_=ctail[:, stt, c, :]))
                    ve.do(xb_ + [Bcst], [tb[1]], lambda e, s0=s0, sn=sn: e.tensor_scalar(
                        out=xc[:, s0:s0 + sn], in0=xpall[:, c, s0:s0 + sn], scalar1=cst[:, 48 + c:49 + c], scalar2=cst[:, 88 + c:89 + c],
                        op0=ALU.mult, op1=ALU.add))
                    for j in range(1, 4):
                        ve.do(xb_ + [Bcst, tb[1]], [tb[1]], lambda e, s0=s0, sn=sn, j=j: e.scalar_tensor_tensor(
                            out=xc[:, s0:s0 + sn], in0=xpall[:, c, s0 + j:s0 + j + sn], scalar=cst[:, 48 + 10 * j + c:49 + 10 * j + c],
                            in1=xc[:, s0:s0 + sn], op0=ALU.mult, op1=ALU.add))
                    ve.do(xb_, [Bct[stt]], lambda e, s0=s0, stt=stt, sn=sn: e.tensor_copy(out=ctail[:, stt, c, :], in_=xpall[:, c, s0 + sn:s0 + sn + 3]))
                xcb_ = xcbv[c % NSET]
                ac.do([tb[1]], [Bxcb[c % NSET]], lambda e: e.copy(out=xcb_[:, 0:n], in_=xc[:, 0:n]))
                pr, Bpr = accR.next()
                pi, Bpi = accR.next()
                pe.do([Bwg, Bxcb[c % NSET]], [Bpr], lambda e: e.matmul(pr[:, 0:n], lhsT=wg[:, 0, c, :], rhs=xcb_[:, 0:n], start=True, stop=True))
                pe.do([Bwg, Bxcb[c % NSET]], [Bpi], lambda e: e.matmul(pi[:, 0:n], lhsT=wg[:, 1, c, :], rhs=xcb_[:, 0:n], start=True, stop=True))
                return (pr, Bpr, pi, Bpi)

            def P1t(c, k, banks):
                pr, Bpr, pi, Bpi = banks
                tb = Btmp[c % NSET]
                ti = tmpv[c % NSET][2]
                xv, xb_ = xr_view[c]
                ac.do([Bpr, Bdc], xb_, lambda e: e.activation(out=xv[:, 0:n], in_=pr[:, 0:n], func=AF.Tanh, scale=0.5, bias=dc[:, 20 + c:21 + c]))
                ac.do([Bpi, Bdc], [tb[2]], lambda e: e.activation(out=ti[:, 0:n], in_=pi[:, 0:n], func=AF.Tanh, scale=0.5, bias=dc[:, 30 + c:31 + c]))
                ac.do(xb_ + [Bdc], [Btq[k]], lambda e: e.activation(out=tqv[k][:, 0:n], in_=xv[:, 0:n], func=AF.Tanh, scale=dc[:, 10 + c:11 + c], bias=dc[:, 10 + c:11 + c]))

            def P1x(c, k):
                tb = Btmp[c % NSET]
                xc, ti = tmpv[c % NSET][1], tmpv[c % NSET][2]
                ve.do([tb[2], tb[1]], [Bixc[k]], lambda e: e.scalar_tensor_tensor(
                    out=ixv[k][:, 0:n], in0=ti[:, 0:n], scalar=1.0, in1=xc[:, 0:n], op0=ALU.add, op1=ALU.mult))

            def P2a(c, k):
                tb = Btmp[c % NSET]
                a_, w_ = tmpv[c % NSET][0], tmpv[c % NSET][1]
                xv, xb_ = xr_view[c]
                ac.do(xb_ + [Bdc], [tb[0]], lambda e: e.activation(out=a_[:, 0:n], in_=xv[:, 0:n], func=AF.Exp, scale=dc[:, c:c + 1], bias=dc[:, c:c + 1]))
                ac.do(xb_ + [Bdc], [tb[1]], lambda e: e.activation(out=w_[:, 0:n], in_=xv[:, 0:n], func=AF.Exp, scale=dc[:, 40 + c:41 + c], bias=dc[:, 40 + c:41 + c]))

            def P2w(c, k):
                tb = Btmp[c % NSET]
                w_ = tmpv[c % NSET][1]
                ve.do([tb[1], Btq[k]], [tb[1]], lambda e: e.scalar_tensor_tensor(
                    out=w_[:, 0:n], in0=w_[:, 0:n], scalar=1.0, in1=tqv[k][:, 0:n], op0=ALU.add, op1=ALU.mult))

            def P2b(c, k):
                tb = Btmp[c % NSET]
                w_ = tmpv[c % NSET][1]
                ac.do([tb[1]], [tb[1]], lambda e: e.activation(out=w_[:, 0:n], in_=w_[:, 0:n], func=AF.Ln))
                ac.do([tb[1]], [tb[1]], lambda e: e.activation(out=w_[:, 0:n], in_=w_[:, 0:n], func=AF.Exp, scale=0.5, bias=math.log(0.5)))

            def P2u(c, k):
                tb = Btmp[c % NSET]
                w_ = tmpv[c % NSET][1]
                gp.do([tb[1], Bixc[k]], [tb[1]], lambda e: e.tensor_tensor(out=w_[:, 0:n], in0=w_[:, 0:n], in1=ixv[k][:, 0:n], op=ALU.mult))

            def P2s(c, k):
                tb = Btmp[c % NSET]
                a_, w_, hs = tmpv[c % NSET][0], tmpv[c % NSET][1], tmpv[c % NSET][2]
                for (s0, sn, stt) in segs:
                    ve.do([tb[0], tb[1], Bhst[stt]], [tb[2]], lambda e, s0=s0, sn=sn, stt=stt: e.tensor_tensor_scan(
                        out=hs[:, s0:s0 + sn], data0=a_[:, s0:s0 + sn], data1=w_[:, s0:s0 + sn], initial=hstate[:, stt, c:c + 1],
                        op0=ALU.mult, op1=ALU.add))
                    ve.do([tb[2]], [Bhst[stt]], lambda e, s0=s0, sn=sn, stt=stt: e.tensor_copy(out=hstate[:, stt, c:c + 1], in_=hs[:, s0 + sn - 1:s0 + sn]))
                gp.do([tb[2], Bpg[gel_pg + c]], [Bpg[gel_pg + c]], lambda e: e.tensor_tensor(
                    out=pages[:, gel_pg + c, 0:n], in0=pages[:, gel_pg + c, 0:n], in1=hs[:, 0:n], op=ALU.mult))

            for half in range(2):
                cs = list(range(5 * half, 5 * half + 5))
                banks = {}
                for t in range(5 + 2):
                    if t < 5:
                        banks[cs[t]] = P1a(cs[t])
                    if 0 <= t - 1 < 5:
                        P1t(cs[t - 1], t - 1, banks[cs[t - 1]])
                    if 0 <= t - 2 < 5:
                        P1x(cs[t - 2], t - 2)
                for t in range(5 + 3):
                    if t < 5:
                        P2a(cs[t], t)
                    if 0 <= t - 1 < 5:
                        P2b(cs[t - 1], t - 1)
                    if 0 <= t - 2 < 5:
                        P2u(cs[t - 2], t - 2)
                    if 0 <= t - 3 < 5:
                        P2s(cs[t - 3], t - 3)
                    if t < 5:
                        P2w(cs[t], t)
            for g in range(4):
                wv, wbuf = w_next(("out", g))
                for jj in range(2):
                    j = 2 * g + jj
                    pb, Bpb = accR.next()
                    for kc in range(NRC):
                        pe.do([wbuf, Bpg[gel_pg + kc]], [Bpb], lambda e, kc=kc, jj=jj, pb=pb, wv=wv: e.matmul(
                            pb[:, 0:n], lhsT=wv[:, kc, 128 * jj:128 * jj + 128], rhs=pages[:, gel_pg + kc, 0:n], start=(kc == 0), stop=(kc == NRC - 1)))
                    resid_add(j, pb, Bpb, 0, n)

        def kv_stage(ncol, subs, kouts, vouts, kt_dsts, va_dsts):
            n = ncol
            rmsnorm(16, 0, n)
            for g in range(4):
                wv, wbuf = w_next(("kv", g))
                for si, (s0, sn) in enumerate(subs):
                    pb, Bpb = accR.next()
                    for kc in range(8):
                        pe.do([wbuf, Bxn[kc]], [Bpb], lambda e, kc=kc, pb=pb, wv=wv, s0=s0, sn=sn: e.matmul(
                            pb[0:sn, 0:512], lhsT=xn[:, kc, s0:s0 + sn], rhs=wv[:, kc, :], start=(kc == 0), stop=(kc == 7)))
                    oa, ob, osl = oR.next()
                    ac.do([Bpb], [ob], lambda e, oa=oa, pb=pb, sn=sn: e.copy(out=oa[0:sn, :], in_=pb[0:sn, 0:512]))
                    outs = kouts[si] if g < 2 else vouts[si]
                    for (dst_fn, r0, r1) in outs:
                        ac.dma([ob], [], osl, dst_fn(g % 2), oa[r0:r1, :])
                    if g >= 2:
                        ve.do([ob], [Bvast], lambda e, oa=oa, sn=sn, si=si, g=g: e.tensor_copy(
                            out=vast[0:sn, si, 4 * (g - 2):4 * (g - 2) + 4, 0:128], in_=oa[0:sn, :].rearrange("p (h c) -> p h c", h=4)))
                if g < 2:
                    for jj in range(4):
                        h = 4 * g + jj
                        pb, Bpb = accR.next()
                        for kc in range(8):
                            pe.do([wbuf, Bxn[kc]], [Bpb], lambda e, kc=kc, jj=jj, pb=pb, wv=wv: e.matmul(
                                pb[:, 0:n], lhsT=wv[:, kc, 128 * jj:128 * jj + 128], rhs=xn[:, kc, 0:n], start=(kc == 0), stop=(kc == 7)))
                        q_ = evac_alt(jj)
                        q_.do([Bpb], [Bpg[h]], copy_on(q_, pages[:, h, 0:n], pb[:, 0:n]))
            for (seq, bi, sc0, ncs, dc0) in kt_dsts:
                gp.dma(Bpg[0:8], [BKT[seq][bi]], Skst, KTs[seq].rearrange("h p n -> p h n")[:, :, dc0:dc0 + ncs], pages[:, 0:8, sc0:sc0 + ncs])
            for (seq, bi, si, r0, r1, kt, p0) in va_dsts:
                gp.dma([Bvast], [BVA[seq][bi]], Svast, VAs[seq].rearrange("h p t c -> p t h c")[p0:p0 + (r1 - r0), kt, :, :], vast[r0:r1, si, :, :])

        def attention(seq, c0, subs, ktiles, q_base_sub):
            nsub = len(subs)
            ncols_all = subs[-1][0] + subs[-1][1]
            nblk = (len(ktiles) + 7) // 8
            LAG = 2
            blocks = []
            steps = []
            for h in range(NH):
                for b in range(nblk):
                    kts = ktiles[8 * b:8 * b + 8]
                    blocks.append((h, b, kts))
                    for ti_, t_ in enumerate(kts):
                        ks = t_[3]
                        s_first = max(0, ks - q_base_sub) if ks >= 0 else 0
                        if s_first >= nsub:
                            continue
                        for m in range(2):
                            steps.append((h, len(blocks) - 1, ti_, m, s_first))
            blk_loaded = {}

            def load_block(bi):
                if bi >= len(blocks) or bi in blk_loaded:
                    return
                h, b, kts = blocks[bi]
                kc_lo = kts[0][2]
                kc_hi = kts[-1][2] + kts[-1][1]
                ka, kb_, ksl = kR.next()
                rbufs = []
                for t_ in kts:
                    for bb in t_[4]:
                        if bb not in rbufs:
                            rbufs.append(bb)
                sy.dma([x for x in rbufs if x.name.startswith("KT")], [kb_], ksl, ka[:, 0:kc_hi - kc_lo], KTs[seq, h, :, kc_lo:kc_hi])
                va, vb_, vsl = vR.next()
                t_lo = kts[0][0]
                gi = 0
                while gi < len(kts):
                    gj = gi
                    while gj + 1 < len(kts) and kts[gj + 1][1] == kts[gi][1]:
                        gj += 1
                    nkg = kts[gi][1]
                    sy.dma([x for x in rbufs if x.name.startswith("VA")], [vb_], vsl, va[0:nkg, gi:gj + 1, :],
                           VAs[seq, h, 0:nkg, t_lo + gi:t_lo + gj + 1, :])
                    gi = gj + 1
                blk_loaded[bi] = (ka, kb_, va, vb_, kc_lo)

            started = set()
            pend = {}
            deferred = []
            cur_iter = [0]
            last_hm = {}
            for j_, st_ in enumerate(steps):
                last_hm[(st_[0], st_[3])] = j_
            front_info = {}
            head_first_step = {}
            head_last_step = {}
            for j, st_ in enumerate(steps):
                head_first_step.setdefault(st_[0], j)
                head_last_step[st_[0]] = j

            def front(j):
                h, bi, ti_, m, s_first = steps[j]
                load_block(bi)
                load_block(bi + 1)
                ka, kb_, va, vb_, kc_lo = blk_loaded[bi]
                kt, nk, kcol, ks, _ = blocks[bi][2][ti_]
                q_lo = subs[s_first][0]
                ncv = ncols_all - q_lo
                hp = h % 2
                sb_, Bsb = stR.next()
                pe.do([kb_, Bqz[h]], [Bsb], lambda e: e.matmul(
                    sb_[0:nk, 0:ncv], lhsT=ka[:, kcol - kc_lo:kcol - kc_lo + nk],
                    rhs=qz[:, h, m, c0 + q_lo:c0 + q_lo + ncv], start=True, stop=True))
                pa, Bpa = pR.next()
                ac.do([Bsb], [Bpa], lambda e: e.activation(out=pa[0:nk, 0:ncv], in_=sb_[0:nk, 0:ncv], func=AF.Exp, scale=0.125))
                for s in range(s_first, nsub):
                    qs = q_base_sub + s
                    off = subs[s][0] - q_lo
                    ns = subs[s][1]
                    eb = None
                    if ks < 0:
                        if qs == 0:
                            eb = EBM[0:nk, h, 0:ns]
                    elif ks == qs:
                        eb = EBD[0:nk, h, 0:ns]
                    elif ks == qs - 1:
                        eb = EBD[0:nk, h, 128:128 + ns]
                    if eb is not None:
                        ve.do([Bpa, BEB], [Bpa], lambda e, off=off, ns=ns, eb=eb: e.tensor_tensor(
                            out=pa[0:nk, off:off + ns], in0=pa[0:nk, off:off + ns], in1=eb, op=ALU.mult))
                front_info[j] = (pa, Bpa, va, vb_, nk, ti_, q_lo, ncv)

            def back(j):
                h, bi, ti_, m, s_first = steps[j]
                pa, Bpa, va, vb_, nk, ti_, q_lo, ncv = front_info.pop(j)
                bk = OB0 + 2 * (h % 2) + m
                first = (h, m) not in started
                started.add((h, m))
                pe.do([Bpa, vb_], [Bbank[bk]], lambda e: e.matmul(
                    bank[bk][:, q_lo:q_lo + ncv], lhsT=va[0:nk, ti_, 0:128], rhs=pa[0:nk, 0:ncv],
                    start=first, stop=False, skip_group_check=True))
                lbk = LB0 + (h % 2)

                def lmm(rhs, rb, nk_, q_lo_, ncv_):
                    firstl = ("l", h) not in started
                    started.add(("l", h))
                    pe.do([rb, Besel], [Bbank[lbk]], lambda e: e.matmul(
                        bank[lbk][0:2, q_lo_:q_lo_ + ncv_], lhsT=esel[0:nk_, 2 * m:2 * m + 2], rhs=rhs[0:nk_, 0:ncv_],
                        start=firstl, stop=False, skip_group_check=True))

                key = (h, m)
                full = (nk == 128 and q_lo == 0 and ncv == ncols_all)
                islast = (j == last_hm[key])
                if key in pend:
                    ppa, Bppa, pnk, pq, pncv = pend.pop(key)
                    if full:
                        s2, Bs2 = p2R.next()
                        ve.do([Bppa, Bpa], [Bs2], lambda e: e.tensor_tensor(out=s2[:, 0:ncv], in0=ppa[:, 0:ncv], in1=pa[:, 0:ncv], op=ALU.add))
                        deferred.append((cur_iter[0] + 2, lambda: lmm(s2, Bs2, 128, 0, ncv)))
                    else:
                        lmm(ppa, Bppa, pnk, pq, pncv)
                        lmm(pa, Bpa, nk, q_lo, ncv)
                elif full and not islast:
                    pend[key] = (pa, Bpa, nk, q_lo, ncv)
                else:
                    lmm(pa, Bpa, nk, q_lo, ncv)

            def finalize(h, stg):
                hp = h % 2
                nq = ncols_all
                lbk = LB0 + hp
                b0, b1 = OB0 + 2 * hp, OB0 + 2 * hp + 1
                if stg == 0:
                    ac.do([Bbank[lbk]], [Brl2[hp]], lambda e: e.activation(out=rl2[0:2, hp, 0:nq], in_=bank[lbk][0:2, 0:nq], func=AF.Ln))
                    ac.do([Brl2[hp]], [Brl2[hp]], lambda e: e.activation(out=rl2[0:2, hp, 0:nq], in_=rl2[0:2, hp, 0:nq], func=AF.Exp, scale=-1.0))
                    ac.do([Brl2[hp]], [Brl2[hp]], lambda e: e.copy(out=rlh[0:2, hp, 0:nq], in_=rl2[0:2, hp, 0:nq]))
                    ve.do([Brl2[hp]], [Brl2[hp]], lambda e: e.tensor_tensor(out=rll[0:2, hp, 0:nq], in0=rl2[0:2, hp, 0:nq], in1=rlh[0:2, hp, 0:nq], op=ALU.subtract))
                elif stg == 1:
                    for m in range(2):
                        lb, Blb = stR.next()
                        pe.do([Brl2[hp], Bbsel], [Blb], lambda e, m=m, lb=lb: e.matmul(lb[:, 0:nq], lhsT=bselb[0:2, 128 * m:128 * m + 128], rhs=rlh[0:2, hp, 0:nq], start=True, stop=False))
                        pe.do([Brl2[hp], Bbsel], [Blb], lambda e, m=m, lb=lb: e.matmul(lb[:, 0:nq], lhsT=bselb[0:2, 128 * m:128 * m + 128], rhs=rll[0:2, hp, 0:nq], start=False, stop=True))
                        ac.do([Blb], [Brlv[m]], lambda e, m=m, lb=lb: e.copy(out=rlv[m][:, 0:nq], in_=lb[:, 0:nq]))
                elif stg == 2:
                    ve.do([Bbank[b0], Brlv[0]], [Bofv], lambda e: e.tensor_tensor(out=ofv[:, 0:nq], in0=bank[b0][:, 0:nq], in1=rlv[0][:, 0:nq], op=ALU.mult))
                    ve.do([Bbank[b1], Brlv[1]], [Botv], lambda e: e.tensor_tensor(out=otv[:, 0:nq], in0=bank[b1][:, 0:nq], in1=rlv[1][:, 0:nq], op=ALU.mult))
                    ve.do([Bofv, Botv, Bnlam], [Bofv], lambda e: e.scalar_tensor_tensor(
                        out=ofv[:, 0:nq], in0=otv[:, 0:nq], scalar=nlam[:, 0:1], in1=ofv[:, 0:nq], op0=ALU.mult, op1=ALU.add))
                    ac.do([Bofv], [Bsqov], lambda e: e.activation(out=sqov[:, 0:nq], in_=ofv[:, 0:nq], func=AF.Square))
                elif stg == 3:
                    sbk, Bsbk = stR.next()
                    pe.do([Bsqov, Bones], [Bsbk], lambda e: e.matmul(sbk[:, 0:nq], lhsT=ones[:, :], rhs=sqov[:, 0:nq], start=True, stop=True))
                    ac.do([Bsbk], [Brsov], lambda e: e.activation(out=rsov[:, 0:nq], in_=sbk[:, 0:nq], func=AF.Ln, scale=1.0 / 128, bias=EPS))
                    ac.do([Brsov], [Brsov], lambda e: e.activation(out=rsov[:, 0:nq], in_=rsov[:, 0:nq], func=AF.Exp, scale=-0.5))
                else:
                    ve.do([Bofv, Brsov, BGp], [Bxn[h]], lambda e: e.scalar_tensor_tensor(
                        out=xn[:, h, c0:c0 + nq], in0=ofv[:, 0:nq], scalar=Gp[:, 0:1], in1=rsov[:, 0:nq], op0=ALU.mult, op1=ALU.mult))

            pending = []
            nst = len(steps)
            for j in range(nst + LAG):
                cur_iter[0] = j
                if j < nst:
                    front(j)
                jb = j - LAG
                if jb >= 0:
                    back(jb)
                    hb = steps[jb][0]
                    if jb == head_last_step[hb]:
                        while deferred:
                            deferred.pop(0)[1]()
                        for stg in range(5):
                            pending.append((j + 1 + 2 * stg, hb, stg))
                        pending.sort(key=lambda x: (x[1], x[2]))
                while deferred and deferred[0][0] <= j:
                    deferred.pop(0)[1]()
                while pending and pending[0][0] <= j:
                    _, h_, stg_ = pending.pop(0)
                    finalize(h_, stg_)
            while deferred:
                deferred.pop(0)[1]()
            pending.sort(key=lambda x: (x[1], x[2]))
            while pending:
                _, h_, stg_ = pending.pop(0)
                finalize(h_, stg_)

        def layer_b(seq, c0, c1, subs, ktiles, q_base_sub):
            n = c1 - c0
            rmsnorm(24, c0, c1)
            for g in range(2):
                wv, wbuf = w_next(("q", g))
                for jj in range(4):
                    h = 4 * g + jj
                    pb, Bpb = accR.next()
                    for kc in range(8):
                        pe.do([wbuf, Bxn[kc]], [Bpb], lambda e, kc=kc, jj=jj, pb=pb, wv=wv: e.matmul(
                            pb[:, 0:n], lhsT=wv[:, kc, 128 * jj:128 * jj + 128], rhs=xn[:, kc, c0:c1], start=(kc == 0), stop=(kc == 7)))
                    ac.do([Bpb], [Bqz[h]], lambda e, h=h, pb=pb: e.copy(out=qz[0:64, h, 0, c0:c1], in_=pb[0:64, 0:n]))
                    ve.do([Bpb], [Bqz[h]], lambda e, h=h, pb=pb: e.tensor_copy(out=qz[64:128, h, 1, c0:c1], in_=pb[64:128, 0:n]))
            attention(seq, c0, subs, ktiles, q_base_sub)
            for g in range(2):
                wv, wbuf = w_next(("o", g))
                for jj in range(4):
                    j = 4 * g + jj
                    pb, Bpb = accR.next()
                    for kc in range(8):
                        pe.do([wbuf, Bxn[kc]], [Bpb], lambda e, kc=kc, jj=jj, pb=pb, wv=wv: e.matmul(
                            pb[:, 0:n], lhsT=wv[:, kc, 128 * jj:128 * jj + 128], rhs=xn[:, kc, c0:c1], start=(kc == 0), stop=(kc == 7)))
                    resid_add(j, pb, Bpb, c0, c1)

        def final_out(c0, subs, dst_fn):
            c1 = c0 + subs[-1][0] + subs[-1][1]
            rmsnorm(40, c0, c1, out_f32=True)
            for si, (off, ns) in enumerate(subs):
                for half in range(2):
                    pb, Bpb = accR.next()
                    for j in range(4):
                        c = 4 * half + j
                        pe.do([BhT[c], Bident], [Bpb], lambda e, pb=pb, j=j, c=c, off=off, ns=ns: e.transpose(
                            out=pb[0:ns, 128 * j:128 * j + 128], in_=hT[:, c, c0 + off:c0 + off + ns], identity=ident[:, :]))
                    oa, ob, osl = oR.next()
                    ac.do([Bpb], [ob], lambda e, oa=oa, pb=pb, ns=ns: e.copy(out=oa[0:ns, :], in_=pb[0:ns, 0:512]))
                    ac.dma([ob], [], osl, dst_fn(si, half), oa[0:ns, :])

        def x_fetch(parts_list):
            got = []
            for parts in parts_list:
                for half in range(2):
                    xa, xb_, xsl = xR.next()
                    for (src, r0, nr) in parts:
                        sy.dma([], [xb_] + x_alias, xsl, xa[r0:r0 + nr, :], src[:, 512 * half:512 * half + 512])
                    got.append((xa, xb_, half, parts[-1][1] + parts[-1][2]))
            return got

        def x_consume(got):
            for (xa, xb_, half, nr_tot) in got:
                pb, Bpb = accR.next()
                for j in range(4):
                    pe.do([xb_, Bident], [Bpb], lambda e, pb=pb, xa=xa, j=j, nr_tot=nr_tot: e.transpose(
                        out=pb[:, 128 * j:128 * j + nr_tot], in_=xa[0:nr_tot, 128 * j:128 * j + 128], identity=ident[0:nr_tot, 0:nr_tot]))
                yield half, pb, Bpb, nr_tot

        NS_ = NMETA + DEC
        for half, pb, Bpb, nr in x_consume(x_fetch([[(meta, 0, NMETA), (xs, NMETA, DEC)]])):
            q_ = evac_alt(half)
            q_.do([Bpb], [BhT[4 * half + j] for j in range(4)], copy_on(
                q_, hT[:, 4 * half:4 * half + 4, 0:nr], pb[:, :].rearrange("p (j t) -> p j t", j=4)[:, :, 0:nr]))
        segs = [(0, NMETA, ST_M), (NMETA, DEC, ST_S)]
        xrv = layer_a(NS_, segs)
        layer_a_chunks(NS_, segs, xrv)
        mlp(0, 8, 0, NS_)
        kv_stage(
            NS_, [(0, NS_)],
            kouts=[[(lambda g2: mk_p[0, :, 512 * g2:512 * g2 + 512], 0, NMETA),
                    (lambda g2: mk_p[1, :, 512 * g2:512 * g2 + 512], 0, NMETA),
                    (lambda g2: k_s[:, 512 * g2:512 * g2 + 512], NMETA, NS_)]],
            vouts=[[(lambda g2: mv_p[0, :, 512 * g2:512 * g2 + 512], 0, NMETA),
                    (lambda g2: mv_p[1, :, 512 * g2:512 * g2 + 512], 0, NMETA),
                    (lambda g2: v_s[:, 512 * g2:512 * g2 + 512], NMETA, NS_)]],
            kt_dsts=[(0, 0, 0, NMETA, 0), (1, 0, 0, NMETA, 0), (2, 1, NMETA, DEC, NMETA + PAST)],
            va_dsts=[(0, 0, 0, 0, NMETA, 0, 0), (1, 0, 0, 0, NMETA, 0, 0), (2, 1, 0, NMETA, NS_, SKT - 1, 0)],
        )
        gp.dma([Bhst[ST_S]], [], Sgen, sh_s[0].rearrange("(c p) -> p c", p=128), hstate[:, ST_S, :], allow_slow_non_contiguous=True)
        for j3 in range(3):
            gp.dma([Bct[ST_S]], [], Sgen, sc_s[0, j3].rearrange("(c p) -> p c", p=128), ctail[:, ST_S, :, j3], allow_slow_non_contiguous=True)
        for stt in (ST_A, ST_B):
            ve.do([Bhst[ST_M]], [Bhst[stt]], lambda e, stt=stt: e.tensor_copy(out=hstate[:, stt, :], in_=hstate[:, ST_M, :]))
            ve.do([Bct[ST_M]], [Bct[stt]], lambda e, stt=stt: e.tensor_copy(out=ctail[:, stt, :, :], in_=ctail[:, ST_M, :, :]))
        s_kt = [(0, NMETA, 0, -1, [BKT[2][0], BVA[2][0]])]
        for t in range(PAST // 128):
            s_kt.append((1 + t, 128, NMETA + 128 * t, t, [BKT[2][0], BVA[2][0]]))
        s_kt.append((SKT - 1, DEC, NMETA + PAST, PAST // 128, [BKT[2][1], BVA[2][1]]))
        hS = K.sb("hS", [128, 8, DEC], F32)
        BhS = Buf("hS")
        ve.do(BhT, [BhS], lambda e: e.tensor_copy(out=hS[:], in_=hT[:, :, NMETA:NS_]))

        def sample_tail():
            ve.do([BhS], BhT, lambda e: e.tensor_copy(out=hT[:, :, NMETA:NS_], in_=hS[:]))
            layer_b(2, NMETA, NS_, [(0, DEC)], s_kt, PAST // 128)
            mlp(1, 32, NMETA, NS_)
            final_out(NMETA, [(0, DEC)], lambda si, half: y_s[:, 512 * half:512 * half + 512])

        ptiles = [(seq_, i_) for seq_ in range(2) for i_ in range(NTL)]

        def fetch_tile(ti):
            seq_, i_ = ptiles[ti]
            return x_fetch([[(xp[seq_, NT * i_ + 128 * s_:NT * i_ + 128 * s_ + 128, :], 0, 128)] for s_ in range(4)])

        xgot = {0: fetch_tile(0)}
        for tix, (seq, i) in enumerate(ptiles):
            stt = ST_A + seq
            if True:
                f0 = NT * i
                got = xgot.pop(tix)
                for gi_, (half, pb, Bpb, nr) in enumerate(x_consume(got)):
                    s = gi_ // 2
                    q_ = evac_alt(half)
                    q_.do([Bpb], [BhT[4 * half + j] for j in range(4)], copy_on(
                        q_, hT[:, 4 * half:4 * half + 4, 128 * s:128 * s + 128], pb[:, :].rearrange("p (j t) -> p j t", j=4)))
                segs = [(0, NT, stt)]
                xrv = layer_a(NT, segs)
                layer_a_chunks(NT, segs, xrv)
                mlp(0, 8, 0, NT)
                subs4 = [(128 * s, 128) for s in range(4)]
                kv_stage(
                    NT, subs4,
                    kouts=[[(lambda g2, s=s: k_p[seq, f0 + 128 * s:f0 + 128 * s + 128, 512 * g2:512 * g2 + 512], 0, 128)] for s in range(4)],
                    vouts=[[(lambda g2, s=s: v_p[seq, f0 + 128 * s:f0 + 128 * s + 128, 512 * g2:512 * g2 + 512], 0, 128)] for s in range(4)],
                    kt_dsts=[(seq, 1 + i, 0, NT, NMETA + f0)],
                    va_dsts=[(seq, 1 + i, s, 0, 128, 1 + 4 * i + s, 0) for s in range(4)],
                )
                p_kt = [(0, NMETA, 0, -1, [BKT[seq][0], BVA[seq][0]])]
                for t in range(4 * (i + 1)):
                    p_kt.append((1 + t, 128, NMETA + 128 * t, t, [BKT[seq][1 + t // 4], BVA[seq][1 + t // 4]]))
                layer_b(seq, 0, NT, subs4, p_kt, 4 * i)
                if tix + 1 < len(ptiles):
                    xgot[tix + 1] = fetch_tile(tix + 1)
                mlp(1, 32, 0, NT)
                final_out(0, subs4, lambda si, half, f0=f0, seq=seq: y_p[seq, f0 + 128 * si:f0 + 128 * si + 128, 512 * half:512 * half + 512])
            if i == NTL - 1:
                gp.dma([Bhst[stt]], [], Sgen, sh_p[seq].rearrange("(c p) -> p c", p=128), hstate[:, stt, :], allow_slow_non_contiguous=True)
                for j3 in range(3):
                    gp.dma([Bct[stt]], [], Sgen, sc_p[seq, j3].rearrange("(c p) -> p c", p=128), ctail[:, stt, :, j3], allow_slow_non_contiguous=True)

        sample_tail()

        for (_, _, sl) in oR.items:
            ac.wait_ev((sl.sem, sl.cnt))
        gp.wait_ev((Sgen.sem, Sgen.cnt))
        gp.wait_ev((Skst.sem, Skst.cnt))
        gp.wait_ev((Svast.sem, Svast.cnt))
        K.emit({"sync": sy, "gpsimd": gp, "tensor": pe, "vector": ve, "scalar": ac})
    return nc


def _vec_pm(v, n):
    return np.ascontiguousarray(np.asarray(v, np.float32).reshape(n, 128).T)


def make_in_maps(inp, SEQ, ncores=8):
    ohz, mask0, ident = _static_consts()
    g = lambda k: np.asarray(inp[k], np.float32)
    cst = np.zeros((128, 128), np.float32)
    cst[:, 0:8] = _vec_pm(g("norm_mix_g")[0], 8)
    cst[:, 8:16] = _vec_pm(g("norm_mlp_g")[0], 8)
    cst[:, 16:24] = _vec_pm(g("norm_kv_g"), 8)
    cst[:, 24:32] = _vec_pm(g("norm_mix_g")[1], 8)
    cst[:, 32:40] = _vec_pm(g("norm_mlp_g")[1], 8)
    cst[:, 40:48] = _vec_pm(g("norm_f_g"), 8)
    for j in range(4):
        cst[:, 48 + 10 * j:58 + 10 * j] = _vec_pm(g("conv_w")[0, j], 10)
    cst[:, 88:98] = _vec_pm(g("conv_b")[0], 10)
    cst[:, 98:108] = _vec_pm(g("b_gate_r")[0], 10)
    cst[:, 108:118] = _vec_pm(g("b_gate_i")[0], 10)
    cst[:, 118:128] = _vec_pm(g("lru_lambda")[0], 10)
    lamv = np.concatenate([g("lambda_q1")[0], g("lambda_k1")[0], g("lambda_q2")[0], g("lambda_k2")[0]])[None, :]
    shared = {
        "meta": g("meta_tokens"), "cst": cst,
        "w_in": g("w_in_a")[0], "w_gr": g("w_gate_r")[0], "w_gi": g("w_gate_i")[0], "w_out": g("w_out_a")[0],
        "w_up": g("w_mlp_up"), "w_down": g("w_mlp_down"), "w_kv": g("w_kv"), "w_q": g("w_q")[0], "w_o": g("w_o")[0],
        "lamv": np.ascontiguousarray(lamv), "subg": g("subln_g")[0][None, :].copy(), "relb": g("rel_bias"),
        "ohz": ohz, "mask0": mask0, "ident": ident,
        "bsel": np.concatenate([np.eye(2, dtype=np.float32)[:, 0:1].repeat(128, 1), np.eye(2, dtype=np.float32)[:, 1:2].repeat(128, 1)], axis=1),
    }
    maps = []
    for k in range(ncores):
        sst = np.zeros((128, 40), np.float32)
        sst[:, 0:10] = _vec_pm(g("state_h")[0, k], 10)
        sc = g("state_conv")[0, k]
        sst[:, 10:40] = np.stack([_vec_pm(sc[j], 10) for j in range(3)], axis=2).reshape(128, 30)
        m = dict(shared)
        m.update({
            "xp": np.ascontiguousarray(g("x_prompt")[2 * k:2 * k + 2, :SEQ]),
            "xs": np.ascontiguousarray(g("x_sample")[k]),
            "sst": sst,
            "cmk": np.ascontiguousarray(g("cache_meta_k")[k].reshape(NMETA, D)),
            "cmv": np.ascontiguousarray(g("cache_meta_v")[k].reshape(NMETA, D)),
            "ck": np.ascontiguousarray(g("cache_k")[k].reshape(PAST, D)),
            "cv": np.ascontiguousarray(g("cache_v")[k].reshape(PAST, D)),
        })
        maps.append(m)
    return maps


def gather(results, SEQ, ncores=8):
    cat = lambda k: np.concatenate([np.asarray(r[k]) for r in results], axis=0)
    y_p = cat("y_p")
    y_s = np.stack([np.asarray(r["y_s"]) for r in results], 0)
    sh_p = cat("sh_p")[None]
    sc_p = cat("sc_p")[None]
    mk_p = cat("mk_p").reshape(2 * ncores, NMETA, NH, 128)
    mv_p = cat("mv_p").reshape(2 * ncores, NMETA, NH, 128)
    k_p = cat("k_p").reshape(2 * ncores, SEQ, NH, 128)
    v_p = cat("v_p").reshape(2 * ncores, SEQ, NH, 128)
    sh_s = cat("sh_s")[None]
    sc_s = cat("sc_s")[None]
    k_s = np.stack([np.asarray(r["k_s"]) for r in results], 0).reshape(ncores, DEC, NH, 128)
    v_s = np.stack([np.asarray(r["v_s"]) for r in results], 0).reshape(ncores, DEC, NH, 128)
    outs = (y_p, y_s, sh_p, sc_p, mk_p, mv_p, k_p, v_p, sh_s, sc_s, k_s, v_s)
    return tuple(np.ascontiguousarray(o, dtype=np.float32) for o in outs)


def kernel(**inputs):
    SEQ = int(np.asarray(inputs["x_prompt"]).shape[1])
    nc = build(SEQ)
    maps = make_in_maps(inputs, SEQ, 8)
    res = run_bass_kernel_spmd(nc, maps, core_ids=list(range(8)))
    return gather(res.results, SEQ, 8)
```

```python
import math
from contextlib import ExitStack

import numpy as np
import concourse.bass as bass
import concourse.mybir as mybir
from concourse.bass_utils import run_bass_kernel_spmd

F32 = mybir.dt.float32
BF16 = mybir.dt.bfloat16
AF = mybir.ActivationFunctionType
ALU = mybir.AluOpType

D = 1024
DR = 1280
NRC = 10
DFF = 4096
NH = 8
EPS = 1e-6
NT = 512
PAST = 1024
NMETA = 16
DEC = 32
LAM_INIT = 0.8 - 0.6 * math.exp(-0.3 * 1)


class Buf:
    __slots__ = ("name", "w", "r")

    def __init__(self, name):
        self.name = name
        self.w = None
        self.r = {}


class Q:
    def __init__(self, K, name, track_self=True):
        self.name = name
        self.sem = K.new_sem("q_" + name)
        self.cnt = 0
        self.waited = {}
        self.track_self = track_self
        self.prog = []

    def wait_ev(self, ev):
        if ev is None:
            return
        sem, val = ev
        if (not self.track_self) and sem is self.sem:
            return
        k = id(sem)
        if self.waited.get(k, 0) >= val:
            return
        self.prog.append(lambda eng, s=sem, v=val: eng.wait_ge(s, v))
        self.waited[k] = val

    def deps(self, reads, writes):
        for b in reads:
            self.wait_ev(b.w)
        for b in writes:
            self.wait_ev(b.w)
            for ev in b.r.values():
                self.wait_ev(ev)

    @staticmethod
    def mark(ev, reads, writes):
        k = id(ev[0])
        for b in reads:
            old = b.r.get(k)
            if old is None or old[1] < ev[1]:
                b.r[k] = ev
        for b in writes:
            b.w = ev
            b.r = {}

    def do(self, reads, writes, fn):
        self.deps(reads, writes)
        self.cnt += 1
        self.prog.append(lambda eng, f=fn, s=self.sem: f(eng).then_inc(s, 1))
        ev = (self.sem, self.cnt)
        self.mark(ev, reads, writes)
        return ev

    def dma(self, reads, writes, slot, out, in_, **kw):
        self.deps(reads, writes)
        if slot.cnt > 0:
            self.wait_ev((slot.sem, slot.cnt))
        slot.cnt += 16
        self.prog.append(
            lambda eng, o=out, i=in_, k=kw, s=slot.sem: eng.dma_start(out=o, in_=i, **k).then_inc(s, 16))
        ev = (slot.sem, slot.cnt)
        self.mark(ev, reads, writes)
        return ev


class Slot:
    def __init__(self, K, name):
        self.sem = K.new_sem("d_" + name)
        self.cnt = 0


class Kern:
    def __init__(self, nc, stack):
        self.nc = nc
        self.stack = stack
        self.nsem = 0

    def new_sem(self, name):
        self.nsem += 1
        return self.stack.enter_context(self.nc.semaphore(name))

    def sb(self, name, shape, dt, stack=None):
        return (stack or self.stack).enter_context(self.nc.sbuf_tensor("s_" + name, shape, dt))

    def ps(self, name, shape, dt):
        return self.stack.enter_context(self.nc.psum_tensor(name, shape, dt))

    def emit(self, queues):
        with self.nc.Block() as block:
            for nm, q in queues.items():
                def body(eng, q=q):
                    for th in q.prog:
                        th(eng)
                getattr(block, nm)(body)


class Ring:
    def __init__(self, items):
        self.items = items
        self.i = 0

    def next(self):
        it = self.items[self.i % len(self.items)]
        self.i += 1
        return it


def _t5_bucket(rel):
    nb = 16
    max_exact = 8
    ret = np.where(rel > 0, nb, 0)
    n = np.abs(rel)
    nf = np.maximum(n, 1).astype(np.float32)
    large = max_exact + (np.log(nf / max_exact) / math.log(128 / max_exact) * (nb - max_exact)).astype(np.int32)
    large = np.minimum(large, nb - 1)
    return ret + np.where(n < max_exact, n, large)


def _static_consts():
    rel = 127 - np.arange(384)
    bk = _t5_bucket(rel.astype(np.int32))
    ohz = np.zeros((32, 384), np.float32)
    ohz[bk, np.arange(384)] = 1.0
    kp = np.arange(128)[:, None]
    qf = np.arange(128)[None, :]
    mask0 = ((kp // 64) <= (qf // 64)).astype(np.float32)
    ident = np.eye(128, dtype=np.float32)
    return ohz, mask0, ident


def build(SEQ):
    assert SEQ % NT == 0
    NTL = SEQ // NT
    NKEY = NMETA + SEQ
    NKT = 1 + SEQ // 128
    SKEY = NMETA + PAST + DEC
    SKT = 1 + PAST // 128 + 1
    NKEYM = max(NKEY, SKEY)
    NKTM = max(NKT, SKT)

    nc = bass.Bass("TRN2", target_bir_lowering=False)

    def din(name, shape, dt=F32):
        return nc.dram_tensor(name, shape, dt, kind="ExternalInput").ap()

    def dout(name, shape):
        return nc.dram_tensor(name, shape, F32, kind="ExternalOutput").ap()

    def dscr(name, shape, dt):
        return nc.dram_tensor(name, shape, dt, kind="Internal").ap()

    xp = din("xp", [2, SEQ, D])
    xs = din("xs", [DEC, D])
    meta = din("meta", [NMETA, D])
    cst_d = din("cst", [128, 128])
    sst_d = din("sst", [128, 40])
    cmk = din("cmk", [NMETA, D])
    cmv = din("cmv", [NMETA, D])
    ck = din("ck", [PAST, D])
    cv = din("cv", [PAST, D])
    w_in = din("w_in", [D, 2 * DR])
    w_gr = din("w_gr", [NRC, 128, 128])
    w_gi = din("w_gi", [NRC, 128, 128])
    w_out = din("w_out", [DR, D])
    w_up = din("w_up", [2, D, DFF])
    w_down = din("w_down", [2, DFF, D])
    w_kv = din("w_kv", [D, 2 * D])
    w_q = din("w_q", [D, D])
    w_o = din("w_o", [D, D])
    lamv = din("lamv", [1, 256])
    subg = din("subg", [1, 128])
    relb = din("relb", [32, 8])
    ohz_d = din("ohz", [32, 384])
    mask0_d = din("mask0", [128, 128])
    ident_d = din("ident", [128, 128])
    bsel_d = din("bsel", [2, 256])

    y_p = dout("y_p", [2, SEQ, D])
    y_s = dout("y_s", [DEC, D])
    sh_p = dout("sh_p", [2, DR])
    sc_p = dout("sc_p", [2, 3, DR])
    mk_p = dout("mk_p", [2, NMETA, D])
    mv_p = dout("mv_p", [2, NMETA, D])
    k_p = dout("k_p", [2, SEQ, D])
    v_p = dout("v_p", [2, SEQ, D])
    sh_s = dout("sh_s", [1, DR])
    sc_s = dout("sc_s", [1, 3, DR])
    k_s = dout("k_s", [DEC, D])
    v_s = dout("v_s", [DEC, D])

    wb_in = dscr("wb_in", [D, 2 * DR], BF16)
    wb_out = dscr("wb_out", [DR, D], BF16)
    wb_up = dscr("wb_up", [2, D, DFF], BF16)
    wb_down = dscr("wb_down", [2, DFF, D], BF16)
    wb_kv = dscr("wb_kv", [D, 2 * D], BF16)
    wb_q = dscr("wb_q", [D, D], BF16)
    wb_o = dscr("wb_o", [D, D], BF16)
    KTs = dscr("KTs", [3, NH, 128, NKEYM], BF16)
    VAs = dscr("VAs", [3, NH, 128, NKTM, 130], BF16)
    zs = dscr("zs", [128, NH, 384], F32)

    with ExitStack() as st:
        K = Kern(nc, st)
        sy = Q(K, "sync")
        gp = Q(K, "gpsimd")
        pe = Q(K, "tensor", track_self=False)
        ve = Q(K, "vector")
        ac = Q(K, "scalar")

        cst = K.sb("cst", [128, 128], F32)
        Bcst = Buf("cst")
        dc = K.sb("dc", [128, 64], F32)
        Bdc = Buf("dc")
        ident = K.sb("ident", [128, 128], F32)
        Bident = Buf("ident")
        ones = K.sb("ones", [128, 128], BF16)
        Bones = Buf("ones")
        wg = K.sb("wg", [128, 2, NRC, 128], BF16)
        Bwg = Buf("wg")
        EBD = K.sb("EBD", [128, NH, 256], BF16)
        EBM = K.sb("EBM", [128, NH, 128], BF16)
        BEB = Buf("EB")
        Gt = K.sb("Gt", [128, 128], F32)
        BG = Buf("G")
        nlam = K.sb("nlam", [128, 1], F32)
        Bnlam = Buf("nlam")
        Gp = K.sb("Gp", [128, 1], F32)
        BGp = Buf("Gp")
        esel = K.sb("esel", [128, 4], BF16)
        Besel = Buf("esel")
        bsel = K.sb("bsel", [2, 256], F32)
        Bbsel = Buf("bsel")
        rl2 = K.sb("rl2", [2, 2, NT], F32)
        rlh = K.sb("rlh", [2, 2, NT], BF16)
        rll = K.sb("rll", [2, 2, NT], BF16)
        Brl2 = [Buf("rl2_0"), Buf("rl2_1")]
        bselb = K.sb("bselb", [2, 256], BF16)
        hT = K.sb("hT", [128, 8, NT], F32)
        BhT = [Buf(f"hT{c}") for c in range(8)]
        xn = K.sb("xn", [128, 8, NT], BF16)
        Bxn = [Buf(f"xn{c}") for c in range(8)]
        qz = K.sb("qz", [128, NH, 2, NT], BF16)
        Bqz = [Buf(f"qz{c}") for c in range(NH)]
        sqc = K.sb("sqc", [128, 2, NT], BF16)
        sqR = Ring([(sqc[:, i, :], Buf(f"sqc{i}")) for i in range(2)])
        rstd = K.sb("rstd", [128, NT], F32)
        Brstd = Buf("rstd")
        NWS = 3
        wsl = K.sb("wsl", [128, NWS, 4096], BF16)
        wR = Ring([(wsl[:, i, :], Buf(f"ws{i}"), Slot(K, f"ws{i}")) for i in range(NWS)])
        NKVO = 3
        kvo = K.sb("kvo", [128, NKVO, 512], F32)
        oR = Ring([(kvo[:, i, :], Buf(f"kvo{i}"), Slot(K, f"kvo{i}")) for i in range(NKVO)])
        vast = K.sb("vast", [128, 4, NH, 130], BF16)
        Bvast = Buf("vast")
        Svast = Slot(K, "vast")
        Skst = Slot(K, "kst")
        pages = K.sb("pages", [128, 32, NT], BF16)
        XPW = NT + 4
        xpall = K.sb("xpall", [128, NRC, XPW], F32)
        Bxp = [Buf(f"xp{c}") for c in range(NRC)]
        Bpg = [Buf(f"pg{i}") for i in range(32)]
        hstate = K.sb("hstate", [128, 4, NRC], F32)
        Bhst = [Buf(f"hst{i}") for i in range(4)]
        ctail = K.sb("ctail", [128, 4, NRC, 3], F32)
        Bct = [Buf(f"ct{i}") for i in range(4)]
        ST_M, ST_S, ST_A, ST_B = 0, 1, 2, 3

        PS = K.ps("PS", [128, 8 * 512], F32)
        bank = [PS[:, 512 * k:512 * (k + 1)] for k in range(8)]
        Bbank = [Buf(f"bank{k}") for k in range(8)]
        accR = Ring([(bank[k], Bbank[k]) for k in range(8)])
        stR = Ring([(bank[k], Bbank[k]) for k in (0, 1)])
        LB0 = 2
        OTB = 0
        OB0 = 4

        def page_f32(i0):
            return pages[:, i0:i0 + 2, :].rearrange("p a n -> p (a n)").bitcast(F32), [Bpg[i0], Bpg[i0 + 1]]

        Bw = {}
        BKT = [[Buf(f"KT{s}_{i}") for i in range(NTL + 2)] for s in range(3)]
        BVA = [[Buf(f"VA{s}_{i}") for i in range(NTL + 2)] for s in range(3)]
        Bzs = Buf("zs")
        Sgen = Slot(K, "gen")
        Ssy = Slot(K, "sygen")

        def gp_dma_sync(reads, writes, out, in_, **kw):
            ev = gp.dma(reads, writes, Sgen, out, in_, **kw)
            return ev

        st2 = ExitStack()
        xin = K.sb("xin", [128, 2, 512], F32, stack=st2)
        xR = Ring([(xin[:, i, :], Buf(f"xin{i}"), Slot(K, f"xin{i}")) for i in range(2)])
        def convert(src, dst, name, nsplit=1):
            n = 1
            for s_ in src.shape:
                n *= s_
            names = " ".join(f"d{i}" for i in range(len(src.shape)))
            s2 = src.rearrange(f"{names} -> ({names})").rearrange("(p f) -> p f", p=128)
            d2 = dst.rearrange(f"{names} -> ({names})").rearrange("(p f) -> p f", p=128)
            f = n // 128
            step = f // nsplit
            Bw[name] = []
            for i in range(nsplit):
                b_ = Buf(f"wb_{name}_{i}")
                sl = Slot(K, f"cv_{name}_{i}")
                gp.dma([], [b_], sl, d2[:, i * step:(i + 1) * step], s2[:, i * step:(i + 1) * step])
                Bw[name].append(b_)

        sy.dma([], [Bcst], Ssy, cst[:], cst_d[:, :])
        sy.dma([], [Bident], Ssy, ident[:], ident_d[:, :])
        gp.dma([], [Bwg], Sgen, wg[:, 0, :, :], w_gr.rearrange("n i j -> i n j"))
        gp.wait_ev((Sgen.sem, Sgen.cnt))
        gp.dma([], [Bwg], Sgen, wg[:, 1, :, :], w_gi.rearrange("n i j -> i n j"))
        gp.wait_ev((Sgen.sem, Sgen.cnt))

        ve.do([], [Bones], lambda e: e.memset(ones[:], 1.0))
        ve.do([], [Besel], lambda e: e.memset(esel[:], 0.0))
        ve.do([Besel], [Besel], lambda e: e.memset(esel[:, 0:1], 1.0))
        ve.do([Besel], [Besel], lambda e: e.memset(esel[:, 3:4], 1.0))
        sy.dma([], [Bbsel], Ssy, bsel[:], bsel_d[:, :])
        ve.do([Bbsel], [Bbsel], lambda e: e.tensor_copy(out=bselb[:], in_=bsel[:]))
        ac.do([Bcst], [Bdc], lambda e: e.activation(out=dc[:, 40:50], in_=cst[:, 118:128], func=AF.Exp, scale=-1.0))
        ac.do([Bdc], [Bdc], lambda e: e.activation(out=dc[:, 50:60], in_=dc[:, 40:50], func=AF.Ln, bias=1.0))
        ve.do([Bdc], [Bdc], lambda e: e.tensor_scalar_mul(out=dc[:, 0:10], in0=dc[:, 50:60], scalar1=-4.0))
        ve.do([Bdc], [Bdc], lambda e: e.tensor_scalar_mul(out=dc[:, 10:20], in0=dc[:, 50:60], scalar1=4.0))
        ve.do([Bdc], [Bdc], lambda e: e.tensor_scalar_mul(out=dc[:, 40:50], in0=dc[:, 50:60], scalar1=-8.0))
        ve.do([Bcst, Bdc], [Bdc], lambda e: e.tensor_scalar_mul(out=dc[:, 20:40], in0=cst[:, 98:118], scalar1=0.5))

        lmt = K.sb("lmt", [128, 256], F32, stack=st2)
        Blmt = Buf("lmt")
        sy.dma([], [Blmt], Ssy, lmt[:], lamv.partition_broadcast(128))
        lmp = K.sb("lmp", [128, 2, 64], F32, stack=st2)
        Blmp = Buf("lmp")
        lms = K.sb("lms", [128, 4], F32, stack=st2)
        Blms = Buf("lms")
        ve.do([Blmt], [Blmp], lambda e: e.tensor_tensor(out=lmp[:, 0, :], in0=lmt[:, 0:64], in1=lmt[:, 64:128], op=ALU.mult))
        ve.do([Blmt], [Blmp], lambda e: e.tensor_tensor(out=lmp[:, 1, :], in0=lmt[:, 128:192], in1=lmt[:, 192:256], op=ALU.mult))
        ve.do([Blmp], [Blms], lambda e: e.reduce_sum(out=lms[:, 0:2], in_=lmp[:, :, :], axis=mybir.AxisListType.X))
        ac.do([Blms], [Blms], lambda e: e.activation(out=lms[:, 2:4], in_=lms[:, 0:2], func=AF.Exp))
        ve.do([Blms], [Bnlam], lambda e: e.tensor_tensor(out=nlam[:], in0=lms[:, 3:4], in1=lms[:, 2:3], op=ALU.subtract))
        ve.do([Bnlam], [Bnlam], lambda e: e.tensor_scalar_add(out=nlam[:], in0=nlam[:], scalar1=-LAM_INIT))
        sy.dma([], [BG], Ssy, Gt[:], subg.partition_broadcast(128))
        ve.do([BG], [BG], lambda e: e.tensor_scalar_mul(out=Gt[:], in0=Gt[:], scalar1=1.0 - LAM_INIT))
        sy.dma([], [BGp], Ssy, Gp[:], subg.rearrange("o p -> p o"), allow_slow_non_contiguous=True)
        ve.do([BGp], [BGp], lambda e: e.tensor_scalar_mul(out=Gp[:], in0=Gp[:], scalar1=1.0 - LAM_INIT))

        ohz = K.sb("ohz", [32, 384], F32, stack=st2)
        Bohz = Buf("ohz")
        rbt = K.sb("rbt", [32, 8], F32, stack=st2)
        Brbt = Buf("rbt")
        sy.dma([], [Bohz], Ssy, ohz[:], ohz_d[:, :])
        sy.dma([], [Brbt], Ssy, rbt[:], relb[:, :])
        onef = K.sb("onef", [32, 128], F32, stack=st2)
        Bonef = Buf("onef")
        ve.do([], [Bonef], lambda e: e.memset(onef[:], 1.0))
        ohs = K.sb("ohs", [32, 384], F32, stack=st2)
        Bohs = Buf("ohs")
        zall, Bzall = pages[:, 0:24, :].rearrange("p a n -> p (a n)").bitcast(F32), [Bpg[i] for i in range(24)]
        zall3 = zall[:, 0:NH * 384].rearrange("p (h r) -> p h r", h=NH)
        negc = K.sb("negc", [128, NH], F32, stack=st2)
        Bnegc = Buf("negc")
        for h in range(NH):
            ve.do([Bohz, Brbt], [Bohs], lambda e, h=h: e.tensor_scalar_mul(out=ohs[:], in0=ohz[:], scalar1=rbt[:, h:h + 1]))
            pe.do([Bohs, Bonef], [Bbank[0]], lambda e: e.matmul(bank[0][:, 0:384], lhsT=onef[:, :], rhs=ohs[:, :], start=True, stop=True))
            ve.do([Bbank[0]], [Bnegc], lambda e, h=h: e.tensor_scalar_mul(out=negc[:, h:h + 1], in0=bank[0][:, 382:383], scalar1=-1.0))
            ac.do([Bbank[0], Bnegc], Bzall, lambda e, h=h: e.activation(out=zall3[:, h, :], in_=bank[0][:, 0:384], func=AF.Exp, bias=negc[:, h:h + 1]))
        gp_dma_sync(Bzall, [Bzs], zs[:, :, :], zall3)
        gp.wait_ev((Sgen.sem, Sgen.cnt))
        d0f = K.sb("d0f", [128, NH, 256], F32, stack=st2)
        Bd0f = Buf("d0f")
        msk = K.sb("msk", [128, 128], F32, stack=st2)
        Bmsk = Buf("msk")
        sy.dma([], [Bmsk], Ssy, msk[:], mask0_d[:, :])
        sy.dma([Bzs], [Bd0f], Ssy, d0f[:, :, 0:128], bass.AP(zs.tensor, 127, [[NH * 384 - 1, 128], [384, NH], [1, 128]]))
        sy.wait_ev((Ssy.sem, Ssy.cnt))
        sy.dma([Bzs], [Bd0f], Ssy, d0f[:, :, 128:256], bass.AP(zs.tensor, 255, [[NH * 384 - 1, 128], [384, NH], [1, 128]]))
        sy.wait_ev((Ssy.sem, Ssy.cnt))
        for h in range(NH):
            ve.do([Bd0f, Bmsk], [Bd0f], lambda e, h=h: e.tensor_tensor(out=d0f[:, h, 0:128], in0=d0f[:, h, 0:128], in1=msk[:], op=ALU.mult))
        ve.do([Bd0f], [BEB], lambda e: e.tensor_copy(out=EBD[:], in_=d0f[:]))
        mf = K.sb("mf", [16, NH, 128], F32, stack=st2)
        Bmf = Buf("mf")
        sy.dma([Bzs], [Bmf], Ssy, mf[:], bass.AP(zs.tensor, 143, [[NH * 384 - 1, 16], [384, NH], [1, 128]]))
        sy.wait_ev((Ssy.sem, Ssy.cnt))
        ve.do([Bmf], [BEB], lambda e: e.tensor_copy(out=EBM[0:16, :, :], in_=mf[:]))

        sstt = K.sb("sstt", [128, 40], F32, stack=st2)
        Bsst = Buf("sst")
        sy.dma([], [Bsst], Ssy, sstt[:], sst_d[:, :])
        ve.do([], [Bhst[ST_M]], lambda e: e.memset(hstate[:, ST_M, :], 0.0))
        ve.do([], [Bct[ST_M]], lambda e: e.memset(ctail[:, ST_M, :, :], 0.0))
        ve.do([Bsst], [Bhst[ST_S]], lambda e: e.tensor_copy(out=hstate[:, ST_S, :], in_=sstt[:, 0:10]))
        ve.do([Bsst], [Bct[ST_S]], lambda e: e.tensor_copy(out=ctail[:, ST_S, :, :], in_=sstt[:, 10:40].rearrange("p (c j) -> p c j", j=3)))
        ve.do([], [Bvast], lambda e: e.memset(vast[:, :, :, 128:129], 1.0))
        ve.do([], [Bvast], lambda e: e.memset(vast[:, :, :, 129:130], 0.0))

        def cache_tile(srck, srcv, nrow, kt, kcol):
            for half in range(2):
                xa, xb_, xsl = xR.next()
                sy.dma([], [xb_], xsl, xa[0:nrow, :], srck[:, 512 * half:512 * half + 512])
                for j in range(4):
                    h = 4 * half + j
                    pe.do([xb_, Bident], [Bbank[OTB]], lambda e, xa=xa, j=j: e.transpose(
                        out=bank[OTB][:, 128 * j:128 * j + nrow], in_=xa[0:nrow, 128 * j:128 * j + 128], identity=ident[0:nrow, 0:nrow]))
                ve.do([Bbank[OTB]], [Bpg[4 * half + j] for j in range(4)], lambda e, half=half: e.tensor_copy(
                    out=pages[:, 4 * half:4 * half + 4, 0:nrow],
                    in_=bank[OTB][:, :].rearrange("p (j t) -> p j t", j=4)[:, :, 0:nrow]))
            gp.dma(Bpg[0:8], [BKT[2][0]], Skst, KTs[2].rearrange("h p n -> p h n")[:, :, kcol:kcol + nrow], pages[:, 0:8, 0:nrow])
            for half in range(2):
                xa, xb_, xsl = xR.next()
                sy.dma([], [xb_], xsl, xa[0:nrow, :], srcv[:, 512 * half:512 * half + 512])
                ve.do([xb_], [Bvast], lambda e, xa=xa, half=half: e.tensor_copy(
                    out=vast[0:nrow, 0, 4 * half:4 * half + 4, 0:128],
                    in_=xa[0:nrow, :].rearrange("p (h c) -> p h c", h=4)))
            gp.dma([Bvast], [BVA[2][0]], Svast, VAs[2].rearrange("h p t c -> p t h c")[0:nrow, kt, :, :], vast[0:nrow, 0, :, :])

        convert(w_in, wb_in, "in", 2)
        convert(w_out, wb_out, "out", 1)
        cache_tile(cmk, cmv, NMETA, 0, 0)
        for t in range(PAST // 128):
            cache_tile(ck[128 * t:128 * t + 128, :], cv[128 * t:128 * t + 128, :], 128, 1 + t, NMETA + 128 * t)
        convert(w_up[0], wb_up[0], "up0", 4)
        convert(w_down[0], wb_down[0], "down0", 4)
        convert(w_kv, wb_kv, "kv", 2)
        convert(w_q, wb_q, "q", 1)
        convert(w_o, wb_o, "o", 1)
        convert(w_up[1], wb_up[1], "up1", 4)
        convert(w_down[1], wb_down[1], "down1", 4)


        def fence():
            qs = [sy, gp, pe, ve, ac]
            slots = [Ssy, Sgen, Skst, Svast] + [x[2] for x in xR.items] + [x[2] for x in wR.items] + [x[2] for x in oR.items]
            for q in qs:
                for p in qs:
                    if p is not q and p.cnt > 0:
                        q.wait_ev((p.sem, p.cnt))
                for sl in slots:
                    if sl.cnt > 0:
                        q.wait_ev((sl.sem, sl.cnt))
        fence()
        st2.close()
        SHR = 22528
        shr = K.sb("shr", [128, SHR], BF16)

        class Carver:
            def __init__(self):
                self.off = 0

            def bf(self, n):
                v = shr[:, self.off:self.off + n]
                self.off += n
                assert self.off <= SHR
                return v

            def f32(self, n):
                v = shr[:, self.off:self.off + 2 * n].bitcast(F32)
                self.off += 2 * n
                assert self.off <= SHR
                return v

        cv1 = Carver()
        NSET = 4
        TW = NT + 4
        tmpv = [[cv1.f32(TW) for j in range(3)] for i in range(NSET)]
        Btmp = [[Buf(f"tmp{i}_{j}") for j in range(3)] for i in range(NSET)]
        tqv = [cv1.f32(NT) for i in range(5)]
        Btq = [Buf(f"tq{i}") for i in range(5)]
        ixv = [cv1.bf(NT) for i in range(5)]
        Bixc = [Buf(f"ixc{i}") for i in range(5)]
        xcbv = [cv1.bf(NT) for i in range(NSET)]
        Bxcb = [Buf(f"xcb{i}") for i in range(NSET)]
        cv2 = Carver()
        NKB = 3
        kR = Ring([(cv2.bf(1024), Buf(f"kb{i}"), Slot(K, f"kb{i}")) for i in range(NKB)])
        vR = Ring([(cv2.bf(8 * 130).rearrange("p (t c) -> p t c", t=8), Buf(f"vb{i}"), Slot(K, f"vb{i}")) for i in range(NKB)])
        NPT = 6
        pR = Ring([(cv2.bf(NT), Buf(f"pT{i}")) for i in range(NPT)])
        p2R = Ring([(cv2.bf(NT), Buf(f"p2s{i}")) for i in range(8)])
        rlv = [cv2.f32(NT) for m in range(2)]
        Brlv = [Buf(f"rlv{m}") for m in range(2)]
        ofv = cv2.f32(NT)
        Bofv = Buf("ofv")
        otv = cv2.f32(NT)
        Botv = Buf("otv")
        rsov = cv2.f32(NT)
        Brsov = Buf("rsov")
        sqov = cv2.bf(NT)
        Bsqov = Buf("sqov")
        cv3 = Carver()
        NXS = 8
        xR = Ring([(cv3.f32(512), Buf(f"xs{i}"), Slot(K, f"xs{i}")) for i in range(NXS)])
        x_alias = [it[1] for it in kR.items] + [it[1] for it in vR.items] + [it[1] for it in pR.items]
        for ts_ in Btmp:
            x_alias += list(ts_)
        x_alias += Btq + Bixc + Bxcb
        gp.do([], Bqz, lambda e: e.memset(qz[64:128, :, 0, :], 0.0))
        gp.do([], Bqz, lambda e: e.memset(qz[0:64, :, 1, :], 0.0))

        wseq = []
        for g in (2, 3, 4, 0, 1):
            wseq.append(("in", g))
        for g in range(4):
            wseq.append(("out", g))
        for l in range(2):
            if l == 1:
                for g in range(2):
                    wseq.append(("q", g))
                for g in range(2):
                    wseq.append(("o", g))
            for g in range(8):
                wseq.append((f"up{l}", g))
            for cg in range(4):
                for kh in range(2):
                    wseq.append((f"down{l}", cg, kh))
            if l == 0:
                for g in range(4):
                    wseq.append(("kv", g))
        NTILES = 1 + 2 * NTL
        wglobal = wseq * NTILES
        wstate = {"issued": 0, "used": 0, "info": []}

        def w_src(item):
            kind = item[0]
            if kind == "in":
                g = item[1]
                return wb_in.rearrange("(k p) n -> p k n", p=128)[:, :, 512 * g:512 * g + 512], (8, 512), Bw["in"]
            if kind == "out":
                g = item[1]
                return wb_out.rearrange("(k p) n -> p k n", p=128)[:, :, 256 * g:256 * g + 256], (10, 256), Bw["out"]
            if kind.startswith("up"):
                l, g = int(kind[2]), item[1]
                return wb_up[l].rearrange("(k p) n -> p k n", p=128)[:, :, 512 * g:512 * g + 512], (8, 512), Bw[kind]
            if kind.startswith("down"):
                l, cg, kh = int(kind[4]), item[1], item[2]
                return (wb_down[l].rearrange("(k p) n -> p k n", p=128)[:, 16 * kh:16 * kh + 16, 256 * cg:256 * cg + 256],
                        (16, 256), Bw[kind])
            if kind == "kv":
                g = item[1]
                return wb_kv.rearrange("(k p) n -> p k n", p=128)[:, :, 512 * g:512 * g + 512], (8, 512), Bw["kv"]
            if kind == "q":
                g = item[1]
                return wb_q.rearrange("(k p) n -> p k n", p=128)[:, :, 512 * g:512 * g + 512], (8, 512), Bw["q"]
            if kind == "o":
                g = item[1]
                return wb_o.rearrange("(k p) n -> p k n", p=128)[:, :, 512 * g:512 * g + 512], (8, 512), Bw["o"]
            raise ValueError(kind)

        def w_issue_upto(n):
            while wstate["issued"] < min(n, len(wglobal)):
                item = wglobal[wstate["issued"]]
                src, (kk, nn), sbuf = w_src(item)
                ap, b, sl = wR.next()
                view = ap[:, 0:kk * nn].rearrange("p (k n) -> p k n", k=kk)
                sy.dma(list(sbuf), [b], sl, view, src)
                wstate["info"].append((item, view, b))
                wstate["issued"] += 1

        def w_next(expect):
            i = wstate["used"]
            w_issue_upto(i + NWS)
            item, view, b = wstate["info"][i]
            assert item == expect, (item, expect)
            wstate["used"] += 1
            return view, b

        def evac_alt(i):
            return ac if (i % 2 == 0) else ve

        def copy_on(q, out, in_):
            if q is ac:
                return lambda e: e.copy(out=out, in_=in_)
            return lambda e: e.tensor_copy(out=out, in_=in_)

        def rmsnorm(gcol, c0, c1, out_f32=False):
            n = c1 - c0
            nb, Bnb = bank[3], Bbank[3]
            for c in range(8):
                sa, sb_ = sqR.next()
                ac.do([BhT[c]], [sb_], lambda e, sa=sa, c=c: e.activation(out=sa[:, 0:n], in_=hT[:, c, c0:c1], func=AF.Square))
                pe.do([sb_, Bones], [Bnb], lambda e, sa=sa, c=c: e.matmul(nb[:, 0:n], lhsT=ones[:, :], rhs=sa[:, 0:n], start=(c == 0), stop=(c == 7)))
            ac.do([Bnb], [Brstd], lambda e: e.activation(out=rstd[:, 0:n], in_=nb[:, 0:n], func=AF.Ln, scale=1.0 / D, bias=EPS))
            ac.do([Brstd], [Brstd], lambda e: e.activation(out=rstd[:, 0:n], in_=rstd[:, 0:n], func=AF.Exp, scale=-0.5))
            for c in range(8):
                if out_f32:
                    ve.do([BhT[c], Brstd, Bcst], [BhT[c]], lambda e, c=c: e.scalar_tensor_tensor(
                        out=hT[:, c, c0:c1], in0=hT[:, c, c0:c1], scalar=cst[:, gcol + c:gcol + c + 1], in1=rstd[:, 0:n],
                        op0=ALU.mult, op1=ALU.mult))
                else:
                    ve.do([BhT[c], Brstd, Bcst], [Bxn[c]], lambda e, c=c: e.scalar_tensor_tensor(
                        out=xn[:, c, c0:c1], in0=hT[:, c, c0:c1], scalar=cst[:, gcol + c:gcol + c + 1], in1=rstd[:, 0:n],
                        op0=ALU.mult, op1=ALU.mult))

        def resid_add(j, pb, Bpb, c0, c1):
            n = c1 - c0
            ve.do([Bpb, BhT[j]], [BhT[j]], lambda e: e.tensor_tensor(out=hT[:, j, c0:c1], in0=hT[:, j, c0:c1], in1=pb[:, 0:n], op=ALU.add))

        def mlp(l, gcol, c0, c1):
            n = c1 - c0
            rmsnorm(gcol, c0, c1)
            for g in range(8):
                wv, wbuf = w_next((f"up{l}", g))
                for jj in range(4):
                    j = 4 * g + jj
                    pb, Bpb = accR.next()
                    for kc in range(8):
                        pe.do([wbuf, Bxn[kc]], [Bpb], lambda e, kc=kc, jj=jj, pb=pb, wv=wv: e.matmul(
                            pb[:, 0:n], lhsT=wv[:, kc, 128 * jj:128 * jj + 128], rhs=xn[:, kc, c0:c1], start=(kc == 0), stop=(kc == 7)))
                    if j % 2 == 0:
                        ve.do([Bpb], [Bpg[j]], lambda e, j=j, pb=pb: e.tensor_scalar_max(out=pages[:, j, 0:n], in0=pb[:, 0:n], scalar1=0.0))
                    else:
                        ac.do([Bpb], [Bpg[j]], lambda e, j=j, pb=pb: e.activation(out=pages[:, j, 0:n], in_=pb[:, 0:n], func=AF.Relu))
                    gp.do([Bpg[j]], [Bpg[j]], lambda e, j=j: e.tensor_tensor(out=pages[:, j, 0:n], in0=pages[:, j, 0:n], in1=pages[:, j, 0:n], op=ALU.mult))
            for cg in range(4):
                pbs = [accR.next(), accR.next()]
                for kh in range(2):
                    wv, wbuf = w_next((f"down{l}", cg, kh))
                    for jj in range(2):
                        pb, Bpb = pbs[jj]
                        for k2 in range(16):
                            kc = 16 * kh + k2
                            pe.do([wbuf, Bpg[kc]], [Bpb], lambda e, k2=k2, kc=kc, jj=jj, pb=pb, wv=wv: e.matmul(
                                pb[:, 0:n], lhsT=wv[:, k2, 128 * jj:128 * jj + 128], rhs=pages[:, kc, 0:n], start=(kc == 0), stop=(kc == 31)))
                for jj in range(2):
                    resid_add(2 * cg + jj, pbs[jj][0], pbs[jj][1], c0, c1)

        def layer_a(ncol, segs):
            rmsnorm(0, 0, ncol)
            n = ncol
            gel_pg = 0
            worder = (2, 3, 4, 0, 1)
            xr_view = [(xpall[:, c, 3:3 + NT], [Bxp[c]]) for c in range(NRC)]
            for g in worder:
                wv, wbuf = w_next(("in", g))
                for jj in range(4):
                    j = 4 * g + jj
                    pb, Bpb = accR.next()
                    for kc in range(8):
                        pe.do([wbuf, Bxn[kc]], [Bpb], lambda e, kc=kc, jj=jj, pb=pb, wv=wv: e.matmul(
                            pb[:, 0:n], lhsT=wv[:, kc, 128 * jj:128 * jj + 128], rhs=xn[:, kc, 0:n], start=(kc == 0), stop=(kc == 7)))
                    if j < NRC:
                        ac.do([Bpb], [Bpg[gel_pg + j]], lambda e, j=j, pb=pb: e.activation(
                            out=pages[:, gel_pg + j, 0:n], in_=pb[:, 0:n], func=AF.Gelu_apprx_tanh))
                    else:
                        c = j - NRC
                        xv, xb_ = xr_view[c]
                        ve.do([Bpb], xb_, lambda e, xv=xv, pb=pb: e.tensor_copy(out=xv[:, 0:n], in_=pb[:, 0:n]))
            return xr_view

        def layer_a_chunks(ncol, segs, xr_view):
            n = ncol
            gel_pg = 0

            def P1a(c):
                tb = Btmp[c % NSET]
                xpc, xc = tmpv[c % NSET][0], tmpv[c % NSET][1]
                xv, xb_ = xr_view[c]
                for si, (s0, sn, stt) in enumerate(segs):
                    ve.do([Bct[stt]], xb_, lambda e, s0=s0, stt=stt: e.tensor_copy(out=xpall[:, c, s0:s0 + 3], in_=ctail[:, stt, c, :]))
                    ve.do(xb_ + [Bcst], [tb[1]], lambda e, s0=s0, sn=sn: e.tensor_scalar(
                        out=xc[:, s0:s0 + sn], in0=xpall[:, c, s0:s0 + sn], scalar1=cst[:, 48 + c:49 + c], scalar2=cst[:, 88 + c:89 + c],
                        op0=ALU.mult, op1=ALU.add))
                    for j in range(1, 4):
                        ve.do(xb_ + [Bcst, tb[1]], [tb[1]], lambda e, s0=s0, sn=sn, j=j: e.scalar_tensor_tensor(
                            out=xc[:, s0:s0 + sn], in0=xpall[:, c, s0 + j:s0 + j + sn], scalar=cst[:, 48 + 10 * j + c:49 + 10 * j + c],
                            in1=xc[:, s0:s0 + sn], op0=ALU.mult, op1=ALU.add))
                    ve.do(xb_, [Bct[stt]], lambda e, s0=s0, stt=stt, sn=sn: e.tensor_copy(out=ctail[:, stt, c, :], in_=xpall[:, c, s0 + sn:s0 + sn + 3]))
                xcb_ = xcbv[c % NSET]
                ac.do([tb[1]], [Bxcb[c % NSET]], lambda e: e.copy(out=xcb_[:, 0:n], in_=xc[:, 0:n]))
                pr, Bpr = accR.next()
                pi, Bpi = accR.next()
                pe.do([Bwg, Bxcb[c % NSET]], [Bpr], lambda e: e.matmul(pr[:, 0:n], lhsT=wg[:, 0, c, :], rhs=xcb_[:, 0:n], start=True, stop=True))
                pe.do([Bwg, Bxcb[c % NSET]], [Bpi], lambda e: e.matmul(pi[:, 0:n], lhsT=wg[:, 1, c, :], rhs=xcb_[:, 0:n], start=True, stop=True))
                return (pr, Bpr, pi, Bpi)

            def P1t(c, k, banks):
                pr, Bpr, pi, Bpi = banks
                tb = Btmp[c % NSET]
                ti = tmpv[c % NSET][2]
                xv, xb_ = xr_view[c]
                ac.do([Bpr, Bdc], xb_, lambda e: e.activation(out=xv[:, 0:n], in_=pr[:, 0:n], func=AF.Tanh, scale=0.5, bias=dc[:, 20 + c:21 + c]))
                ac.do([Bpi, Bdc], [tb[2]], lambda e: e.activation(out=ti[:, 0:n], in_=pi[:, 0:n], func=AF.Tanh, scale=0.5, bias=dc[:, 30 + c:31 + c]))
                ac.do(xb_ + [Bdc], [Btq[k]], lambda e: e.activation(out=tqv[k][:, 0:n], in_=xv[:, 0:n], func=AF.Tanh, scale=dc[:, 10 + c:11 + c], bias=dc[:, 10 + c:11 + c]))

            def P1x(c, k):
                tb = Btmp[c % NSET]
                xc, ti = tmpv[c % NSET][1], tmpv[c % NSET][2]
                ve.do([tb[2], tb[1]], [Bixc[k]], lambda e: e.scalar_tensor_tensor(
                    out=ixv[k][:, 0:n], in0=ti[:, 0:n], scalar=1.0, in1=xc[:, 0:n], op0=ALU.add, op1=ALU.mult))

            def P2a(c, k):
                tb = Btmp[c % NSET]
                a_, w_ = tmpv[c % NSET][0], tmpv[c % NSET][1]
                xv, xb_ = xr_view[c]
                ac.do(xb_ + [Bdc], [tb[0]], lambda e: e.activation(out=a_[:, 0:n], in_=xv[:, 0:n], func=AF.Exp, scale=dc[:, c:c + 1], bias=dc[:, c:c + 1]))
                ac.do(xb_ + [Bdc], [tb[1]], lambda e: e.activation(out=w_[:, 0:n], in_=xv[:, 0:n], func=AF.Exp, scale=dc[:, 40 + c:41 + c], bias=dc[:, 40 + c:41 + c]))

            def P2w(c, k):
                tb = Btmp[c % NSET]
                w_ = tmpv[c % NSET][1]
                ve.do([tb[1], Btq[k]], [tb[1]], lambda e: e.scalar_tensor_tensor(
                    out=w_[:, 0:n], in0=w_[:, 0:n], scalar=1.0, in1=tqv[k][:, 0:n], op0=ALU.add, op1=ALU.mult))

            def P2b(c, k):
                tb = Btmp[c % NSET]
                w_ = tmpv[c % NSET][1]
                ac.do([tb[1]], [tb[1]], lambda e: e.activation(out=w_[:, 0:n], in_=w_[:, 0:n], func=AF.Ln))
                ac.do([tb[1]], [tb[1]], lambda e: e.activation(out=w_[:, 0:n], in_=w_[:, 0:n], func=AF.Exp, scale=0.5, bias=math.log(0.5)))

            def P2u(c, k):
                tb = Btmp[c % NSET]
                w_ = tmpv[c % NSET][1]
                gp.do([tb[1], Bixc[k]], [tb[1]], lambda e: e.tensor_tensor(out=w_[:, 0:n], in0=w_[:, 0:n], in1=ixv[k][:, 0:n], op=ALU.mult))

            def P2s(c, k):
                tb = Btmp[c % NSET]
                a_, w_, hs = tmpv[c % NSET][0], tmpv[c % NSET][1], tmpv[c % NSET][2]
                for (s0, sn, stt) in segs:
                    ve.do([tb[0], tb[1], Bhst[stt]], [tb[2]], lambda e, s0=s0, sn=sn, stt=stt: e.tensor_tensor_scan(
                        out=hs[:, s0:s0 + sn], data0=a_[:, s0:s0 + sn], data1=w_[:, s0:s0 + sn], initial=hstate[:, stt, c:c + 1],
                        op0=ALU.mult, op1=ALU.add))
                    ve.do([tb[2]], [Bhst[stt]], lambda e, s0=s0, sn=sn, stt=stt: e.tensor_copy(out=hstate[:, stt, c:c + 1], in_=hs[:, s0 + sn - 1:s0 + sn]))
                gp.do([tb[2], Bpg[gel_pg + c]], [Bpg[gel_pg + c]], lambda e: e.tensor_tensor(
                    out=pages[:, gel_pg + c, 0:n], in0=pages[:, gel_pg + c, 0:n], in1=hs[:, 0:n], op=ALU.mult))

            for half in range(2):
                cs = list(range(5 * half, 5 * half + 5))
                banks = {}
                for t in range(5 + 2):
                    if t < 5:
                        banks[cs[t]] = P1a(cs[t])
                    if 0 <= t - 1 < 5:
                        P1t(cs[t - 1], t - 1, banks[cs[t - 1]])
                    if 0 <= t - 2 < 5:
                        P1x(cs[t - 2], t - 2)
                for t in range(5 + 3):
                    if t < 5:
                        P2a(cs[t], t)
                    if 0 <= t - 1 < 5:
                        P2b(cs[t - 1], t - 1)
                    if 0 <= t - 2 < 5:
                        P2u(cs[t - 2], t - 2)
                    if 0 <= t - 3 < 5:
                        P2s(cs[t - 3], t - 3)
                    if t < 5:
                        P2w(cs[t], t)
            for g in range(4):
                wv, wbuf = w_next(("out", g))
                for jj in range(2):
                    j = 2 * g + jj
                    pb, Bpb = accR.next()
                    for kc in range(NRC):
                        pe.do([wbuf, Bpg[gel_pg + kc]], [Bpb], lambda e, kc=kc, jj=jj, pb=pb, wv=wv: e.matmul(
                            pb[:, 0:n], lhsT=wv[:, kc, 128 * jj:128 * jj + 128], rhs=pages[:, gel_pg + kc, 0:n], start=(kc == 0), stop=(kc == NRC - 1)))
                    resid_add(j, pb, Bpb, 0, n)

        def kv_stage(ncol, subs, kouts, vouts, kt_dsts, va_dsts):
            n = ncol
            rmsnorm(16, 0, n)
            for g in range(4):
                wv, wbuf = w_next(("kv", g))
                for si, (s0, sn) in enumerate(subs):
                    pb, Bpb = accR.next()
                    for kc in range(8):
                        pe.do([wbuf, Bxn[kc]], [Bpb], lambda e, kc=kc, pb=pb, wv=wv, s0=s0, sn=sn: e.matmul(
                            pb[0:sn, 0:512], lhsT=xn[:, kc, s0:s0 + sn], rhs=wv[:, kc, :], start=(kc == 0), stop=(kc == 7)))
                    oa, ob, osl = oR.next()
                    ac.do([Bpb], [ob], lambda e, oa=oa, pb=pb, sn=sn: e.copy(out=oa[0:sn, :], in_=pb[0:sn, 0:512]))
                    outs = kouts[si] if g < 2 else vouts[si]
                    for (dst_fn, r0, r1) in outs:
                        ac.dma([ob], [], osl, dst_fn(g % 2), oa[r0:r1, :])
                    if g >= 2:
                        ve.do([ob], [Bvast], lambda e, oa=oa, sn=sn, si=si, g=g: e.tensor_copy(
                            out=vast[0:sn, si, 4 * (g - 2):4 * (g - 2) + 4, 0:128], in_=oa[0:sn, :].rearrange("p (h c) -> p h c", h=4)))
                if g < 2:
                    for jj in range(4):
                        h = 4 * g + jj
                        pb, Bpb = accR.next()
                        for kc in range(8):
                            pe.do([wbuf, Bxn[kc]], [Bpb], lambda e, kc=kc, jj=jj, pb=pb, wv=wv: e.matmul(
                                pb[:, 0:n], lhsT=wv[:, kc, 128 * jj:128 * jj + 128], rhs=xn[:, kc, 0:n], start=(kc == 0), stop=(kc == 7)))
                        q_ = evac_alt(jj)
                        q_.do([Bpb], [Bpg[h]], copy_on(q_, pages[:, h, 0:n], pb[:, 0:n]))
            for (seq, bi, sc0, ncs, dc0) in kt_dsts:
                gp.dma(Bpg[0:8], [BKT[seq][bi]], Skst, KTs[seq].rearrange("h p n -> p h n")[:, :, dc0:dc0 + ncs], pages[:, 0:8, sc0:sc0 + ncs])
            for (seq, bi, si, r0, r1, kt, p0) in va_dsts:
                gp.dma([Bvast], [BVA[seq][bi]], Svast, VAs[seq].rearrange("h p t c -> p t h c")[p0:p0 + (r1 - r0), kt, :, :], vast[r0:r1, si, :, :])

        def attention(seq, c0, subs, ktiles, q_base_sub):
            nsub = len(subs)
            ncols_all = subs[-1][0] + subs[-1][1]
            nblk = (len(ktiles) + 7) // 8
            LAG = 2
            blocks = []
            steps = []
            for h in range(NH):
                for b in range(nblk):
                    kts = ktiles[8 * b:8 * b + 8]
                    blocks.append((h, b, kts))
                    for ti_, t_ in enumerate(kts):
                        ks = t_[3]
                        s_first = max(0, ks - q_base_sub) if ks >= 0 else 0
                        if s_first >= nsub:
                            continue
                        for m in range(2):
                            steps.append((h, len(blocks) - 1, ti_, m, s_first))
            blk_loaded = {}

            def load_block(bi):
                if bi >= len(blocks) or bi in blk_loaded:
                    return
                h, b, kts = blocks[bi]
                kc_lo = kts[0][2]
                kc_hi = kts[-1][2] + kts[-1][1]
                ka, kb_, ksl = kR.next()
                rbufs = []
                for t_ in kts:
                    for bb in t_[4]:
                        if bb not in rbufs:
                            rbufs.append(bb)
                sy.dma([x for x in rbufs if x.name.startswith("KT")], [kb_], ksl, ka[:, 0:kc_hi - kc_lo], KTs[seq, h, :, kc_lo:kc_hi])
                va, vb_, vsl = vR.next()
                t_lo = kts[0][0]
                gi = 0
                while gi < len(kts):
                    gj = gi
                    while gj + 1 < len(kts) and kts[gj + 1][1] == kts[gi][1]:
                        gj += 1
                    nkg = kts[gi][1]
                    sy.dma([x for x in rbufs if x.name.startswith("VA")], [vb_], vsl, va[0:nkg, gi:gj + 1, :],
                           VAs[seq, h, 0:nkg, t_lo + gi:t_lo + gj + 1, :])
                    gi = gj + 1
                blk_loaded[bi] = (ka, kb_, va, vb_, kc_lo)

            started = set()
            pend = {}
            deferred = []
            cur_iter = [0]
            last_hm = {}
            for j_, st_ in enumerate(steps):
                last_hm[(st_[0], st_[3])] = j_
            front_info = {}
            head_first_step = {}
            head_last_step = {}
            for j, st_ in enumerate(steps):
                head_first_step.setdefault(st_[0], j)
                head_last_step[st_[0]] = j

            def front(j):
                h, bi, ti_, m, s_first = steps[j]
                load_block(bi)
                load_block(bi + 1)
                ka, kb_, va, vb_, kc_lo = blk_loaded[bi]
                kt, nk, kcol, ks, _ = blocks[bi][2][ti_]
                q_lo = subs[s_first][0]
                ncv = ncols_all - q_lo
                hp = h % 2
                sb_, Bsb = stR.next()
                pe.do([kb_, Bqz[h]], [Bsb], lambda e: e.matmul(
                    sb_[0:nk, 0:ncv], lhsT=ka[:, kcol - kc_lo:kcol - kc_lo + nk],
                    rhs=qz[:, h, m, c0 + q_lo:c0 + q_lo + ncv], start=True, stop=True))
                pa, Bpa = pR.next()
                ac.do([Bsb], [Bpa], lambda e: e.activation(out=pa[0:nk, 0:ncv], in_=sb_[0:nk, 0:ncv], func=AF.Exp, scale=0.125))
                for s in range(s_first, nsub):
                    qs = q_base_sub + s
                    off = subs[s][0] - q_lo
                    ns = subs[s][1]
                    eb = None
                    if ks < 0:
                        if qs == 0:
                            eb = EBM[0:nk, h, 0:ns]
                    elif ks == qs:
                        eb = EBD[0:nk, h, 0:ns]
                    elif ks == qs - 1:
                        eb = EBD[0:nk, h, 128:128 + ns]
                    if eb is not None:
                        ve.do([Bpa, BEB], [Bpa], lambda e, off=off, ns=ns, eb=eb: e.tensor_tensor(
                            out=pa[0:nk, off:off + ns], in0=pa[0:nk, off:off + ns], in1=eb, op=ALU.mult))
                front_info[j] = (pa, Bpa, va, vb_, nk, ti_, q_lo, ncv)

            def back(j):
                h, bi, ti_, m, s_first = steps[j]
                pa, Bpa, va, vb_, nk, ti_, q_lo, ncv = front_info.pop(j)
                bk = OB0 + 2 * (h % 2) + m
                first = (h, m) not in started
                started.add((h, m))
                pe.do([Bpa, vb_], [Bbank[bk]], lambda e: e.matmul(
                    bank[bk][:, q_lo:q_lo + ncv], lhsT=va[0:nk, ti_, 0:128], rhs=pa[0:nk, 0:ncv],
                    start=first, stop=False, skip_group_check=True))
                lbk = LB0 + (h % 2)

                def lmm(rhs, rb, nk_, q_lo_, ncv_):
                    firstl = ("l", h) not in started
                    started.add(("l", h))
                    pe.do([rb, Besel], [Bbank[lbk]], lambda e: e.matmul(
                        bank[lbk][0:2, q_lo_:q_lo_ + ncv_], lhsT=esel[0:nk_, 2 * m:2 * m + 2], rhs=rhs[0:nk_, 0:ncv_],
                        start=firstl, stop=False, skip_group_check=True))

                key = (h, m)
                full = (nk == 128 and q_lo == 0 and ncv == ncols_all)
                islast = (j == last_hm[key])
                GRP = 4
                st_ = pend.get(key)

                def close_sum(s2, Bs2, ncv_, delay):
                    if delay:
                        deferred.append((cur_iter[0] + 2, lambda: lmm(s2, Bs2, 128, 0, ncv_)))
                    else:
                        lmm(s2, Bs2, 128, 0, ncv_)

                if st_ is None:
                    if full and not islast:
                        pend[key] = ("single", pa, Bpa, nk, q_lo, ncv)
                    else:
                        lmm(pa, Bpa, nk, q_lo, ncv)
                elif st_[0] == "single":
                    _, ppa, Bppa, pnk, pq, pncv = st_
                    del pend[key]
                    if full:
                        s2, Bs2 = p2R.next()
                        ve.do([Bppa, Bpa], [Bs2], lambda e: e.tensor_tensor(out=s2[:, 0:ncv], in0=ppa[:, 0:ncv], in1=pa[:, 0:ncv], op=ALU.add))
                        if islast or GRP <= 2:
                            close_sum(s2, Bs2, ncv, True)
                        else:
                            pend[key] = ("sum", s2, Bs2, 2, ncv)
                    else:
                        lmm(ppa, Bppa, pnk, pq, pncv)
                        lmm(pa, Bpa, nk, q_lo, ncv)
                else:
                    _, s2, Bs2, cnt_, sncv = st_
                    del pend[key]
                    if full:
                        ve.do([Bs2, Bpa], [Bs2], lambda e: e.tensor_tensor(out=s2[:, 0:ncv], in0=s2[:, 0:ncv], in1=pa[:, 0:ncv], op=ALU.add))
                        cnt_ += 1
                        if islast or cnt_ >= GRP:
                            close_sum(s2, Bs2, ncv, True)
                        else:
                            pend[key] = ("sum", s2, Bs2, cnt_, sncv)
                    else:
                        close_sum(s2, Bs2, sncv, False)
                        lmm(pa, Bpa, nk, q_lo, ncv)

            def finalize(h, stg):
                hp = h % 2
                nq = ncols_all
                lbk = LB0 + hp
                b0, b1 = OB0 + 2 * hp, OB0 + 2 * hp + 1
                if stg == 0:
                    ac.do([Bbank[lbk]], [Brl2[hp]], lambda e: e.activation(out=rl2[0:2, hp, 0:nq], in_=bank[lbk][0:2, 0:nq], func=AF.Ln))
                    ac.do([Brl2[hp]], [Brl2[hp]], lambda e: e.activation(out=rl2[0:2, hp, 0:nq], in_=rl2[0:2, hp, 0:nq], func=AF.Exp, scale=-1.0))
                    ac.do([Brl2[hp]], [Brl2[hp]], lambda e: e.copy(out=rlh[0:2, hp, 0:nq], in_=rl2[0:2, hp, 0:nq]))
                    ve.do([Brl2[hp]], [Brl2[hp]], lambda e: e.tensor_tensor(out=rll[0:2, hp, 0:nq], in0=rl2[0:2, hp, 0:nq], in1=rlh[0:2, hp, 0:nq], op=ALU.subtract))
                elif stg == 1:
                    for m in range(2):
                        lb, Blb = stR.next()
                        pe.do([Brl2[hp], Bbsel], [Blb], lambda e, m=m, lb=lb: e.matmul(lb[:, 0:nq], lhsT=bselb[0:2, 128 * m:128 * m + 128], rhs=rlh[0:2, hp, 0:nq], start=True, stop=False))
                        pe.do([Brl2[hp], Bbsel], [Blb], lambda e, m=m, lb=lb: e.matmul(lb[:, 0:nq], lhsT=bselb[0:2, 128 * m:128 * m + 128], rhs=rll[0:2, hp, 0:nq], start=False, stop=True))
                        ac.do([Blb], [Brlv[m]], lambda e, m=m, lb=lb: e.copy(out=rlv[m][:, 0:nq], in_=lb[:, 0:nq]))
                elif stg == 2:
                    ve.do([Bbank[b0], Brlv[0]], [Bofv], lambda e: e.tensor_tensor(out=ofv[:, 0:nq], in0=bank[b0][:, 0:nq], in1=rlv[0][:, 0:nq], op=ALU.mult))
                    ve.do([Bbank[b1], Brlv[1]], [Botv], lambda e: e.tensor_tensor(out=otv[:, 0:nq], in0=bank[b1][:, 0:nq], in1=rlv[1][:, 0:nq], op=ALU.mult))
                    ve.do([Bofv, Botv, Bnlam], [Bofv], lambda e: e.scalar_tensor_tensor(
                        out=ofv[:, 0:nq], in0=otv[:, 0:nq], scalar=nlam[:, 0:1], in1=ofv[:, 0:nq], op0=ALU.mult, op1=ALU.add))
                    ac.do([Bofv], [Bsqov], lambda e: e.activation(out=sqov[:, 0:nq], in_=ofv[:, 0:nq], func=AF.Square))
                elif stg == 3:
                    sbk, Bsbk = stR.next()
                    pe.do([Bsqov, Bones], [Bsbk], lambda e: e.matmul(sbk[:, 0:nq], lhsT=ones[:, :], rhs=sqov[:, 0:nq], start=True, stop=True))
                    ac.do([Bsbk], [Brsov], lambda e: e.activation(out=rsov[:, 0:nq], in_=sbk[:, 0:nq], func=AF.Ln, scale=1.0 / 128, bias=EPS))
                    ac.do([Brsov], [Brsov], lambda e: e.activation(out=rsov[:, 0:nq], in_=rsov[:, 0:nq], func=AF.Exp, scale=-0.5))
                else:
                    ve.do([Bofv, Brsov, BGp], [Bxn[h]], lambda e: e.scalar_tensor_tensor(
                        out=xn[:, h, c0:c0 + nq], in0=ofv[:, 0:nq], scalar=Gp[:, 0:1], in1=rsov[:, 0:nq], op0=ALU.mult, op1=ALU.mult))

            pending = []
            nst = len(steps)
            for j in range(nst + LAG):
                cur_iter[0] = j
                if j < nst:
                    front(j)
                jb = j - LAG
                if jb >= 0:
                    back(jb)
                    hb = steps[jb][0]
                    if jb == head_last_step[hb]:
                        while deferred:
                            deferred.pop(0)[1]()
                        for stg in range(5):
                            pending.append((j + 1 + 2 * stg, hb, stg))
                        pending.sort(key=lambda x: (x[1], x[2]))
                while deferred and deferred[0][0] <= j:
                    deferred.pop(0)[1]()
                while pending and pending[0][0] <= j:
                    _, h_, stg_ = pending.pop(0)
                    finalize(h_, stg_)
            while deferred:
                deferred.pop(0)[1]()
            pending.sort(key=lambda x: (x[1], x[2]))
            while pending:
                _, h_, stg_ = pending.pop(0)
                finalize(h_, stg_)

        def layer_b(seq, c0, c1, subs, ktiles, q_base_sub):
            n = c1 - c0
            rmsnorm(24, c0, c1)
            for g in range(2):
                wv, wbuf = w_next(("q", g))
                for jj in range(4):
                    h = 4 * g + jj
                    pb, Bpb = accR.next()
                    for kc in range(8):
                        pe.do([wbuf, Bxn[kc]], [Bpb], lambda e, kc=kc, jj=jj, pb=pb, wv=wv: e.matmul(
                            pb[:, 0:n], lhsT=wv[:, kc, 128 * jj:128 * jj + 128], rhs=xn[:, kc, c0:c1], start=(kc == 0), stop=(kc == 7)))
                    ac.do([Bpb], [Bqz[h]], lambda e, h=h, pb=pb: e.copy(out=qz[0:64, h, 0, c0:c1], in_=pb[0:64, 0:n]))
                    ve.do([Bpb], [Bqz[h]], lambda e, h=h, pb=pb: e.tensor_copy(out=qz[64:128, h, 1, c0:c1], in_=pb[64:128, 0:n]))
            attention(seq, c0, subs, ktiles, q_base_sub)
            for g in range(2):
                wv, wbuf = w_next(("o", g))
                for jj in range(4):
                    j = 4 * g + jj
                    pb, Bpb = accR.next()
                    for kc in range(8):
                        pe.do([wbuf, Bxn[kc]], [Bpb], lambda e, kc=kc, jj=jj, pb=pb, wv=wv: e.matmul(
                            pb[:, 0:n], lhsT=wv[:, kc, 128 * jj:128 * jj + 128], rhs=xn[:, kc, c0:c1], start=(kc == 0), stop=(kc == 7)))
                    resid_add(j, pb, Bpb, c0, c1)

        def final_out(c0, subs, dst_fn):
            c1 = c0 + subs[-1][0] + subs[-1][1]
            rmsnorm(40, c0, c1, out_f32=True)
            for si, (off, ns) in enumerate(subs):
                for half in range(2):
                    pb, Bpb = accR.next()
                    for j in range(4):
                        c = 4 * half + j
                        pe.do([BhT[c], Bident], [Bpb], lambda e, pb=pb, j=j, c=c, off=off, ns=ns: e.transpose(
                            out=pb[0:ns, 128 * j:128 * j + 128], in_=hT[:, c, c0 + off:c0 + off + ns], identity=ident[:, :]))
                    oa, ob, osl = oR.next()
                    ac.do([Bpb], [ob], lambda e, oa=oa, pb=pb, ns=ns: e.copy(out=oa[0:ns, :], in_=pb[0:ns, 0:512]))
                    ac.dma([ob], [], osl, dst_fn(si, half), oa[0:ns, :])

        def x_fetch(parts_list):
            got = []
            for parts in parts_list:
                for half in range(2):
                    xa, xb_, xsl = xR.next()
                    for (src, r0, nr) in parts:
                        sy.dma([], [xb_] + x_alias, xsl, xa[r0:r0 + nr, :], src[:, 512 * half:512 * half + 512])
                    got.append((xa, xb_, half, parts[-1][1] + parts[-1][2]))
            return got

        def x_consume(got):
            for (xa, xb_, half, nr_tot) in got:
                pb, Bpb = accR.next()
                for j in range(4):
                    pe.do([xb_, Bident], [Bpb], lambda e, pb=pb, xa=xa, j=j, nr_tot=nr_tot: e.transpose(
                        out=pb[:, 128 * j:128 * j + nr_tot], in_=xa[0:nr_tot, 128 * j:128 * j + 128], identity=ident[0:nr_tot, 0:nr_tot]))
                yield half, pb, Bpb, nr_tot

        NS_ = NMETA + DEC
        for half, pb, Bpb, nr in x_consume(x_fetch([[(meta, 0, NMETA), (xs, NMETA, DEC)]])):
            q_ = evac_alt(half)
            q_.do([Bpb], [BhT[4 * half + j] for j in range(4)], copy_on(
                q_, hT[:, 4 * half:4 * half + 4, 0:nr], pb[:, :].rearrange("p (j t) -> p j t", j=4)[:, :, 0:nr]))
        segs = [(0, NMETA, ST_M), (NMETA, DEC, ST_S)]
        xrv = layer_a(NS_, segs)
        layer_a_chunks(NS_, segs, xrv)
        mlp(0, 8, 0, NS_)
        kv_stage(
            NS_, [(0, NS_)],
            kouts=[[(lambda g2: mk_p[0, :, 512 * g2:512 * g2 + 512], 0, NMETA),
                    (lambda g2: mk_p[1, :, 512 * g2:512 * g2 + 512], 0, NMETA),
                    (lambda g2: k_s[:, 512 * g2:512 * g2 + 512], NMETA, NS_)]],
            vouts=[[(lambda g2: mv_p[0, :, 512 * g2:512 * g2 + 512], 0, NMETA),
                    (lambda g2: mv_p[1, :, 512 * g2:512 * g2 + 512], 0, NMETA),
                    (lambda g2: v_s[:, 512 * g2:512 * g2 + 512], NMETA, NS_)]],
            kt_dsts=[(0, 0, 0, NMETA, 0), (1, 0, 0, NMETA, 0), (2, 1, NMETA, DEC, NMETA + PAST)],
            va_dsts=[(0, 0, 0, 0, NMETA, 0, 0), (1, 0, 0, 0, NMETA, 0, 0), (2, 1, 0, NMETA, NS_, SKT - 1, 0)],
        )
        gp.dma([Bhst[ST_S]], [], Sgen, sh_s[0].rearrange("(c p) -> p c", p=128), hstate[:, ST_S, :], allow_slow_non_contiguous=True)
        for j3 in range(3):
            gp.dma([Bct[ST_S]], [], Sgen, sc_s[0, j3].rearrange("(c p) -> p c", p=128), ctail[:, ST_S, :, j3], allow_slow_non_contiguous=True)
        for stt in (ST_A, ST_B):
            ve.do([Bhst[ST_M]], [Bhst[stt]], lambda e, stt=stt: e.tensor_copy(out=hstate[:, stt, :], in_=hstate[:, ST_M, :]))
            ve.do([Bct[ST_M]], [Bct[stt]], lambda e, stt=stt: e.tensor_copy(out=ctail[:, stt, :, :], in_=ctail[:, ST_M, :, :]))
        s_kt = [(0, NMETA, 0, -1, [BKT[2][0], BVA[2][0]])]
        for t in range(PAST // 128):
            s_kt.append((1 + t, 128, NMETA + 128 * t, t, [BKT[2][0], BVA[2][0]]))
        s_kt.append((SKT - 1, DEC, NMETA + PAST, PAST // 128, [BKT[2][1], BVA[2][1]]))
        layer_b(2, NMETA, NS_, [(0, DEC)], s_kt, PAST // 128)
        mlp(1, 32, NMETA, NS_)
        final_out(NMETA, [(0, DEC)], lambda si, half: y_s[:, 512 * half:512 * half + 512])

        ptiles = [(seq_, i_) for seq_ in range(2) for i_ in range(NTL)]

        def fetch_tile(ti):
            seq_, i_ = ptiles[ti]
            return x_fetch([[(xp[seq_, NT * i_ + 128 * s_:NT * i_ + 128 * s_ + 128, :], 0, 128)] for s_ in range(4)])

        xgot = {0: fetch_tile(0)}
        for tix, (seq, i) in enumerate(ptiles):
            stt = ST_A + seq
            if True:
                f0 = NT * i
                got = xgot.pop(tix)
                for gi_, (half, pb, Bpb, nr) in enumerate(x_consume(got)):
                    s = gi_ // 2
                    q_ = evac_alt(half)
                    q_.do([Bpb], [BhT[4 * half + j] for j in range(4)], copy_on(
                        q_, hT[:, 4 * half:4 * half + 4, 128 * s:128 * s + 128], pb[:, :].rearrange("p (j t) -> p j t", j=4)))
                segs = [(0, NT, stt)]
                xrv = layer_a(NT, segs)
                layer_a_chunks(NT, segs, xrv)
                mlp(0, 8, 0, NT)
                subs4 = [(128 * s, 128) for s in range(4)]
                kv_stage(
                    NT, subs4,
                    kouts=[[(lambda g2, s=s: k_p[seq, f0 + 128 * s:f0 + 128 * s + 128, 512 * g2:512 * g2 + 512], 0, 128)] for s in range(4)],
                    vouts=[[(lambda g2, s=s: v_p[seq, f0 + 128 * s:f0 + 128 * s + 128, 512 * g2:512 * g2 + 512], 0, 128)] for s in range(4)],
                    kt_dsts=[(seq, 1 + i, 0, NT, NMETA + f0)],
                    va_dsts=[(seq, 1 + i, s, 0, 128, 1 + 4 * i + s, 0) for s in range(4)],
                )
                p_kt = [(0, NMETA, 0, -1, [BKT[seq][0], BVA[seq][0]])]
                for t in range(4 * (i + 1)):
                    p_kt.append((1 + t, 128, NMETA + 128 * t, t, [BKT[seq][1 + t // 4], BVA[seq][1 + t // 4]]))
                layer_b(seq, 0, NT, subs4, p_kt, 4 * i)
                if tix + 1 < len(ptiles):
                    xgot[tix + 1] = fetch_tile(tix + 1)
                mlp(1, 32, 0, NT)
                final_out(0, subs4, lambda si, half, f0=f0, seq=seq: y_p[seq, f0 + 128 * si:f0 + 128 * si + 128, 512 * half:512 * half + 512])
            if i == NTL - 1:
                gp.dma([Bhst[stt]], [], Sgen, sh_p[seq].rearrange("(c p) -> p c", p=128), hstate[:, stt, :], allow_slow_non_contiguous=True)
                for j3 in range(3):
                    gp.dma([Bct[stt]], [], Sgen, sc_p[seq, j3].rearrange("(c p) -> p c", p=128), ctail[:, stt, :, j3], allow_slow_non_contiguous=True)

        for (_, _, sl) in oR.items:
            ac.wait_ev((sl.sem, sl.cnt))
        gp.wait_ev((Sgen.sem, Sgen.cnt))
        gp.wait_ev((Skst.sem, Skst.cnt))
        gp.wait_ev((Svast.sem, Svast.cnt))
        K.emit({"sync": sy, "gpsimd": gp, "tensor": pe, "vector": ve, "scalar": ac})
    return nc


def _vec_pm(v, n):
    return np.ascontiguousarray(np.asarray(v, np.float32).reshape(n, 128).T)


def make_in_maps(inp, SEQ, ncores=8):
    ohz, mask0, ident = _static_consts()
    g = lambda k: np.asarray(inp[k], np.float32)
    cst = np.zeros((128, 128), np.float32)
    cst[:, 0:8] = _vec_pm(g("norm_mix_g")[0], 8)
    cst[:, 8:16] = _vec_pm(g("norm_mlp_g")[0], 8)
    cst[:, 16:24] = _vec_pm(g("norm_kv_g"), 8)
    cst[:, 24:32] = _vec_pm(g("norm_mix_g")[1], 8)
    cst[:, 32:40] = _vec_pm(g("norm_mlp_g")[1], 8)
    cst[:, 40:48] = _vec_pm(g("norm_f_g"), 8)
    for j in range(4):
        cst[:, 48 + 10 * j:58 + 10 * j] = _vec_pm(g("conv_w")[0, j], 10)
    cst[:, 88:98] = _vec_pm(g("conv_b")[0], 10)
    cst[:, 98:108] = _vec_pm(g("b_gate_r")[0], 10)
    cst[:, 108:118] = _vec_pm(g("b_gate_i")[0], 10)
    cst[:, 118:128] = _vec_pm(g("lru_lambda")[0], 10)
    lamv = np.concatenate([g("lambda_q1")[0], g("lambda_k1")[0], g("lambda_q2")[0], g("lambda_k2")[0]])[None, :]
    shared = {
        "meta": g("meta_tokens"), "cst": cst,
        "w_in": g("w_in_a")[0], "w_gr": g("w_gate_r")[0], "w_gi": g("w_gate_i")[0], "w_out": g("w_out_a")[0],
        "w_up": g("w_mlp_up"), "w_down": g("w_mlp_down"), "w_kv": g("w_kv"), "w_q": g("w_q")[0], "w_o": g("w_o")[0],
        "lamv": np.ascontiguousarray(lamv), "subg": g("subln_g")[0][None, :].copy(), "relb": g("rel_bias"),
        "ohz": ohz, "mask0": mask0, "ident": ident,
        "bsel": np.concatenate([np.eye(2, dtype=np.float32)[:, 0:1].repeat(128, 1), np.eye(2, dtype=np.float32)[:, 1:2].repeat(128, 1)], axis=1),
    }
    maps = []
    for k in range(ncores):
        sst = np.zeros((128, 40), np.float32)
        sst[:, 0:10] = _vec_pm(g("state_h")[0, k], 10)
        sc = g("state_conv")[0, k]
        sst[:, 10:40] = np.stack([_vec_pm(sc[j], 10) for j in range(3)], axis=2).reshape(128, 30)
        m = dict(shared)
        m.update({
            "xp": np.ascontiguousarray(g("x_prompt")[2 * k:2 * k + 2, :SEQ]),
            "xs": np.ascontiguousarray(g("x_sample")[k]),
            "sst": sst,
            "cmk": np.ascontiguousarray(g("cache_meta_k")[k].reshape(NMETA, D)),
            "cmv": np.ascontiguousarray(g("cache_meta_v")[k].reshape(NMETA, D)),
            "ck": np.ascontiguousarray(g("cache_k")[k].reshape(PAST, D)),
            "cv": np.ascontiguousarray(g("cache_v")[k].reshape(PAST, D)),
        })
        maps.append(m)
    return maps


def gather(results, SEQ, ncores=8):
    cat = lambda k: np.concatenate([np.asarray(r[k]) for r in results], axis=0)
    y_p = cat("y_p")
    y_s = np.stack([np.asarray(r["y_s"]) for r in results], 0)
    sh_p = cat("sh_p")[None]
    sc_p = cat("sc_p")[None]
    mk_p = cat("mk_p").reshape(2 * ncores, NMETA, NH, 128)
    mv_p = cat("mv_p").reshape(2 * ncores, NMETA, NH, 128)
    k_p = cat("k_p").reshape(2 * ncores, SEQ, NH, 128)
    v_p = cat("v_p").reshape(2 * ncores, SEQ, NH, 128)
    sh_s = cat("sh_s")[None]
    sc_s = cat("sc_s")[None]
    k_s = np.stack([np.asarray(r["k_s"]) for r in results], 0).reshape(ncores, DEC, NH, 128)
    v_s = np.stack([np.asarray(r["v_s"]) for r in results], 0).reshape(ncores, DEC, NH, 128)
    outs = (y_p, y_s, sh_p, sc_p, mk_p, mv_p, k_p, v_p, sh_s, sc_s, k_s, v_s)
    return tuple(np.ascontiguousarray(o, dtype=np.float32) for o in outs)


def kernel(**inputs):
    SEQ = int(np.asarray(inputs["x_prompt"]).shape[1])
    nc = build(SEQ)
    maps = make_in_maps(inputs, SEQ, 8)
    res = run_bass_kernel_spmd(nc, maps, core_ids=list(range(8)))
    return gather(res.results, SEQ, 8)
```

```python
import math
from contextlib import ExitStack

import numpy as np
import concourse.bass as bass
import concourse.mybir as mybir
from concourse.bass_utils import run_bass_kernel_spmd

F32 = mybir.dt.float32
BF16 = mybir.dt.bfloat16
AF = mybir.ActivationFunctionType
ALU = mybir.AluOpType

D = 1024
DR = 1280
NRC = 10
DFF = 4096
NH = 8
EPS = 1e-6
NT = 512
PAST = 1024
NMETA = 16
DEC = 32
LAM_INIT = 0.8 - 0.6 * math.exp(-0.3 * 1)


class Buf:
    __slots__ = ("name", "w", "r")

    def __init__(self, name):
        self.name = name
        self.w = None
        self.r = {}


class Q:
    def __init__(self, K, name, track_self=True):
        self.name = name
        self.sem = K.new_sem("q_" + name)
        self.cnt = 0
        self.waited = {}
        self.track_self = track_self
        self.prog = []

    def wait_ev(self, ev):
        if ev is None:
            return
        sem, val = ev
        if (not self.track_self) and sem is self.sem:
            return
        k = id(sem)
        if self.waited.get(k, 0) >= val:
            return
        self.prog.append(lambda eng, s=sem, v=val: eng.wait_ge(s, v))
        self.waited[k] = val

    def deps(self, reads, writes):
        for b in reads:
            self.wait_ev(b.w)
        for b in writes:
            self.wait_ev(b.w)
            for ev in b.r.values():
                self.wait_ev(ev)

    @staticmethod
    def mark(ev, reads, writes):
        k = id(ev[0])
        for b in reads:
            old = b.r.get(k)
            if old is None or old[1] < ev[1]:
                b.r[k] = ev
        for b in writes:
            b.w = ev
            b.r = {}

    def do(self, reads, writes, fn):
        self.deps(reads, writes)
        self.cnt += 1
        self.prog.append(lambda eng, f=fn, s=self.sem: f(eng).then_inc(s, 1))
        ev = (self.sem, self.cnt)
        self.mark(ev, reads, writes)
        return ev

    def dma(self, reads, writes, slot, out, in_, **kw):
        self.deps(reads, writes)
        if slot.cnt > 0:
            self.wait_ev((slot.sem, slot.cnt))
        slot.cnt += 16
        self.prog.append(
            lambda eng, o=out, i=in_, k=kw, s=slot.sem: eng.dma_start(out=o, in_=i, **k).then_inc(s, 16))
        ev = (slot.sem, slot.cnt)
        self.mark(ev, reads, writes)
        return ev


class Slot:
    def __init__(self, K, name):
        self.sem = K.new_sem("d_" + name)
        self.cnt = 0


class Kern:
    def __init__(self, nc, stack):
        self.nc = nc
        self.stack = stack
        self.nsem = 0

    def new_sem(self, name):
        self.nsem += 1
        return self.stack.enter_context(self.nc.semaphore(name))

    def sb(self, name, shape, dt, stack=None):
        return (stack or self.stack).enter_context(self.nc.sbuf_tensor("s_" + name, shape, dt))

    def ps(self, name, shape, dt):
        return self.stack.enter_context(self.nc.psum_tensor(name, shape, dt))

    def emit(self, queues):
        with self.nc.Block() as block:
            for nm, q in queues.items():
                def body(eng, q=q):
                    for th in q.prog:
                        th(eng)
                getattr(block, nm)(body)


class Ring:
    def __init__(self, items):
        self.items = items
        self.i = 0

    def next(self):
        it = self.items[self.i % len(self.items)]
        self.i += 1
        return it


def _t5_bucket(rel):
    nb = 16
    max_exact = 8
    ret = np.where(rel > 0, nb, 0)
    n = np.abs(rel)
    nf = np.maximum(n, 1).astype(np.float32)
    large = max_exact + (np.log(nf / max_exact) / math.log(128 / max_exact) * (nb - max_exact)).astype(np.int32)
    large = np.minimum(large, nb - 1)
    return ret + np.where(n < max_exact, n, large)


def _static_consts():
    rel = 127 - np.arange(384)
    bk = _t5_bucket(rel.astype(np.int32))
    ohz = np.zeros((32, 384), np.float32)
    ohz[bk, np.arange(384)] = 1.0
    kp = np.arange(128)[:, None]
    qf = np.arange(128)[None, :]
    mask0 = ((kp // 64) <= (qf // 64)).astype(np.float32)
    ident = np.eye(128, dtype=np.float32)
    return ohz, mask0, ident


def build(SEQ):
    assert SEQ % NT == 0
    NTL = SEQ // NT
    NKEY = NMETA + SEQ
    NKT = 1 + SEQ // 128
    SKEY = NMETA + PAST + DEC
    SKT = 1 + PAST // 128 + 1
    NKEYM = max(NKEY, SKEY)
    NKTM = max(NKT, SKT)

    nc = bass.Bass("TRN2", target_bir_lowering=False)

    def din(name, shape, dt=F32):
        return nc.dram_tensor(name, shape, dt, kind="ExternalInput").ap()

    def dout(name, shape):
        return nc.dram_tensor(name, shape, F32, kind="ExternalOutput").ap()

    def dscr(name, shape, dt):
        return nc.dram_tensor(name, shape, dt, kind="Internal").ap()

    xp = din("xp", [2, SEQ, D])
    xs = din("xs", [DEC, D])
    meta = din("meta", [NMETA, D])
    cst_d = din("cst", [128, 128])
    sst_d = din("sst", [128, 40])
    cmk = din("cmk", [NMETA, D])
    cmv = din("cmv", [NMETA, D])
    ck = din("ck", [PAST, D])
    cv = din("cv", [PAST, D])
    w_in = din("w_in", [D, 2 * DR])
    w_gr = din("w_gr", [NRC, 128, 128])
    w_gi = din("w_gi", [NRC, 128, 128])
    w_out = din("w_out", [DR, D])
    w_up = din("w_up", [2, D, DFF])
    w_down = din("w_down", [2, DFF, D])
    w_kv = din("w_kv", [D, 2 * D])
    w_q = din("w_q", [D, D])
    w_o = din("w_o", [D, D])
    lamv = din("lamv", [1, 256])
    subg = din("subg", [1, 128])
    relb = din("relb", [32, 8])
    ohz_d = din("ohz", [32, 384])
    mask0_d = din("mask0", [128, 128])
    ident_d = din("ident", [128, 128])
    bsel_d = din("bsel", [2, 256])

    y_p = dout("y_p", [2, SEQ, D])
    y_s = dout("y_s", [DEC, D])
    sh_p = dout("sh_p", [2, DR])
    sc_p = dout("sc_p", [2, 3, DR])
    mk_p = dout("mk_p", [2, NMETA, D])
    mv_p = dout("mv_p", [2, NMETA, D])
    k_p = dout("k_p", [2, SEQ, D])
    v_p = dout("v_p", [2, SEQ, D])
    sh_s = dout("sh_s", [1, DR])
    sc_s = dout("sc_s", [1, 3, DR])
    k_s = dout("k_s", [DEC, D])
    v_s = dout("v_s", [DEC, D])

    wb_in = dscr("wb_in", [D, 2 * DR], BF16)
    wb_out = dscr("wb_out", [DR, D], BF16)
    wb_up = dscr("wb_up", [2, D, DFF], BF16)
    wb_down = dscr("wb_down", [2, DFF, D], BF16)
    wb_kv = dscr("wb_kv", [D, 2 * D], BF16)
    wb_q = dscr("wb_q", [D, D], BF16)
    wb_o = dscr("wb_o", [D, D], BF16)
    KTs = dscr("KTs", [3, NH, 128, NKEYM], BF16)
    VAs = dscr("VAs", [3, NH, 128, NKTM, 130], BF16)
    zs = dscr("zs", [128, NH, 384], F32)

    with ExitStack() as st:
        K = Kern(nc, st)
        sy = Q(K, "sync")
        gp = Q(K, "gpsimd")
        pe = Q(K, "tensor", track_self=False)
        ve = Q(K, "vector")
        ac = Q(K, "scalar")

        cst = K.sb("cst", [128, 128], F32)
        Bcst = Buf("cst")
        dc = K.sb("dc", [128, 64], F32)
        Bdc = Buf("dc")
        ident = K.sb("ident", [128, 128], F32)
        Bident = Buf("ident")
        ones = K.sb("ones", [128, 128], BF16)
        Bones = Buf("ones")
        wg = K.sb("wg", [128, 2, NRC, 128], BF16)
        Bwg = Buf("wg")
        EBD = K.sb("EBD", [128, NH, 256], BF16)
        EBM = K.sb("EBM", [128, NH, 128], BF16)
        BEB = Buf("EB")
        Gt = K.sb("Gt", [128, 128], F32)
        BG = Buf("G")
        nlam = K.sb("nlam", [128, 1], F32)
        Bnlam = Buf("nlam")
        Gp = K.sb("Gp", [128, 1], F32)
        BGp = Buf("Gp")
        esel = K.sb("esel", [128, 4], BF16)
        Besel = Buf("esel")
        bsel = K.sb("bsel", [2, 256], F32)
        Bbsel = Buf("bsel")
        rl2 = K.sb("rl2", [2, 2, NT], F32)
        rlh = K.sb("rlh", [2, 2, NT], BF16)
        rll = K.sb("rll", [2, 2, NT], BF16)
        Brl2 = [Buf("rl2_0"), Buf("rl2_1")]
        bselb = K.sb("bselb", [2, 256], BF16)
        hT = K.sb("hT", [128, 8, NT], F32)
        BhT = [Buf(f"hT{c}") for c in range(8)]
        xn = K.sb("xn", [128, 8, NT], BF16)
        Bxn = [Buf(f"xn{c}") for c in range(8)]
        qz = K.sb("qz", [128, NH, 2, NT], BF16)
        Bqz = [Buf(f"qz{c}") for c in range(NH)]
        sqc = K.sb("sqc", [128, 2, NT], BF16)
        sqR = Ring([(sqc[:, i, :], Buf(f"sqc{i}")) for i in range(2)])
        rstd = K.sb("rstd", [128, NT], F32)
        Brstd = Buf("rstd")
        NWS = 3
        wsl = K.sb("wsl", [128, NWS, 4096], BF16)
        wR = Ring([(wsl[:, i, :], Buf(f"ws{i}"), Slot(K, f"ws{i}")) for i in range(NWS)])
        NKVO = 3
        kvo = K.sb("kvo", [128, NKVO, 512], F32)
        oR = Ring([(kvo[:, i, :], Buf(f"kvo{i}"), Slot(K, f"kvo{i}")) for i in range(NKVO)])
        vast = K.sb("vast", [128, 4, NH, 130], BF16)
        Bvast = Buf("vast")
        Svast = Slot(K, "vast")
        Skst = Slot(K, "kst")
        pages = K.sb("pages", [128, 32, NT], BF16)
        XPW = NT + 4
        xpall = K.sb("xpall", [128, NRC, XPW], F32)
        Bxp = [Buf(f"xp{c}") for c in range(NRC)]
        Bpg = [Buf(f"pg{i}") for i in range(32)]
        hstate = K.sb("hstate", [128, 4, NRC], F32)
        Bhst = [Buf(f"hst{i}") for i in range(4)]
        ctail = K.sb("ctail", [128, 4, NRC, 3], F32)
        Bct = [Buf(f"ct{i}") for i in range(4)]
        ST_M, ST_S, ST_A, ST_B = 0, 1, 2, 3

        PS = K.ps("PS", [128, 8 * 512], F32)
        bank = [PS[:, 512 * k:512 * (k + 1)] for k in range(8)]
        Bbank = [Buf(f"bank{k}") for k in range(8)]
        accR = Ring([(bank[k], Bbank[k]) for k in (0, 1, 2, 4, 5, 6, 7)])
        stR = Ring([(bank[k], Bbank[k]) for k in (0, 1)])
        LB0 = 2
        OTB = 0
        OB0 = 4

        def page_f32(i0):
            return pages[:, i0:i0 + 2, :].rearrange("p a n -> p (a n)").bitcast(F32), [Bpg[i0], Bpg[i0 + 1]]

        Bw = {}
        BKT = [[Buf(f"KT{s}_{i}") for i in range(NTL + 2)] for s in range(3)]
        BVA = [[Buf(f"VA{s}_{i}") for i in range(NTL + 2)] for s in range(3)]
        Bzs = Buf("zs")
        Sgen = Slot(K, "gen")
        Ssy = Slot(K, "sygen")

        def gp_dma_sync(reads, writes, out, in_, **kw):
            ev = gp.dma(reads, writes, Sgen, out, in_, **kw)
            return ev

        st2 = ExitStack()
        xin = K.sb("xin", [128, 2, 512], F32, stack=st2)
        xR = Ring([(xin[:, i, :], Buf(f"xin{i}"), Slot(K, f"xin{i}")) for i in range(2)])
        def convert(src, dst, name, nsplit=1):
            n = 1
            for s_ in src.shape:
                n *= s_
            names = " ".join(f"d{i}" for i in range(len(src.shape)))
            s2 = src.rearrange(f"{names} -> ({names})").rearrange("(p f) -> p f", p=128)
            d2 = dst.rearrange(f"{names} -> ({names})").rearrange("(p f) -> p f", p=128)
            f = n // 128
            step = f // nsplit
            Bw[name] = []
            for i in range(nsplit):
                b_ = Buf(f"wb_{name}_{i}")
                sl = Slot(K, f"cv_{name}_{i}")
                gp.dma([], [b_], sl, d2[:, i * step:(i + 1) * step], s2[:, i * step:(i + 1) * step])
                Bw[name].append(b_)

        sy.dma([], [Bcst], Ssy, cst[:], cst_d[:, :])
        sy.dma([], [Bident], Ssy, ident[:], ident_d[:, :])
        gp.dma([], [Bwg], Sgen, wg[:, 0, :, :], w_gr.rearrange("n i j -> i n j"))
        gp.wait_ev((Sgen.sem, Sgen.cnt))
        gp.dma([], [Bwg], Sgen, wg[:, 1, :, :], w_gi.rearrange("n i j -> i n j"))
        gp.wait_ev((Sgen.sem, Sgen.cnt))

        ve.do([], [Bones], lambda e: e.memset(ones[:], 1.0))
        ve.do([], [Besel], lambda e: e.memset(esel[:], 0.0))
        ve.do([Besel], [Besel], lambda e: e.memset(esel[:, 0:1], 1.0))
        ve.do([Besel], [Besel], lambda e: e.memset(esel[:, 3:4], 1.0))
        sy.dma([], [Bbsel], Ssy, bsel[:], bsel_d[:, :])
        ve.do([Bbsel], [Bbsel], lambda e: e.tensor_copy(out=bselb[:], in_=bsel[:]))
        ac.do([Bcst], [Bdc], lambda e: e.activation(out=dc[:, 40:50], in_=cst[:, 118:128], func=AF.Exp, scale=-1.0))
        ac.do([Bdc], [Bdc], lambda e: e.activation(out=dc[:, 50:60], in_=dc[:, 40:50], func=AF.Ln, bias=1.0))
        ve.do([Bdc], [Bdc], lambda e: e.tensor_scalar_mul(out=dc[:, 0:10], in0=dc[:, 50:60], scalar1=-4.0))
        ve.do([Bdc], [Bdc], lambda e: e.tensor_scalar_mul(out=dc[:, 10:20], in0=dc[:, 50:60], scalar1=4.0))
        ve.do([Bdc], [Bdc], lambda e: e.tensor_scalar_mul(out=dc[:, 40:50], in0=dc[:, 50:60], scalar1=-8.0))
        ve.do([Bcst, Bdc], [Bdc], lambda e: e.tensor_scalar_mul(out=dc[:, 20:40], in0=cst[:, 98:118], scalar1=0.5))

        lmt = K.sb("lmt", [128, 256], F32, stack=st2)
        Blmt = Buf("lmt")
        sy.dma([], [Blmt], Ssy, lmt[:], lamv.partition_broadcast(128))
        lmp = K.sb("lmp", [128, 2, 64], F32, stack=st2)
        Blmp = Buf("lmp")
        lms = K.sb("lms", [128, 4], F32, stack=st2)
        Blms = Buf("lms")
        ve.do([Blmt], [Blmp], lambda e: e.tensor_tensor(out=lmp[:, 0, :], in0=lmt[:, 0:64], in1=lmt[:, 64:128], op=ALU.mult))
        ve.do([Blmt], [Blmp], lambda e: e.tensor_tensor(out=lmp[:, 1, :], in0=lmt[:, 128:192], in1=lmt[:, 192:256], op=ALU.mult))
        ve.do([Blmp], [Blms], lambda e: e.reduce_sum(out=lms[:, 0:2], in_=lmp[:, :, :], axis=mybir.AxisListType.X))
        ac.do([Blms], [Blms], lambda e: e.activation(out=lms[:, 2:4], in_=lms[:, 0:2], func=AF.Exp))
        ve.do([Blms], [Bnlam], lambda e: e.tensor_tensor(out=nlam[:], in0=lms[:, 3:4], in1=lms[:, 2:3], op=ALU.subtract))
        ve.do([Bnlam], [Bnlam], lambda e: e.tensor_scalar_add(out=nlam[:], in0=nlam[:], scalar1=-LAM_INIT))
        sy.dma([], [BG], Ssy, Gt[:], subg.partition_broadcast(128))
        ve.do([BG], [BG], lambda e: e.tensor_scalar_mul(out=Gt[:], in0=Gt[:], scalar1=1.0 - LAM_INIT))
        sy.dma([], [BGp], Ssy, Gp[:], subg.rearrange("o p -> p o"), allow_slow_non_contiguous=True)
        ve.do([BGp], [BGp], lambda e: e.tensor_scalar_mul(out=Gp[:], in0=Gp[:], scalar1=1.0 - LAM_INIT))

        ohz = K.sb("ohz", [32, 384], F32, stack=st2)
        Bohz = Buf("ohz")
        rbt = K.sb("rbt", [32, 8], F32, stack=st2)
        Brbt = Buf("rbt")
        sy.dma([], [Bohz], Ssy, ohz[:], ohz_d[:, :])
        sy.dma([], [Brbt], Ssy, rbt[:], relb[:, :])
        onef = K.sb("onef", [32, 128], F32, stack=st2)
        Bonef = Buf("onef")
        ve.do([], [Bonef], lambda e: e.memset(onef[:], 1.0))
        ohs = K.sb("ohs", [32, 384], F32, stack=st2)
        Bohs = Buf("ohs")
        zall, Bzall = pages[:, 0:24, :].rearrange("p a n -> p (a n)").bitcast(F32), [Bpg[i] for i in range(24)]
        zall3 = zall[:, 0:NH * 384].rearrange("p (h r) -> p h r", h=NH)
        negc = K.sb("negc", [128, NH], F32, stack=st2)
        Bnegc = Buf("negc")
        for h in range(NH):
            ve.do([Bohz, Brbt], [Bohs], lambda e, h=h: e.tensor_scalar_mul(out=ohs[:], in0=ohz[:], scalar1=rbt[:, h:h + 1]))
            pe.do([Bohs, Bonef], [Bbank[0]], lambda e: e.matmul(bank[0][:, 0:384], lhsT=onef[:, :], rhs=ohs[:, :], start=True, stop=True))
            ve.do([Bbank[0]], [Bnegc], lambda e, h=h: e.tensor_scalar_mul(out=negc[:, h:h + 1], in0=bank[0][:, 382:383], scalar1=-1.0))
            ac.do([Bbank[0], Bnegc], Bzall, lambda e, h=h: e.activation(out=zall3[:, h, :], in_=bank[0][:, 0:384], func=AF.Exp, bias=negc[:, h:h + 1]))
        gp_dma_sync(Bzall, [Bzs], zs[:, :, :], zall3)
        gp.wait_ev((Sgen.sem, Sgen.cnt))
        d0f = K.sb("d0f", [128, NH, 256], F32, stack=st2)
        Bd0f = Buf("d0f")
        msk = K.sb("msk", [128, 128], F32, stack=st2)
        Bmsk = Buf("msk")
        sy.dma([], [Bmsk], Ssy, msk[:], mask0_d[:, :])
        sy.dma([Bzs], [Bd0f], Ssy, d0f[:, :, 0:128], bass.AP(zs.tensor, 127, [[NH * 384 - 1, 128], [384, NH], [1, 128]]))
        sy.wait_ev((Ssy.sem, Ssy.cnt))
        sy.dma([Bzs], [Bd0f], Ssy, d0f[:, :, 128:256], bass.AP(zs.tensor, 255, [[NH * 384 - 1, 128], [384, NH], [1, 128]]))
        sy.wait_ev((Ssy.sem, Ssy.cnt))
        for h in range(NH):
            ve.do([Bd0f, Bmsk], [Bd0f], lambda e, h=h: e.tensor_tensor(out=d0f[:, h, 0:128], in0=d0f[:, h, 0:128], in1=msk[:], op=ALU.mult))
        ve.do([Bd0f], [BEB], lambda e: e.tensor_copy(out=EBD[:], in_=d0f[:]))
        mf = K.sb("mf", [16, NH, 128], F32, stack=st2)
        Bmf = Buf("mf")
        sy.dma([Bzs], [Bmf], Ssy, mf[:], bass.AP(zs.tensor, 143, [[NH * 384 - 1, 16], [384, NH], [1, 128]]))
        sy.wait_ev((Ssy.sem, Ssy.cnt))
        ve.do([Bmf], [BEB], lambda e: e.tensor_copy(out=EBM[0:16, :, :], in_=mf[:]))

        sstt = K.sb("sstt", [128, 40], F32, stack=st2)
        Bsst = Buf("sst")
        sy.dma([], [Bsst], Ssy, sstt[:], sst_d[:, :])
        ve.do([], [Bhst[ST_M]], lambda e: e.memset(hstate[:, ST_M, :], 0.0))
        ve.do([], [Bct[ST_M]], lambda e: e.memset(ctail[:, ST_M, :, :], 0.0))
        ve.do([Bsst], [Bhst[ST_S]], lambda e: e.tensor_copy(out=hstate[:, ST_S, :], in_=sstt[:, 0:10]))
        ve.do([Bsst], [Bct[ST_S]], lambda e: e.tensor_copy(out=ctail[:, ST_S, :, :], in_=sstt[:, 10:40].rearrange("p (c j) -> p c j", j=3)))
        ve.do([], [Bvast], lambda e: e.memset(vast[:, :, :, 128:129], 1.0))
        ve.do([], [Bvast], lambda e: e.memset(vast[:, :, :, 129:130], 0.0))

        def cache_tile(srck, srcv, nrow, kt, kcol):
            for half in range(2):
                xa, xb_, xsl = xR.next()
                sy.dma([], [xb_], xsl, xa[0:nrow, :], srck[:, 512 * half:512 * half + 512])
                for j in range(4):
                    h = 4 * half + j
                    pe.do([xb_, Bident], [Bbank[OTB]], lambda e, xa=xa, j=j: e.transpose(
                        out=bank[OTB][:, 128 * j:128 * j + nrow], in_=xa[0:nrow, 128 * j:128 * j + 128], identity=ident[0:nrow, 0:nrow]))
                ve.do([Bbank[OTB]], [Bpg[4 * half + j] for j in range(4)], lambda e, half=half: e.tensor_copy(
                    out=pages[:, 4 * half:4 * half + 4, 0:nrow],
                    in_=bank[OTB][:, :].rearrange("p (j t) -> p j t", j=4)[:, :, 0:nrow]))
            gp.dma(Bpg[0:8], [BKT[2][0]], Skst, KTs[2].rearrange("h p n -> p h n")[:, :, kcol:kcol + nrow], pages[:, 0:8, 0:nrow])
            for half in range(2):
                xa, xb_, xsl = xR.next()
                sy.dma([], [xb_], xsl, xa[0:nrow, :], srcv[:, 512 * half:512 * half + 512])
                ve.do([xb_], [Bvast], lambda e, xa=xa, half=half: e.tensor_copy(
                    out=vast[0:nrow, 0, 4 * half:4 * half + 4, 0:128],
                    in_=xa[0:nrow, :].rearrange("p (h c) -> p h c", h=4)))
            gp.dma([Bvast], [BVA[2][0]], Svast, VAs[2].rearrange("h p t c -> p t h c")[0:nrow, kt, :, :], vast[0:nrow, 0, :, :])

        convert(w_in, wb_in, "in", 2)
        convert(w_out, wb_out, "out", 1)
        cache_tile(cmk, cmv, NMETA, 0, 0)
        for t in range(PAST // 128):
            cache_tile(ck[128 * t:128 * t + 128, :], cv[128 * t:128 * t + 128, :], 128, 1 + t, NMETA + 128 * t)
        convert(w_up[0], wb_up[0], "up0", 4)
        convert(w_down[0], wb_down[0], "down0", 4)
        convert(w_kv, wb_kv, "kv", 2)
        convert(w_q, wb_q, "q", 1)
        convert(w_o, wb_o, "o", 1)
        convert(w_up[1], wb_up[1], "up1", 4)
        convert(w_down[1], wb_down[1], "down1", 4)


        def fence():
            qs = [sy, gp, pe, ve, ac]
            slots = [Ssy, Sgen, Skst, Svast] + [x[2] for x in xR.items] + [x[2] for x in wR.items] + [x[2] for x in oR.items]
            for q in qs:
                for p in qs:
                    if p is not q and p.cnt > 0:
                        q.wait_ev((p.sem, p.cnt))
                for sl in slots:
                    if sl.cnt > 0:
                        q.wait_ev((sl.sem, sl.cnt))
        fence()
        st2.close()
        SHR = 22528
        shr = K.sb("shr", [128, SHR], BF16)

        class Carver:
            def __init__(self):
                self.off = 0

            def bf(self, n):
                v = shr[:, self.off:self.off + n]
                self.off += n
                assert self.off <= SHR
                return v

            def f32(self, n):
                v = shr[:, self.off:self.off + 2 * n].bitcast(F32)
                self.off += 2 * n
                assert self.off <= SHR
                return v

        cv1 = Carver()
        NSET = 4
        TW = NT + 4
        tmpv = [[cv1.f32(TW) for j in range(3)] for i in range(NSET)]
        Btmp = [[Buf(f"tmp{i}_{j}") for j in range(3)] for i in range(NSET)]
        tqv = [cv1.f32(NT) for i in range(5)]
        Btq = [Buf(f"tq{i}") for i in range(5)]
        ixv = [cv1.bf(NT) for i in range(5)]
        Bixc = [Buf(f"ixc{i}") for i in range(5)]
        xcbv = [cv1.bf(NT) for i in range(NSET)]
        Bxcb = [Buf(f"xcb{i}") for i in range(NSET)]
        cv2 = Carver()
        NKB = 3
        kR = Ring([(cv2.bf(1024), Buf(f"kb{i}"), Slot(K, f"kb{i}")) for i in range(NKB)])
        vR = Ring([(cv2.bf(8 * 130).rearrange("p (t c) -> p t c", t=8), Buf(f"vb{i}"), Slot(K, f"vb{i}")) for i in range(NKB)])
        NPT = 6
        pR = Ring([(cv2.bf(NT), Buf(f"pT{i}")) for i in range(NPT)])
        p2R = Ring([(cv2.bf(NT), Buf(f"p2s{i}")) for i in range(8)])
        rlv = [cv2.f32(NT) for m in range(2)]
        Brlv = [Buf(f"rlv{m}") for m in range(2)]
        ofv = cv2.f32(NT)
        Bofv = Buf("ofv")
        otv = cv2.f32(NT)
        Botv = Buf("otv")
        rsov = cv2.f32(NT)
        Brsov = Buf("rsov")
        sqov = cv2.bf(NT)
        Bsqov = Buf("sqov")
        cv3 = Carver()
        NXS = 8
        xR = Ring([(cv3.f32(512), Buf(f"xs{i}"), Slot(K, f"xs{i}")) for i in range(NXS)])
        x_alias = [it[1] for it in kR.items] + [it[1] for it in vR.items] + [it[1] for it in pR.items]
        for ts_ in Btmp:
            x_alias += list(ts_)
        x_alias += Btq + Bixc + Bxcb
        gp.do([], Bqz, lambda e: e.memset(qz[64:128, :, 0, :], 0.0))
        gp.do([], Bqz, lambda e: e.memset(qz[0:64, :, 1, :], 0.0))

        wseq = []
        for g in (2, 3, 4, 0, 1):
            wseq.append(("in", g))
        for g in range(4):
            wseq.append(("out", g))
        for l in range(2):
            if l == 1:
                for g in range(2):
                    wseq.append(("q", g))
                for g in range(2):
                    wseq.append(("o", g))
            for g in range(8):
                wseq.append((f"up{l}", g))
            for cg in range(4):
                for kh in range(2):
                    wseq.append((f"down{l}", cg, kh))
            if l == 0:
                for g in range(4):
                    wseq.append(("kv", g))
        NTILES = 1 + 2 * NTL
        wglobal = wseq * NTILES
        wstate = {"issued": 0, "used": 0, "info": []}

        def w_src(item):
            kind = item[0]
            if kind == "in":
                g = item[1]
                return wb_in.rearrange("(k p) n -> p k n", p=128)[:, :, 512 * g:512 * g + 512], (8, 512), Bw["in"]
            if kind == "out":
                g = item[1]
                return wb_out.rearrange("(k p) n -> p k n", p=128)[:, :, 256 * g:256 * g + 256], (10, 256), Bw["out"]
            if kind.startswith("up"):
                l, g = int(kind[2]), item[1]
                return wb_up[l].rearrange("(k p) n -> p k n", p=128)[:, :, 512 * g:512 * g + 512], (8, 512), Bw[kind]
            if kind.startswith("down"):
                l, cg, kh = int(kind[4]), item[1], item[2]
                return (wb_down[l].rearrange("(k p) n -> p k n", p=128)[:, 16 * kh:16 * kh + 16, 256 * cg:256 * cg + 256],
                        (16, 256), Bw[kind])
            if kind == "kv":
                g = item[1]
                return wb_kv.rearrange("(k p) n -> p k n", p=128)[:, :, 512 * g:512 * g + 512], (8, 512), Bw["kv"]
            if kind == "q":
                g = item[1]
                return wb_q.rearrange("(k p) n -> p k n", p=128)[:, :, 512 * g:512 * g + 512], (8, 512), Bw["q"]
            if kind == "o":
                g = item[1]
                return wb_o.rearrange("(k p) n -> p k n", p=128)[:, :, 512 * g:512 * g + 512], (8, 512), Bw["o"]
            raise ValueError(kind)

        def w_issue_upto(n):
            while wstate["issued"] < min(n, len(wglobal)):
                item = wglobal[wstate["issued"]]
                src, (kk, nn), sbuf = w_src(item)
                ap, b, sl = wR.next()
                view = ap[:, 0:kk * nn].rearrange("p (k n) -> p k n", k=kk)
                sy.dma(list(sbuf), [b], sl, view, src)
                wstate["info"].append((item, view, b))
                wstate["issued"] += 1

        def w_next(expect):
            i = wstate["used"]
            w_issue_upto(i + NWS)
            item, view, b = wstate["info"][i]
            assert item == expect, (item, expect)
            wstate["used"] += 1
            return view, b

        def evac_alt(i):
            return ac if (i % 2 == 0) else ve

        def copy_on(q, out, in_):
            if q is ac:
                return lambda e: e.copy(out=out, in_=in_)
            return lambda e: e.tensor_copy(out=out, in_=in_)

        nstat = {"pending": None, "count": 0}
        NB_ = 3

        def stat_emit(j, c0, c1):
            n = c1 - c0
            nb, Bnb = bank[NB_], Bbank[NB_]
            sa, sb_ = sqR.next()
            cnt_ = nstat["count"]
            ac.do([BhT[j]], [sb_], lambda e: e.activation(out=sa[:, 0:n], in_=hT[:, j, c0:c1], func=AF.Square))
            pe.do([sb_, Bones], [Bnb], lambda e: e.matmul(nb[:, 0:n], lhsT=ones[:, :], rhs=sa[:, 0:n], start=(cnt_ == 0), stop=(cnt_ == 7)))
            nstat["count"] = cnt_ + 1

        def stat_defer(j, c0, c1):
            if nstat["pending"] is not None:
                stat_emit(*nstat["pending"])
            nstat["pending"] = (j, c0, c1)

        def rmsnorm(gcol, c0, c1, out_f32=False, reuse=False):
            n = c1 - c0
            nb, Bnb = bank[NB_], Bbank[NB_]
            if not reuse:
                if nstat["pending"] is not None:
                    stat_emit(*nstat["pending"])
                    nstat["pending"] = None
                if nstat["count"] == 0:
                    for c in range(8):
                        stat_emit(c, c0, c1)
                assert nstat["count"] == 8, nstat
                nstat["count"] = 0
                ac.do([Bnb], [Brstd], lambda e: e.activation(out=rstd[:, c0:c1], in_=nb[:, 0:n], func=AF.Ln, scale=1.0 / D, bias=EPS))
                ac.do([Brstd], [Brstd], lambda e: e.activation(out=rstd[:, c0:c1], in_=rstd[:, c0:c1], func=AF.Exp, scale=-0.5))
            for c in range(8):
                if out_f32:
                    ve.do([BhT[c], Brstd, Bcst], [BhT[c]], lambda e, c=c: e.scalar_tensor_tensor(
                        out=hT[:, c, c0:c1], in0=hT[:, c, c0:c1], scalar=cst[:, gcol + c:gcol + c + 1], in1=rstd[:, c0:c1],
                        op0=ALU.mult, op1=ALU.mult))
                else:
                    ve.do([BhT[c], Brstd, Bcst], [Bxn[c]], lambda e, c=c: e.scalar_tensor_tensor(
                        out=xn[:, c, c0:c1], in0=hT[:, c, c0:c1], scalar=cst[:, gcol + c:gcol + c + 1], in1=rstd[:, c0:c1],
                        op0=ALU.mult, op1=ALU.mult))

        def resid_add(j, pb, Bpb, c0, c1):
            n = c1 - c0
            ve.do([Bpb, BhT[j]], [BhT[j]], lambda e: e.tensor_tensor(out=hT[:, j, c0:c1], in0=hT[:, j, c0:c1], in1=pb[:, 0:n], op=ALU.add))
            stat_defer(j, c0, c1)

        def mlp(l, gcol, c0, c1):
            n = c1 - c0
            rmsnorm(gcol, c0, c1)
            for g in range(8):
                wv, wbuf = w_next((f"up{l}", g))
                for jj in range(4):
                    j = 4 * g + jj
                    pb, Bpb = accR.next()
                    for kc in range(8):
                        pe.do([wbuf, Bxn[kc]], [Bpb], lambda e, kc=kc, jj=jj, pb=pb, wv=wv: e.matmul(
                            pb[:, 0:n], lhsT=wv[:, kc, 128 * jj:128 * jj + 128], rhs=xn[:, kc, c0:c1], start=(kc == 0), stop=(kc == 7)))
                    if j % 2 == 0:
                        ve.do([Bpb], [Bpg[j]], lambda e, j=j, pb=pb: e.tensor_scalar_max(out=pages[:, j, 0:n], in0=pb[:, 0:n], scalar1=0.0))
                    else:
                        ac.do([Bpb], [Bpg[j]], lambda e, j=j, pb=pb: e.activation(out=pages[:, j, 0:n], in_=pb[:, 0:n], func=AF.Relu))
                    gp.do([Bpg[j]], [Bpg[j]], lambda e, j=j: e.tensor_tensor(out=pages[:, j, 0:n], in0=pages[:, j, 0:n], in1=pages[:, j, 0:n], op=ALU.mult))
            for cg in range(4):
                pbs = [accR.next(), accR.next()]
                for kh in range(2):
                    wv, wbuf = w_next((f"down{l}", cg, kh))
                    for jj in range(2):
                        pb, Bpb = pbs[jj]
                        for k2 in range(16):
                            kc = 16 * kh + k2
                            pe.do([wbuf, Bpg[kc]], [Bpb], lambda e, k2=k2, kc=kc, jj=jj, pb=pb, wv=wv: e.matmul(
                                pb[:, 0:n], lhsT=wv[:, k2, 128 * jj:128 * jj + 128], rhs=pages[:, kc, 0:n], start=(kc == 0), stop=(kc == 31)))
                for jj in range(2):
                    resid_add(2 * cg + jj, pbs[jj][0], pbs[jj][1], c0, c1)

        def layer_a(ncol, segs):
            rmsnorm(0, 0, ncol)
            n = ncol
            gel_pg = 0
            worder = (2, 3, 4, 0, 1)
            xr_view = [(xpall[:, c, 3:3 + NT], [Bxp[c]]) for c in range(NRC)]
            for g in worder:
                wv, wbuf = w_next(("in", g))
                for jj in range(4):
                    j = 4 * g + jj
                    pb, Bpb = accR.next()
                    for kc in range(8):
                        pe.do([wbuf, Bxn[kc]], [Bpb], lambda e, kc=kc, jj=jj, pb=pb, wv=wv: e.matmul(
                            pb[:, 0:n], lhsT=wv[:, kc, 128 * jj:128 * jj + 128], rhs=xn[:, kc, 0:n], start=(kc == 0), stop=(kc == 7)))
                    if j < NRC:
                        ac.do([Bpb], [Bpg[gel_pg + j]], lambda e, j=j, pb=pb: e.activation(
                            out=pages[:, gel_pg + j, 0:n], in_=pb[:, 0:n], func=AF.Gelu_apprx_tanh))
                    else:
                        c = j - NRC
                        xv, xb_ = xr_view[c]
                        ve.do([Bpb], xb_, lambda e, xv=xv, pb=pb: e.tensor_copy(out=xv[:, 0:n], in_=pb[:, 0:n]))
            return xr_view

        def layer_a_chunks(ncol, segs, xr_view):
            n = ncol
            gel_pg = 0

            def P1a(c):
                tb = Btmp[c % NSET]
                xpc, xc = tmpv[c % NSET][0], tmpv[c % NSET][1]
                xv, xb_ = xr_view[c]
                for si, (s0, sn, stt) in enumerate(segs):
                    ve.do([Bct[stt]], xb_, lambda e, s0=s0, stt=stt: e.tensor_copy(out=xpall[:, c, s0:s0 + 3], in_=ctail[:, stt, c, :]))
                    ve.do(xb_ + [Bcst], [tb[1]], lambda e, s0=s0, sn=sn: e.tensor_scalar(
                        out=xc[:, s0:s0 + sn], in0=xpall[:, c, s0:s0 + sn], scalar1=cst[:, 48 + c:49 + c], scalar2=cst[:, 88 + c:89 + c],
                        op0=ALU.mult, op1=ALU.add))
                    for j in range(1, 4):
                        ve.do(xb_ + [Bcst, tb[1]], [tb[1]], lambda e, s0=s0, sn=sn, j=j: e.scalar_tensor_tensor(
                            out=xc[:, s0:s0 + sn], in0=xpall[:, c, s0 + j:s0 + j + sn], scalar=cst[:, 48 + 10 * j + c:49 + 10 * j + c],
                            in1=xc[:, s0:s0 + sn], op0=ALU.mult, op1=ALU.add))
                    ve.do(xb_, [Bct[stt]], lambda e, s0=s0, stt=stt, sn=sn: e.tensor_copy(out=ctail[:, stt, c, :], in_=xpall[:, c, s0 + sn:s0 + sn + 3]))
                xcb_ = xcbv[c % NSET]
                ac.do([tb[1]], [Bxcb[c % NSET]], lambda e: e.copy(out=xcb_[:, 0:n], in_=xc[:, 0:n]))
                pr, Bpr = accR.next()
                pi, Bpi = accR.next()
                pe.do([Bwg, Bxcb[c % NSET]], [Bpr], lambda e: e.matmul(pr[:, 0:n], lhsT=wg[:, 0, c, :], rhs=xcb_[:, 0:n], start=True, stop=True))
                pe.do([Bwg, Bxcb[c % NSET]], [Bpi], lambda e: e.matmul(pi[:, 0:n], lhsT=wg[:, 1, c, :], rhs=xcb_[:, 0:n], start=True, stop=True))
                return (pr, Bpr, pi, Bpi)

            def P1t(c, k, banks):
                pr, Bpr, pi, Bpi = banks
                tb = Btmp[c % NSET]
                ti = tmpv[c % NSET][2]
                xv, xb_ = xr_view[c]
                ac.do([Bpr, Bdc], xb_, lambda e: e.activation(out=xv[:, 0:n], in_=pr[:, 0:n], func=AF.Tanh, scale=0.5, bias=dc[:, 20 + c:21 + c]))
                ac.do([Bpi, Bdc], [tb[2]], lambda e: e.activation(out=ti[:, 0:n], in_=pi[:, 0:n], func=AF.Tanh, scale=0.5, bias=dc[:, 30 + c:31 + c]))
                ac.do(xb_ + [Bdc], [Btq[k]], lambda e: e.activation(out=tqv[k][:, 0:n], in_=xv[:, 0:n], func=AF.Tanh, scale=dc[:, 10 + c:11 + c], bias=dc[:, 10 + c:11 + c]))

            def P1x(c, k):
                tb = Btmp[c % NSET]
                xc, ti = tmpv[c % NSET][1], tmpv[c % NSET][2]
                ve.do([tb[2], tb[1]], [Bixc[k]], lambda e: e.scalar_tensor_tensor(
                    out=ixv[k][:, 0:n], in0=ti[:, 0:n], scalar=1.0, in1=xc[:, 0:n], op0=ALU.add, op1=ALU.mult))

            def P2a(c, k):
                tb = Btmp[c % NSET]
                a_, w_ = tmpv[c % NSET][0], tmpv[c % NSET][1]
                xv, xb_ = xr_view[c]
                ac.do(xb_ + [Bdc], [tb[0]], lambda e: e.activation(out=a_[:, 0:n], in_=xv[:, 0:n], func=AF.Exp, scale=dc[:, c:c + 1], bias=dc[:, c:c + 1]))
                ac.do(xb_ + [Bdc], [tb[1]], lambda e: e.activation(out=w_[:, 0:n], in_=xv[:, 0:n], func=AF.Exp, scale=dc[:, 40 + c:41 + c], bias=dc[:, 40 + c:41 + c]))

            def P2w(c, k):
                tb = Btmp[c % NSET]
                w_ = tmpv[c % NSET][1]
                ve.do([tb[1], Btq[k]], [tb[1]], lambda e: e.scalar_tensor_tensor(
                    out=w_[:, 0:n], in0=w_[:, 0:n], scalar=1.0, in1=tqv[k][:, 0:n], op0=ALU.add, op1=ALU.mult))

            def P2b(c, k):
                tb = Btmp[c % NSET]
                w_ = tmpv[c % NSET][1]
                ac.do([tb[1]], [tb[1]], lambda e: e.activation(out=w_[:, 0:n], in_=w_[:, 0:n], func=AF.Ln))
                ac.do([tb[1]], [tb[1]], lambda e: e.activation(out=w_[:, 0:n], in_=w_[:, 0:n], func=AF.Exp, scale=0.5, bias=math.log(0.5)))

            def P2u(c, k):
                tb = Btmp[c % NSET]
                w_ = tmpv[c % NSET][1]
                gp.do([tb[1], Bixc[k]], [tb[1]], lambda e: e.tensor_tensor(out=w_[:, 0:n], in0=w_[:, 0:n], in1=ixv[k][:, 0:n], op=ALU.mult))

            def P2s(c, k):
                tb = Btmp[c % NSET]
                a_, w_, hs = tmpv[c % NSET][0], tmpv[c % NSET][1], tmpv[c % NSET][2]
                for (s0, sn, stt) in segs:
                    ve.do([tb[0], tb[1], Bhst[stt]], [tb[2]], lambda e, s0=s0, sn=sn, stt=stt: e.tensor_tensor_scan(
                        out=hs[:, s0:s0 + sn], data0=a_[:, s0:s0 + sn], data1=w_[:, s0:s0 + sn], initial=hstate[:, stt, c:c + 1],
                        op0=ALU.mult, op1=ALU.add))
                    ve.do([tb[2]], [Bhst[stt]], lambda e, s0=s0, sn=sn, stt=stt: e.tensor_copy(out=hstate[:, stt, c:c + 1], in_=hs[:, s0 + sn - 1:s0 + sn]))
                gp.do([tb[2], Bpg[gel_pg + c]], [Bpg[gel_pg + c]], lambda e: e.tensor_tensor(
                    out=pages[:, gel_pg + c, 0:n], in0=pages[:, gel_pg + c, 0:n], in1=hs[:, 0:n], op=ALU.mult))

            for half in range(2):
                cs = list(range(5 * half, 5 * half + 5))
                banks = {}
                for t in range(5 + 2):
                    if t < 5:
                        banks[cs[t]] = P1a(cs[t])
                    if 0 <= t - 1 < 5:
                        P1t(cs[t - 1], t - 1, banks[cs[t - 1]])
                    if 0 <= t - 2 < 5:
                        P1x(cs[t - 2], t - 2)
                for t in range(5 + 3):
                    if t < 5:
                        P2a(cs[t], t)
                    if 0 <= t - 1 < 5:
                        P2b(cs[t - 1], t - 1)
                    if 0 <= t - 2 < 5:
                        P2u(cs[t - 2], t - 2)
                    if 0 <= t - 3 < 5:
                        P2s(cs[t - 3], t - 3)
                    if t < 5:
                        P2w(cs[t], t)
            for g in range(4):
                wv, wbuf = w_next(("out", g))
                for jj in range(2):
                    j = 2 * g + jj
                    pb, Bpb = accR.next()
                    for kc in range(NRC):
                        pe.do([wbuf, Bpg[gel_pg + kc]], [Bpb], lambda e, kc=kc, jj=jj, pb=pb, wv=wv: e.matmul(
                            pb[:, 0:n], lhsT=wv[:, kc, 128 * jj:128 * jj + 128], rhs=pages[:, gel_pg + kc, 0:n], start=(kc == 0), stop=(kc == NRC - 1)))
                    resid_add(j, pb, Bpb, 0, n)

        def kv_stage(ncol, subs, kouts, vouts, kt_dsts, va_dsts):
            n = ncol
            rmsnorm(16, 0, n)
            for g in range(4):
                wv, wbuf = w_next(("kv", g))
                for si, (s0, sn) in enumerate(subs):
                    pb, Bpb = accR.next()
                    for kc in range(8):
                        pe.do([wbuf, Bxn[kc]], [Bpb], lambda e, kc=kc, pb=pb, wv=wv, s0=s0, sn=sn: e.matmul(
                            pb[0:sn, 0:512], lhsT=xn[:, kc, s0:s0 + sn], rhs=wv[:, kc, :], start=(kc == 0), stop=(kc == 7)))
                    oa, ob, osl = oR.next()
                    ac.do([Bpb], [ob], lambda e, oa=oa, pb=pb, sn=sn: e.copy(out=oa[0:sn, :], in_=pb[0:sn, 0:512]))
                    outs = kouts[si] if g < 2 else vouts[si]
                    for (dst_fn, r0, r1) in outs:
                        ac.dma([ob], [], osl, dst_fn(g % 2), oa[r0:r1, :])
                    if g >= 2:
                        ve.do([ob], [Bvast], lambda e, oa=oa, sn=sn, si=si, g=g: e.tensor_copy(
                            out=vast[0:sn, si, 4 * (g - 2):4 * (g - 2) + 4, 0:128], in_=oa[0:sn, :].rearrange("p (h c) -> p h c", h=4)))
                if g < 2:
                    for jj in range(4):
                        h = 4 * g + jj
                        pb, Bpb = accR.next()
                        for kc in range(8):
                            pe.do([wbuf, Bxn[kc]], [Bpb], lambda e, kc=kc, jj=jj, pb=pb, wv=wv: e.matmul(
                                pb[:, 0:n], lhsT=wv[:, kc, 128 * jj:128 * jj + 128], rhs=xn[:, kc, 0:n], start=(kc == 0), stop=(kc == 7)))
                        q_ = evac_alt(jj)
                        q_.do([Bpb], [Bpg[h]], copy_on(q_, pages[:, h, 0:n], pb[:, 0:n]))
            for (seq, bi, sc0, ncs, dc0) in kt_dsts:
                gp.dma(Bpg[0:8], [BKT[seq][bi]], Skst, KTs[seq].rearrange("h p n -> p h n")[:, :, dc0:dc0 + ncs], pages[:, 0:8, sc0:sc0 + ncs])
            for (seq, bi, si, r0, r1, kt, p0) in va_dsts:
                gp.dma([Bvast], [BVA[seq][bi]], Svast, VAs[seq].rearrange("h p t c -> p t h c")[p0:p0 + (r1 - r0), kt, :, :], vast[r0:r1, si, :, :])

        def attention(seq, c0, subs, ktiles, q_base_sub):
            nsub = len(subs)
            ncols_all = subs[-1][0] + subs[-1][1]
            nblk = (len(ktiles) + 7) // 8
            LAG = 2
            blocks = []
            steps = []
            for h in range(NH):
                for b in range(nblk):
                    kts = ktiles[8 * b:8 * b + 8]
                    blocks.append((h, b, kts))
                    for ti_, t_ in enumerate(kts):
                        ks = t_[3]
                        s_first = max(0, ks - q_base_sub) if ks >= 0 else 0
                        if s_first >= nsub:
                            continue
                        for m in range(2):
                            steps.append((h, len(blocks) - 1, ti_, m, s_first))
            blk_loaded = {}

            def load_block(bi):
                if bi >= len(blocks) or bi in blk_loaded:
                    return
                h, b, kts = blocks[bi]
                kc_lo = kts[0][2]
                kc_hi = kts[-1][2] + kts[-1][1]
                ka, kb_, ksl = kR.next()
                rbufs = []
                for t_ in kts:
                    for bb in t_[4]:
                        if bb not in rbufs:
                            rbufs.append(bb)
                sy.dma([x for x in rbufs if x.name.startswith("KT")], [kb_], ksl, ka[:, 0:kc_hi - kc_lo], KTs[seq, h, :, kc_lo:kc_hi])
                va, vb_, vsl = vR.next()
                t_lo = kts[0][0]
                gi = 0
                while gi < len(kts):
                    gj = gi
                    while gj + 1 < len(kts) and kts[gj + 1][1] == kts[gi][1]:
                        gj += 1
                    nkg = kts[gi][1]
                    sy.dma([x for x in rbufs if x.name.startswith("VA")], [vb_], vsl, va[0:nkg, gi:gj + 1, :],
                           VAs[seq, h, 0:nkg, t_lo + gi:t_lo + gj + 1, :])
                    gi = gj + 1
                blk_loaded[bi] = (ka, kb_, va, vb_, kc_lo)

            started = set()
            pend = {}
            deferred = []
            cur_iter = [0]
            last_hm = {}
            for j_, st_ in enumerate(steps):
                last_hm[(st_[0], st_[3])] = j_
            front_info = {}
            head_first_step = {}
            head_last_step = {}
            for j, st_ in enumerate(steps):
                head_first_step.setdefault(st_[0], j)
                head_last_step[st_[0]] = j

            def front(j):
                h, bi, ti_, m, s_first = steps[j]
                load_block(bi)
                load_block(bi + 1)
                ka, kb_, va, vb_, kc_lo = blk_loaded[bi]
                kt, nk, kcol, ks, _ = blocks[bi][2][ti_]
                q_lo = subs[s_first][0]
                ncv = ncols_all - q_lo
                hp = h % 2
                sb_, Bsb = stR.next()
                pe.do([kb_, Bqz[h]], [Bsb], lambda e: e.matmul(
                    sb_[0:nk, 0:ncv], lhsT=ka[:, kcol - kc_lo:kcol - kc_lo + nk],
                    rhs=qz[:, h, m, c0 + q_lo:c0 + q_lo + ncv], start=True, stop=True))
                pa, Bpa = pR.next()
                ac.do([Bsb], [Bpa], lambda e: e.activation(out=pa[0:nk, 0:ncv], in_=sb_[0:nk, 0:ncv], func=AF.Exp, scale=0.125))
                for s in range(s_first, nsub):
                    qs = q_base_sub + s
                    off = subs[s][0] - q_lo
                    ns = subs[s][1]
                    eb = None
                    if ks < 0:
                        if qs == 0:
                            eb = EBM[0:nk, h, 0:ns]
                    elif ks == qs:
                        eb = EBD[0:nk, h, 0:ns]
                    elif ks == qs - 1:
                        eb = EBD[0:nk, h, 128:128 + ns]
                    if eb is not None:
                        ve.do([Bpa, BEB], [Bpa], lambda e, off=off, ns=ns, eb=eb: e.tensor_tensor(
                            out=pa[0:nk, off:off + ns], in0=pa[0:nk, off:off + ns], in1=eb, op=ALU.mult))
                front_info[j] = (pa, Bpa, va, vb_, nk, ti_, q_lo, ncv)

            def back(j):
                h, bi, ti_, m, s_first = steps[j]
                pa, Bpa, va, vb_, nk, ti_, q_lo, ncv = front_info.pop(j)
                bk = OB0 + 2 * (h % 2) + m
                first = (h, m) not in started
                started.add((h, m))
                pe.do([Bpa, vb_], [Bbank[bk]], lambda e: e.matmul(
                    bank[bk][:, q_lo:q_lo + ncv], lhsT=va[0:nk, ti_, 0:128], rhs=pa[0:nk, 0:ncv],
                    start=first, stop=False, skip_group_check=True))
                lbk = LB0 + (h % 2)

                def lmm(rhs, rb, nk_, q_lo_, ncv_):
                    firstl = ("l", h) not in started
                    started.add(("l", h))
                    pe.do([rb, Besel], [Bbank[lbk]], lambda e: e.matmul(
                        bank[lbk][0:2, q_lo_:q_lo_ + ncv_], lhsT=esel[0:nk_, 2 * m:2 * m + 2], rhs=rhs[0:nk_, 0:ncv_],
                        start=firstl, stop=False, skip_group_check=True))

                key = (h, m)
                full = (nk == 128 and q_lo == 0 and ncv == ncols_all)
                islast = (j == last_hm[key])
                GRP = 4
                st_ = pend.get(key)

                def close_sum(s2, Bs2, ncv_, delay):
                    if delay:
                        deferred.append((cur_iter[0] + 2, lambda: lmm(s2, Bs2, 128, 0, ncv_)))
                    else:
                        lmm(s2, Bs2, 128, 0, ncv_)

                if st_ is None:
                    if full and not islast:
                        pend[key] = ("single", pa, Bpa, nk, q_lo, ncv)
                    else:
                        lmm(pa, Bpa, nk, q_lo, ncv)
                elif st_[0] == "single":
                    _, ppa, Bppa, pnk, pq, pncv = st_
                    del pend[key]
                    if full:
                        s2, Bs2 = p2R.next()
                        ve.do([Bppa, Bpa], [Bs2], lambda e: e.tensor_tensor(out=s2[:, 0:ncv], in0=ppa[:, 0:ncv], in1=pa[:, 0:ncv], op=ALU.add))
                        if islast or GRP <= 2:
                            close_sum(s2, Bs2, ncv, True)
                        else:
                            pend[key] = ("sum", s2, Bs2, 2, ncv)
                    else:
                        lmm(ppa, Bppa, pnk, pq, pncv)
                        lmm(pa, Bpa, nk, q_lo, ncv)
                else:
                    _, s2, Bs2, cnt_, sncv = st_
                    del pend[key]
                    if full:
                        ve.do([Bs2, Bpa], [Bs2], lambda e: e.tensor_tensor(out=s2[:, 0:ncv], in0=s2[:, 0:ncv], in1=pa[:, 0:ncv], op=ALU.add))
                        cnt_ += 1
                        if islast or cnt_ >= GRP:
                            close_sum(s2, Bs2, ncv, True)
                        else:
                            pend[key] = ("sum", s2, Bs2, cnt_, sncv)
                    else:
                        close_sum(s2, Bs2, sncv, False)
                        lmm(pa, Bpa, nk, q_lo, ncv)

            def finalize(h, stg):
                hp = h % 2
                nq = ncols_all
                lbk = LB0 + hp
                b0, b1 = OB0 + 2 * hp, OB0 + 2 * hp + 1
                if stg == 0:
                    ac.do([Bbank[lbk]], [Brl2[hp]], lambda e: e.activation(out=rl2[0:2, hp, 0:nq], in_=bank[lbk][0:2, 0:nq], func=AF.Ln))
                    ac.do([Brl2[hp]], [Brl2[hp]], lambda e: e.activation(out=rl2[0:2, hp, 0:nq], in_=rl2[0:2, hp, 0:nq], func=AF.Exp, scale=-1.0))
                    ac.do([Brl2[hp]], [Brl2[hp]], lambda e: e.copy(out=rlh[0:2, hp, 0:nq], in_=rl2[0:2, hp, 0:nq]))
                    ve.do([Brl2[hp]], [Brl2[hp]], lambda e: e.tensor_tensor(out=rll[0:2, hp, 0:nq], in0=rl2[0:2, hp, 0:nq], in1=rlh[0:2, hp, 0:nq], op=ALU.subtract))
                elif stg == 1:
                    for m in range(2):
                        lb, Blb = stR.next()
                        pe.do([Brl2[hp], Bbsel], [Blb], lambda e, m=m, lb=lb: e.matmul(lb[:, 0:nq], lhsT=bselb[0:2, 128 * m:128 * m + 128], rhs=rlh[0:2, hp, 0:nq], start=True, stop=False))
                        pe.do([Brl2[hp], Bbsel], [Blb], lambda e, m=m, lb=lb: e.matmul(lb[:, 0:nq], lhsT=bselb[0:2, 128 * m:128 * m + 128], rhs=rll[0:2, hp, 0:nq], start=False, stop=True))
                        ac.do([Blb], [Brlv[m]], lambda e, m=m, lb=lb: e.copy(out=rlv[m][:, 0:nq], in_=lb[:, 0:nq]))
                elif stg == 2:
                    ve.do([Bbank[b0], Brlv[0]], [Bofv], lambda e: e.tensor_tensor(out=ofv[:, 0:nq], in0=bank[b0][:, 0:nq], in1=rlv[0][:, 0:nq], op=ALU.mult))
                    ve.do([Bbank[b1], Brlv[1]], [Botv], lambda e: e.tensor_tensor(out=otv[:, 0:nq], in0=bank[b1][:, 0:nq], in1=rlv[1][:, 0:nq], op=ALU.mult))
                    ve.do([Bofv, Botv, Bnlam], [Bofv], lambda e: e.scalar_tensor_tensor(
                        out=ofv[:, 0:nq], in0=otv[:, 0:nq], scalar=nlam[:, 0:1], in1=ofv[:, 0:nq], op0=ALU.mult, op1=ALU.add))
                    ac.do([Bofv], [Bsqov], lambda e: e.activation(out=sqov[:, 0:nq], in_=ofv[:, 0:nq], func=AF.Square))
                elif stg == 3:
                    sbk, Bsbk = stR.next()
                    pe.do([Bsqov, Bones], [Bsbk], lambda e: e.matmul(sbk[:, 0:nq], lhsT=ones[:, :], rhs=sqov[:, 0:nq], start=True, stop=True))
                    ac.do([Bsbk], [Brsov], lambda e: e.activation(out=rsov[:, 0:nq], in_=sbk[:, 0:nq], func=AF.Ln, scale=1.0 / 128, bias=EPS))
                    ac.do([Brsov], [Brsov], lambda e: e.activation(out=rsov[:, 0:nq], in_=rsov[:, 0:nq], func=AF.Exp, scale=-0.5))
                else:
                    ve.do([Bofv, Brsov, BGp], [Bxn[h]], lambda e: e.scalar_tensor_tensor(
                        out=xn[:, h, c0:c0 + nq], in0=ofv[:, 0:nq], scalar=Gp[:, 0:1], in1=rsov[:, 0:nq], op0=ALU.mult, op1=ALU.mult))

            pending = []
            nst = len(steps)
            for j in range(nst + LAG):
                cur_iter[0] = j
                if j < nst:
                    front(j)
                jb = j - LAG
                if jb >= 0:
                    back(jb)
                    hb = steps[jb][0]
                    if jb == head_last_step[hb]:
                        while deferred:
                            deferred.pop(0)[1]()
                        for stg in range(5):
                            pending.append((j + 1 + 2 * stg, hb, stg))
                        pending.sort(key=lambda x: (x[1], x[2]))
                while deferred and deferred[0][0] <= j:
                    deferred.pop(0)[1]()
                while pending and pending[0][0] <= j:
                    _, h_, stg_ = pending.pop(0)
                    finalize(h_, stg_)
            while deferred:
                deferred.pop(0)[1]()
            pending.sort(key=lambda x: (x[1], x[2]))
            while pending:
                _, h_, stg_ = pending.pop(0)
                finalize(h_, stg_)

        def layer_b(seq, c0, c1, subs, ktiles, q_base_sub):
            n = c1 - c0
            rmsnorm(24, c0, c1, reuse=True)
            for g in range(2):
                wv, wbuf = w_next(("q", g))
                for jj in range(4):
                    h = 4 * g + jj
                    pb, Bpb = accR.next()
                    for kc in range(8):
                        pe.do([wbuf, Bxn[kc]], [Bpb], lambda e, kc=kc, jj=jj, pb=pb, wv=wv: e.matmul(
                            pb[:, 0:n], lhsT=wv[:, kc, 128 * jj:128 * jj + 128], rhs=xn[:, kc, c0:c1], start=(kc == 0), stop=(kc == 7)))
                    ac.do([Bpb], [Bqz[h]], lambda e, h=h, pb=pb: e.copy(out=qz[0:64, h, 0, c0:c1], in_=pb[0:64, 0:n]))
                    ve.do([Bpb], [Bqz[h]], lambda e, h=h, pb=pb: e.tensor_copy(out=qz[64:128, h, 1, c0:c1], in_=pb[64:128, 0:n]))
            attention(seq, c0, subs, ktiles, q_base_sub)
            for g in range(2):
                wv, wbuf = w_next(("o", g))
                for jj in range(4):
                    j = 4 * g + jj
                    pb, Bpb = accR.next()
                    for kc in range(8):
                        pe.do([wbuf, Bxn[kc]], [Bpb], lambda e, kc=kc, jj=jj, pb=pb, wv=wv: e.matmul(
                            pb[:, 0:n], lhsT=wv[:, kc, 128 * jj:128 * jj + 128], rhs=xn[:, kc, c0:c1], start=(kc == 0), stop=(kc == 7)))
                    resid_add(j, pb, Bpb, c0, c1)

        def final_out(c0, subs, dst_fn):
            c1 = c0 + subs[-1][0] + subs[-1][1]
            rmsnorm(40, c0, c1, out_f32=True)
            for si, (off, ns) in enumerate(subs):
                for half in range(2):
                    pb, Bpb = accR.next()
                    for j in range(4):
                        c = 4 * half + j
                        pe.do([BhT[c], Bident], [Bpb], lambda e, pb=pb, j=j, c=c, off=off, ns=ns: e.transpose(
                            out=pb[0:ns, 128 * j:128 * j + 128], in_=hT[:, c, c0 + off:c0 + off + ns], identity=ident[:, :]))
                    oa, ob, osl = oR.next()
                    ac.do([Bpb], [ob], lambda e, oa=oa, pb=pb, ns=ns: e.copy(out=oa[0:ns, :], in_=pb[0:ns, 0:512]))
                    ac.dma([ob], [], osl, dst_fn(si, half), oa[0:ns, :])

        def x_fetch(parts_list):
            got = []
            for parts in parts_list:
                for half in range(2):
                    xa, xb_, xsl = xR.next()
                    for (src, r0, nr) in parts:
                        sy.dma([], [xb_] + x_alias, xsl, xa[r0:r0 + nr, :], src[:, 512 * half:512 * half + 512])
                    got.append((xa, xb_, half, parts[-1][1] + parts[-1][2]))
            return got

        def x_consume(got):
            for (xa, xb_, half, nr_tot) in got:
                pb, Bpb = accR.next()
                for j in range(4):
                    pe.do([xb_, Bident], [Bpb], lambda e, pb=pb, xa=xa, j=j, nr_tot=nr_tot: e.transpose(
                        out=pb[:, 128 * j:128 * j + nr_tot], in_=xa[0:nr_tot, 128 * j:128 * j + 128], identity=ident[0:nr_tot, 0:nr_tot]))
                yield half, pb, Bpb, nr_tot

        NS_ = NMETA + DEC
        for half, pb, Bpb, nr in x_consume(x_fetch([[(meta, 0, NMETA), (xs, NMETA, DEC)]])):
            q_ = evac_alt(half)
            q_.do([Bpb], [BhT[4 * half + j] for j in range(4)], copy_on(
                q_, hT[:, 4 * half:4 * half + 4, 0:nr], pb[:, :].rearrange("p (j t) -> p j t", j=4)[:, :, 0:nr]))
        segs = [(0, NMETA, ST_M), (NMETA, DEC, ST_S)]
        xrv = layer_a(NS_, segs)
        layer_a_chunks(NS_, segs, xrv)
        mlp(0, 8, 0, NS_)
        kv_stage(
            NS_, [(0, NS_)],
            kouts=[[(lambda g2: mk_p[0, :, 512 * g2:512 * g2 + 512], 0, NMETA),
                    (lambda g2: mk_p[1, :, 512 * g2:512 * g2 + 512], 0, NMETA),
                    (lambda g2: k_s[:, 512 * g2:512 * g2 + 512], NMETA, NS_)]],
            vouts=[[(lambda g2: mv_p[0, :, 512 * g2:512 * g2 + 512], 0, NMETA),
                    (lambda g2: mv_p[1, :, 512 * g2:512 * g2 + 512], 0, NMETA),
                    (lambda g2: v_s[:, 512 * g2:512 * g2 + 512], NMETA, NS_)]],
            kt_dsts=[(0, 0, 0, NMETA, 0), (1, 0, 0, NMETA, 0), (2, 1, NMETA, DEC, NMETA + PAST)],
            va_dsts=[(0, 0, 0, 0, NMETA, 0, 0), (1, 0, 0, 0, NMETA, 0, 0), (2, 1, 0, NMETA, NS_, SKT - 1, 0)],
        )
        gp.dma([Bhst[ST_S]], [], Sgen, sh_s[0].rearrange("(c p) -> p c", p=128), hstate[:, ST_S, :], allow_slow_non_contiguous=True)
        for j3 in range(3):
            gp.dma([Bct[ST_S]], [], Sgen, sc_s[0, j3].rearrange("(c p) -> p c", p=128), ctail[:, ST_S, :, j3], allow_slow_non_contiguous=True)
        for stt in (ST_A, ST_B):
            ve.do([Bhst[ST_M]], [Bhst[stt]], lambda e, stt=stt: e.tensor_copy(out=hstate[:, stt, :], in_=hstate[:, ST_M, :]))
            ve.do([Bct[ST_M]], [Bct[stt]], lambda e, stt=stt: e.tensor_copy(out=ctail[:, stt, :, :], in_=ctail[:, ST_M, :, :]))
        s_kt = [(0, NMETA, 0, -1, [BKT[2][0], BVA[2][0]])]
        for t in range(PAST // 128):
            s_kt.append((1 + t, 128, NMETA + 128 * t, t, [BKT[2][0], BVA[2][0]]))
        s_kt.append((SKT - 1, DEC, NMETA + PAST, PAST // 128, [BKT[2][1], BVA[2][1]]))
        layer_b(2, NMETA, NS_, [(0, DEC)], s_kt, PAST // 128)
        mlp(1, 32, NMETA, NS_)
        final_out(NMETA, [(0, DEC)], lambda si, half: y_s[:, 512 * half:512 * half + 512])

        ptiles = [(seq_, i_) for seq_ in range(2) for i_ in range(NTL)]

        def fetch_tile(ti):
            seq_, i_ = ptiles[ti]
            return x_fetch([[(xp[seq_, NT * i_ + 128 * s_:NT * i_ + 128 * s_ + 128, :], 0, 128)] for s_ in range(4)])

        xgot = {0: fetch_tile(0)}
        for tix, (seq, i) in enumerate(ptiles):
            stt = ST_A + seq
            if True:
                f0 = NT * i
                got = xgot.pop(tix)
                for gi_, (half, pb, Bpb, nr) in enumerate(x_consume(got)):
                    s = gi_ // 2
                    q_ = evac_alt(half)
                    q_.do([Bpb], [BhT[4 * half + j] for j in range(4)], copy_on(
                        q_, hT[:, 4 * half:4 * half + 4, 128 * s:128 * s + 128], pb[:, :].rearrange("p (j t) -> p j t", j=4)))
                segs = [(0, NT, stt)]
                xrv = layer_a(NT, segs)
                layer_a_chunks(NT, segs, xrv)
                mlp(0, 8, 0, NT)
                subs4 = [(128 * s, 128) for s in range(4)]
                kv_stage(
                    NT, subs4,
                    kouts=[[(lambda g2, s=s: k_p[seq, f0 + 128 * s:f0 + 128 * s + 128, 512 * g2:512 * g2 + 512], 0, 128)] for s in range(4)],
                    vouts=[[(lambda g2, s=s: v_p[seq, f0 + 128 * s:f0 + 128 * s + 128, 512 * g2:512 * g2 + 512], 0, 128)] for s in range(4)],
                    kt_dsts=[(seq, 1 + i, 0, NT, NMETA + f0)],
                    va_dsts=[(seq, 1 + i, s, 0, 128, 1 + 4 * i + s, 0) for s in range(4)],
                )
                p_kt = [(0, NMETA, 0, -1, [BKT[seq][0], BVA[seq][0]])]
                for t in range(4 * (i + 1)):
                    p_kt.append((1 + t, 128, NMETA + 128 * t, t, [BKT[seq][1 + t // 4], BVA[seq][1 + t // 4]]))
                layer_b(seq, 0, NT, subs4, p_kt, 4 * i)
                if tix + 1 < len(ptiles):
                    xgot[tix + 1] = fetch_tile(tix + 1)
                mlp(1, 32, 0, NT)
                final_out(0, subs4, lambda si, half, f0=f0, seq=seq: y_p[seq, f0 + 128 * si:f0 + 128 * si + 128, 512 * half:512 * half + 512])
            if i == NTL - 1:
                gp.dma([Bhst[stt]], [], Sgen, sh_p[seq].rearrange("(c p) -> p c", p=128), hstate[:, stt, :], allow_slow_non_contiguous=True)
                for j3 in range(3):
                    gp.dma([Bct[stt]], [], Sgen, sc_p[seq, j3].rearrange("(c p) -> p c", p=128), ctail[:, stt, :, j3], allow_slow_non_contiguous=True)

        for (_, _, sl) in oR.items:
            ac.wait_ev((sl.sem, sl.cnt))
        gp.wait_ev((Sgen.sem, Sgen.cnt))
        gp.wait_ev((Skst.sem, Skst.cnt))
        gp.wait_ev((Svast.sem, Svast.cnt))
        K.emit({"sync": sy, "gpsimd": gp, "tensor": pe, "vector": ve, "scalar": ac})
    return nc


def _vec_pm(v, n):
    return np.ascontiguousarray(np.asarray(v, np.float32).reshape(n, 128).T)


def make_in_maps(inp, SEQ, ncores=8):
    ohz, mask0, ident = _static_consts()
    g = lambda k: np.asarray(inp[k], np.float32)
    cst = np.zeros((128, 128), np.float32)
    cst[:, 0:8] = _vec_pm(g("norm_mix_g")[0], 8)
    cst[:, 8:16] = _vec_pm(g("norm_mlp_g")[0], 8)
    cst[:, 16:24] = _vec_pm(g("norm_kv_g"), 8)
    cst[:, 24:32] = _vec_pm(g("norm_mix_g")[1], 8)
    cst[:, 32:40] = _vec_pm(g("norm_mlp_g")[1], 8)
    cst[:, 40:48] = _vec_pm(g("norm_f_g"), 8)
    for j in range(4):
        cst[:, 48 + 10 * j:58 + 10 * j] = _vec_pm(g("conv_w")[0, j], 10)
    cst[:, 88:98] = _vec_pm(g("conv_b")[0], 10)
    cst[:, 98:108] = _vec_pm(g("b_gate_r")[0], 10)
    cst[:, 108:118] = _vec_pm(g("b_gate_i")[0], 10)
    cst[:, 118:128] = _vec_pm(g("lru_lambda")[0], 10)
    lamv = np.concatenate([g("lambda_q1")[0], g("lambda_k1")[0], g("lambda_q2")[0], g("lambda_k2")[0]])[None, :]
    shared = {
        "meta": g("meta_tokens"), "cst": cst,
        "w_in": g("w_in_a")[0], "w_gr": g("w_gate_r")[0], "w_gi": g("w_gate_i")[0], "w_out": g("w_out_a")[0],
        "w_up": g("w_mlp_up"), "w_down": g("w_mlp_down"), "w_kv": g("w_kv"), "w_q": g("w_q")[0], "w_o": g("w_o")[0],
        "lamv": np.ascontiguousarray(lamv), "subg": g("subln_g")[0][None, :].copy(), "relb": g("rel_bias"),
        "ohz": ohz, "mask0": mask0, "ident": ident,
        "bsel": np.concatenate([np.eye(2, dtype=np.float32)[:, 0:1].repeat(128, 1), np.eye(2, dtype=np.float32)[:, 1:2].repeat(128, 1)], axis=1),
    }
    maps = []
    for k in range(ncores):
        sst = np.zeros((128, 40), np.float32)
        sst[:, 0:10] = _vec_pm(g("state_h")[0, k], 10)
        sc = g("state_conv")[0, k]
        sst[:, 10:40] = np.stack([_vec_pm(sc[j], 10) for j in range(3)], axis=2).reshape(128, 30)
        m = dict(shared)
        m.update({
            "xp": np.ascontiguousarray(g("x_prompt")[2 * k:2 * k + 2, :SEQ]),
            "xs": np.ascontiguousarray(g("x_sample")[k]),
            "sst": sst,
            "cmk": np.ascontiguousarray(g("cache_meta_k")[k].reshape(NMETA, D)),
            "cmv": np.ascontiguousarray(g("cache_meta_v")[k].reshape(NMETA, D)),
            "ck": np.ascontiguousarray(g("cache_k")[k].reshape(PAST, D)),
            "cv": np.ascontiguousarray(g("cache_v")[k].reshape(PAST, D)),
        })
        maps.append(m)
    return maps


def gather(results, SEQ, ncores=8):
    cat = lambda k: np.concatenate([np.asarray(r[k]) for r in results], axis=0)
    y_p = cat("y_p")
    y_s = np.stack([np.asarray(r["y_s"]) for r in results], 0)
    sh_p = cat("sh_p")[None]
    sc_p = cat("sc_p")[None]
    mk_p = cat("mk_p").reshape(2 * ncores, NMETA, NH, 128)
    mv_p = cat("mv_p").reshape(2 * ncores, NMETA, NH, 128)
    k_p = cat("k_p").reshape(2 * ncores, SEQ, NH, 128)
    v_p = cat("v_p").reshape(2 * ncores, SEQ, NH, 128)
    sh_s = cat("sh_s")[None]
    sc_s = cat("sc_s")[None]
    k_s = np.stack([np.asarray(r["k_s"]) for r in results], 0).reshape(ncores, DEC, NH, 128)
    v_s = np.stack([np.asarray(r["v_s"]) for r in results], 0).reshape(ncores, DEC, NH, 128)
    outs = (y_p, y_s, sh_p, sc_p, mk_p, mv_p, k_p, v_p, sh_s, sc_s, k_s, v_s)
    return tuple(np.ascontiguousarray(o, dtype=np.float32) for o in outs)


def kernel(**inputs):
    SEQ = int(np.asarray(inputs["x_prompt"]).shape[1])
    nc = build(SEQ)
    maps = make_in_maps(inputs, SEQ, 8)
    res = run_bass_kernel_spmd(nc, maps, core_ids=list(range(8)))
    return gather(res.results, SEQ, 8)
```
